# Optimizing a Trainium2 kernel written in Bass

```python
import jax, jax.numpy as jnp
from jax import lax
import numpy as np

D_MODEL = 1024
BATCH = 2
SEQ = 16384
DEPTH = 1
DEC_BATCH = 8
DEC_SEQ = 16
PAST_LEN = 4096

CHUNK = 64
BAND_PAST_CHUNKS = 8
BAND_CHUNKS = BAND_PAST_CHUNKS + 1
HEAD_DIM = 64
H_A = 8
H_B = 8
W_A = H_A * HEAD_DIM
W_B = H_B * HEAD_DIM
REL_CLIP_PAST = 128
REL_TABLE = REL_CLIP_PAST + CHUNK
Q_BLOCK = 128
EPS = 1e-6
NEG_INF = -1e30
FORGET_BIAS_MEAN = 2.0
SCALE = HEAD_DIM ** -0.5
SPLIT_SIZES = [W_A] * 4 + [W_B] * 4 + [H_B, D_MODEL, D_MODEL]
N_IN = sum(SPLIT_SIZES)

kernel_name = "chunk_band_fox_hybrid_step"


def rmsnorm(x, g):
    x32 = x.astype(jnp.float32)
    y = x32 * lax.rsqrt(jnp.mean(x32 * x32, axis=-1, keepdims=True) + EPS) * g.astype(jnp.float32)
    return y.astype(x.dtype)


def rel_bias(table, dist):
    idx = jnp.clip(dist, -(CHUNK - 1), REL_CLIP_PAST) + (CHUNK - 1)
    return jnp.take(table, idx, axis=1).astype(jnp.float32)


def mixer_inputs(x, g_pre, w_in, b_f):
    h = rmsnorm(x, g_pre)
    z = jnp.einsum('bsd,dn->bsn', h, w_in)
    cuts = np.cumsum(SPLIT_SIZES)[:-1].tolist()
    qa, ka, va, za, qb, kb, vb, zb, fl, ma, mb = jnp.split(z, cuts, axis=-1)
    b, s = x.shape[0], x.shape[1]
    heads = lambda t, nh: t.reshape(b, s, nh, HEAD_DIM)
    logf = jax.nn.log_sigmoid((fl + b_f).astype(jnp.float32))
    return (heads(qa, H_A), heads(ka, H_A), heads(va, H_A), za,
            heads(qb, H_B), heads(kb, H_B), heads(vb, H_B), zb, logf, ma, mb)


def mixer_outputs(x, ya, yb, za, zb, ma, mb, w_br_a, w_br_b, w_out, g_post):
    ua = jnp.einsum('bsw,wd->bsd', ya.astype(x.dtype) * jax.nn.silu(za), w_br_a)
    ub = jnp.einsum('bsw,wd->bsd', yb.astype(x.dtype) * jax.nn.silu(zb), w_br_b)
    merged = jax.nn.sigmoid(ma) * ua + jax.nn.sigmoid(mb) * ub
    out = jnp.einsum('bsd,de->bse', merged, w_out)
    return x + rmsnorm(out, g_post)


def chunk_band_attention_prompt(q, k, v, table):
    b, s, h, d = q.shape
    nc = s // CHUNK
    pad = BAND_PAST_CHUNKS * CHUNK
    band = BAND_CHUNKS * CHUNK
    kp = jnp.pad(k, ((0, 0), (pad, 0), (0, 0), (0, 0)))
    vp = jnp.pad(v, ((0, 0), (pad, 0), (0, 0), (0, 0)))
    qc = jnp.moveaxis(q.reshape(b, nc, CHUNK, h, d), 1, 0)
    ki = jnp.arange(band)
    qi = jnp.arange(CHUNK)
    bias = rel_bias(table, (qi[:, None] + pad) - ki[None, :])

    def one_chunk(args):
        c, q_c = args
        k_c = lax.dynamic_slice_in_dim(kp, c * CHUNK, band, axis=1)
        v_c = lax.dynamic_slice_in_dim(vp, c * CHUNK, band, axis=1)
        sc = jnp.einsum('bqhd,bkhd->bhqk', q_c, k_c).astype(jnp.float32) * SCALE + bias[None]
        valid = (c * CHUNK - pad + ki) >= 0
        sc = jnp.where(valid[None, None, None, :], sc, NEG_INF)
        p = jax.nn.softmax(sc, axis=-1).astype(v.dtype)
        return jnp.einsum('bhqk,bkhd->bqhd', p, v_c)

    out = lax.map(one_chunk, (jnp.arange(nc), qc))
    return jnp.moveaxis(out, 0, 1).reshape(b, s, h * d)


def chunk_band_attention_sample(q, k_all, v_all, n_cache, table):
    b, t, h, d = q.shape
    sc = jnp.einsum('bqhd,bkhd->bhqk', q, k_all).astype(jnp.float32) * SCALE
    q_pos = n_cache + jnp.arange(t)
    k_pos = jnp.arange(n_cache + t)
    sc = sc + rel_bias(table, q_pos[:, None] - k_pos[None, :])[None]
    p = jax.nn.softmax(sc, axis=-1).astype(v_all.dtype)
    return jnp.einsum('bhqk,bkhd->bqhd', p, v_all).reshape(b, t, h * d)


def fox_prompt(q, k, v, logf):
    b, s, h, d = q.shape
    nb = s // Q_BLOCK
    F = jnp.cumsum(logf, axis=1)
    F_k = jnp.transpose(F, (0, 2, 1))
    qb = jnp.moveaxis(q.reshape(b, nb, Q_BLOCK, h, d), 1, 0)
    Fq = jnp.moveaxis(F.reshape(b, nb, Q_BLOCK, h), 1, 0)
    k_pos = jnp.arange(s)

    def one_block(args):
        i, q_i, F_i = args
        sc = jnp.einsum('bqhd,bkhd->bhqk', q_i, k).astype(jnp.float32) * SCALE
        sc = sc + jnp.transpose(F_i, (0, 2, 1))[..., None] - F_k[:, :, None, :]
        q_pos = i * Q_BLOCK + jnp.arange(Q_BLOCK)
        sc = jnp.where((k_pos[None, :] <= q_pos[:, None])[None, None], sc, NEG_INF)
        p = jax.nn.softmax(sc, axis=-1).astype(v.dtype)
        return jnp.einsum('bhqk,bkhd->bqhd', p, v)

    out = lax.map(one_block, (jnp.arange(nb), qb, Fq))
    return jnp.moveaxis(out, 0, 1).reshape(b, s, h * d)


def fox_sample(q, k_all, v_all, logf_all, n_cache):
    b, t, h, d = q.shape
    F = jnp.cumsum(logf_all.astype(jnp.float32), axis=1)
    F_k = jnp.transpose(F, (0, 2, 1))
    F_q = F_k[:, :, n_cache:]
    sc = jnp.einsum('bqhd,bkhd->bhqk', q, k_all).astype(jnp.float32) * SCALE
    sc = sc + F_q[..., None] - F_k[:, :, None, :]
    q_pos = n_cache + jnp.arange(t)
    k_pos = jnp.arange(n_cache + t)
    sc = jnp.where((k_pos[None, :] <= q_pos[:, None])[None, None], sc, NEG_INF)
    p = jax.nn.softmax(sc, axis=-1).astype(v_all.dtype)
    return jnp.einsum('bhqk,bkhd->bqhd', p, v_all).reshape(b, t, h * d)


def setup_inputs(seed: int = 0) -> dict:
    key = jax.random.key(seed)
    ks = jax.random.split(key, 16)
    a_keep = min(BAND_PAST_CHUNKS * CHUNK, PAST_LEN)
    nrm = lambda k, shape, s=1.0: s * jax.random.normal(k, shape, jnp.float32)
    return {
        "x_prompt": nrm(ks[0], (BATCH, SEQ, D_MODEL)),
        "x_sample": nrm(ks[1], (DEC_BATCH, DEC_SEQ, D_MODEL)),
        "cache_a_k": nrm(ks[2], (DEPTH, DEC_BATCH, a_keep, H_A, HEAD_DIM)),
        "cache_a_v": nrm(ks[3], (DEPTH, DEC_BATCH, a_keep, H_A, HEAD_DIM)),
        "cache_b_k": nrm(ks[4], (DEPTH, DEC_BATCH, PAST_LEN, H_B, HEAD_DIM)),
        "cache_b_v": nrm(ks[5], (DEPTH, DEC_BATCH, PAST_LEN, H_B, HEAD_DIM)),
        "cache_b_logf": jax.nn.log_sigmoid(FORGET_BIAS_MEAN + nrm(ks[6], (DEPTH, DEC_BATCH, PAST_LEN, H_B))),
        "g_pre": 1.0 + nrm(ks[7], (DEPTH, D_MODEL), 0.05),
        "w_in": nrm(ks[8], (DEPTH, D_MODEL, N_IN), D_MODEL ** -0.5),
        "b_f": FORGET_BIAS_MEAN + nrm(ks[9], (DEPTH, H_B), 0.5),
        "rel_table": nrm(ks[10], (DEPTH, H_A, REL_TABLE), 0.5),
        "w_br_a": nrm(ks[11], (DEPTH, W_A, D_MODEL), W_A ** -0.5),
        "w_br_b": nrm(ks[12], (DEPTH, W_B, D_MODEL), W_B ** -0.5),
        "w_out": nrm(ks[13], (DEPTH, D_MODEL, D_MODEL), D_MODEL ** -0.5),
        "g_post": 1.0 + nrm(ks[14], (DEPTH, D_MODEL), 0.05),
    }


def reference(x_prompt, x_sample, cache_a_k, cache_a_v, cache_b_k, cache_b_v, cache_b_logf,
              g_pre, w_in, b_f, rel_table, w_br_a, w_br_b, w_out, g_post):
    xp, xs = x_prompt, x_sample
    n_keep_p = min(BAND_PAST_CHUNKS * CHUNK, x_prompt.shape[1])
    n_cache_a = cache_a_k.shape[2]
    n_cache_b = cache_b_k.shape[2]
    akp, avp, bkp, bvp, blp = [], [], [], [], []
    aks, avs, bks, bvs, bls = [], [], [], [], []
    for l in range(DEPTH):
        qa, ka, va, za, qb, kb, vb, zb, logf, ma, mb = mixer_inputs(xp, g_pre[l], w_in[l], b_f[l])
        ya = chunk_band_attention_prompt(qa, ka, va, rel_table[l])
        yb = fox_prompt(qb, kb, vb, logf)
        akp.append(ka[:, -n_keep_p:])
        avp.append(va[:, -n_keep_p:])
        bkp.append(kb)
        bvp.append(vb)
        blp.append(logf)
        xp = mixer_outputs(xp, ya, yb, za, zb, ma, mb, w_br_a[l], w_br_b[l], w_out[l], g_post[l])

        qa, ka, va, za, qb, kb, vb, zb, logf, ma, mb = mixer_inputs(xs, g_pre[l], w_in[l], b_f[l])
        ka_all = jnp.concatenate([cache_a_k[l], ka], axis=1)
        va_all = jnp.concatenate([cache_a_v[l], va], axis=1)
        ya = chunk_band_attention_sample(qa, ka_all, va_all, n_cache_a, rel_table[l])
        kb_all = jnp.concatenate([cache_b_k[l], kb], axis=1)
        vb_all = jnp.concatenate([cache_b_v[l], vb], axis=1)
        lf_all = jnp.concatenate([cache_b_logf[l].astype(jnp.float32), logf], axis=1)
        yb = fox_sample(qb, kb_all, vb_all, lf_all, n_cache_b)
        aks.append(ka_all[:, -n_cache_a:])
        avs.append(va_all[:, -n_cache_a:])
        bks.append(kb)
        bvs.append(vb)
        bls.append(logf)
        xs = mixer_outputs(xs, ya, yb, za, zb, ma, mb, w_br_a[l], w_br_b[l], w_out[l], g_post[l])

    return (xp, xs,
            jnp.stack(akp), jnp.stack(avp), jnp.stack(bkp), jnp.stack(bvp), jnp.stack(blp),
            jnp.stack(aks), jnp.stack(avs), jnp.stack(bks), jnp.stack(bvs), jnp.stack(bls))
```

```python
import numpy as np
from contextlib import ExitStack
import concourse.bass as bass
import concourse.mybir as mybir
from concourse.bass_utils import run_bass_kernel_spmd

F32 = mybir.dt.float32
BF16 = mybir.dt.bfloat16
AF = mybir.ActivationFunctionType
ALU = mybir.AluOpType

D = 1024
DC = 8
C1 = 3080
NT16 = 4
C1P = 3200
C3 = 3072
SCALE = 0.125
BIG = 30000.0
EPS = 1e-6
NSAMP = 16
LS = 4096
LSP = 4224
CH = 512
CH2 = 512


class _Stop(Exception):
    pass


def _aslist(x):
    if x is None:
        return []
    return list(x) if isinstance(x, (list, tuple)) else [x]


def _last2(a, b):
    return _aslist(a) + [b]


class Op:
    __slots__ = ("eng", "fn", "deps", "sig", "val", "dsem", "dval")

    def __init__(self, eng, fn, deps, dsem=None, dval=0):
        self.eng = eng
        self.fn = fn
        self.deps = [d for d in deps if d is not None]
        self.sig = False
        self.val = 0
        self.dsem = dsem
        self.dval = dval


class Prog:
    ENGS = ("pe", "act", "dve", "pool", "sp")

    def __init__(self, nc, sems):
        self.nc = nc
        self.sems = sems
        self.lists = {e: [] for e in self.ENGS}
        self.dma_cnt = {}
        self.dma_sems = {}
        self.base = {e: 0 for e in self.ENGS}

    def op(self, eng, fn, deps=()):
        o = Op(eng, fn, deps)
        self.lists[eng].append(o)
        return o

    def dma(self, eng, sem, fn, deps=()):
        k = id(sem)
        self.dma_cnt[k] = self.dma_cnt.get(k, 0) + 16
        self.dma_sems[k] = sem
        o = Op(eng, fn, deps, dsem=sem, dval=self.dma_cnt[k])
        self.lists[eng].append(o)
        return o

    def finalize(self):
        sems = self.sems
        for e in self.ENGS:
            for o in self.lists[e]:
                for d in o.deps:
                    if d.dsem is None and not (d.eng == "pe" and e == "pe"):
                        d.sig = True
        for e in self.ENGS:
            c = self.base[e]
            for o in self.lists[e]:
                if o.dsem is None and o.sig:
                    c += 1
                o.val = c
            self.base[e] = c
        nc = self.nc
        lists = self.lists
        final = [(self.dma_sems[k], v) for k, v in self.dma_cnt.items()]
        with nc.Block() as block:
            def run(e, engine):
                waited = {}
                for o in lists[e]:
                    for d in o.deps:
                        if d.dsem is not None:
                            s, v = d.dsem, d.dval
                        else:
                            if d.eng == "pe" and e == "pe":
                                continue
                            s, v = sems[d.eng], d.val
                        key = id(s)
                        if waited.get(key, 0) >= v:
                            continue
                        waited[key] = v
                        engine.wait_ge(s, v)
                    ins = o.fn(engine)
                    if o.dsem is not None:
                        ins.then_inc(o.dsem, 16)
                    elif o.sig:
                        ins.then_inc(sems[e], 1)
                if e == "sp":
                    for (s, v) in final:
                        engine.wait_ge(s, v)

            @block.tensor
            def _(eng):
                run("pe", eng)

            @block.scalar
            def _(eng):
                run("act", eng)

            @block.vector
            def _(eng):
                run("dve", eng)

            @block.gpsimd
            def _(eng):
                run("pool", eng)

            @block.sync
            def _(eng):
                run("sp", eng)
        self.lists = {e: [] for e in self.ENGS}


def build(NSLOT=32, sample=True, stop=None):
    try:
        return _build(NSLOT, sample, stop)
    except _Stop as e:
        return e.args[0]


def _build(NSLOT=32, sample=True, stop=None):
    NOWN = NSLOT // 4
    L = NSLOT * 512
    NQ = NOWN * 512
    NQA = NQ + NSAMP
    nc = bass.Bass("TRN2", target_bir_lowering=False)

    def din(name, shape, dt=F32):
        return nc.dram_tensor(name, shape, dt, kind="ExternalInput").ap()

    def dout(name, shape, dt=F32):
        return nc.dram_tensor(name, shape, dt, kind="ExternalOutput").ap()

    def dscr(name, shape, dt):
        return nc.dram_tensor(name, shape, dt, kind="Internal").ap()

    xw = din("xw", [L, D])
    w1 = din("w1", [D, C1])
    w3 = din("w3", [D, C3])
    wbr = din("wbr", [D, D])
    wout = din("wout", [D, D])
    gpre = din("gpre", [D])
    gpost = din("gpost", [D])
    bfv = din("bfv", [8, 1])
    rel = din("rel", [8, 192])
    smask_d = din("smask", [128, 3])
    xs_d = din("xs", [NSAMP, D])
    cak = din("cak", [512, 512])
    cav = din("cav", [512, 512])
    cbk = din("cbk", [LS, 512])
    cbv = din("cbv", [LS, 512])
    cbl = din("cbl", [LS, 8])

    y_own = dout("y_own", [NQ, D])
    bk_own = dout("bk_own", [NQ, 512])
    bv_own = dout("bv_own", [NQ, 512])
    blf_own = dout("blf_own", [NQ, 8])
    akp = dout("akp", [512, 512])
    avp = dout("avp", [512, 512])
    ys_o = dout("ys", [NSAMP, D])
    aks = dout("aks", [512, 512])
    avs = dout("avs", [512, 512])
    bks = dout("bks", [NSAMP, 512])
    bvs = dout("bvs", [NSAMP, 512])
    bls = dout("bls", [NSAMP, 8])

    Ks = dscr("Ks", [8, 68, L], BF16)
    Vs = dscr("Vs", [L, 512], BF16)
    Qs = dscr("Qs", [8, 68, NQA], BF16)
    YA = dscr("YA", [512, NQA], BF16)
    YB = dscr("YB", [512, NQA], BF16)
    ext = dscr("ext", [8, 768], F32)
    Erep = dscr("Erep", [8, 128, 768], F32)
    W3s = dscr("W3s", [D, C3], BF16)
    Wbrs = dscr("Wbrs", [D, D], BF16)
    Wouts = dscr("Wouts", [D, D], BF16)
    Kss = dscr("Kss", [8, 68, LSP], BF16)
    Vss = dscr("Vss", [LSP, 512], BF16)

    outer = ExitStack()
    with outer:
        def osb(name, shape, dt):
            return outer.enter_context(nc.sbuf_tensor(name, shape, dt))

        sems = {e: outer.enter_context(nc.semaphore("s_" + e)) for e in ("pe", "act", "dve", "pool")}
        nsem = [0]

        def newsem(stack):
            nsem[0] += 1
            return outer.enter_context(nc.semaphore("d%d" % nsem[0]))

        P = Prog(nc, sems)

        def chk(tag):
            if stop == tag:
                P.finalize()
                raise _Stop(nc)

        def mm(out, lhsT, rhs, start, stop, deps=(), skip=False):
            return P.op("pe", lambda e: e.matmul(out, lhsT=lhsT, rhs=rhs, start=start, stop=stop,
                                                 skip_group_check=skip), deps)

        def tr(out, in_, idn, deps=()):
            return P.op("pe", lambda e: e.transpose(out=out, in_=in_, identity=idn), deps)

        def act(out, in_, func, deps=(), **kw):
            return P.op("act", lambda e: e.activation(out=out, in_=in_, func=func, **kw), deps)

        def ts(eng, out, in0, s1, s2, op0, op1=None, deps=()):
            if op1 is None:
                return P.op(eng, lambda e: e.tensor_scalar(out=out, in0=in0, scalar1=s1, scalar2=None, op0=op0), deps)
            return P.op(eng, lambda e: e.tensor_scalar(out=out, in0=in0, scalar1=s1, scalar2=s2, op0=op0, op1=op1), deps)

        def tt(eng, out, in0, in1, op, deps=()):
            return P.op(eng, lambda e: e.tensor_tensor(out=out, in0=in0, in1=in1, op=op), deps)

        def stt(out, in0, scalar, in1, op0, op1, deps=()):
            return P.op("dve", lambda e: e.scalar_tensor_tensor(out=out, in0=in0, scalar=scalar, in1=in1,
                                                                op0=op0, op1=op1), deps)

        def cp(eng, out, in_, deps=()):
            if eng == "act":
                return act(out, in_, AF.Copy, deps)
            return P.op(eng, lambda e: e.tensor_copy(out=out, in_=in_), deps)

        def memset(eng, ap, val, deps=()):
            return P.op(eng, lambda e: e.memset(ap, val), deps)

        def dma(q, sem, out, in_, deps=(), slow=False):
            if slow:
                q = "pool"
            return P.dma(q, sem, lambda e: e.dma_start(out=out, in_=in_, allow_slow_non_contiguous=slow), deps)

        ident = osb("ident", [128, 128], BF16)
        tri = osb("tri", [128, 128], BF16)
        onesf = osb("onesf", [128, 128], F32)
        smask = osb("smask_sb", [128, 3], F32)
        nbf = osb("nbf", [8, 1], F32)
        gpre_sb = osb("gpre_sb", [128, DC], F32)

        ph = ExitStack()
        with ph:
            def sb(name, shape, dt):
                return ph.enter_context(nc.sbuf_tensor(name, shape, dt))

            def ps(name, shape, dt):
                return ph.enter_context(nc.psum_tensor(name, shape, dt))

            sem_misc = newsem(ph)
            sem_gp = newsem(ph)
            sem_e = newsem(ph)
            sem_e4 = newsem(ph)
            sem_e1 = newsem(ph)
            sem_tl = newsem(ph)
            sem_w = [newsem(ph), newsem(ph)]
            c_ones = memset("pool", onesf[:], 1.0)
            c_id = P.op("pool", lambda e: e.affine_select(out=ident[:], in_=onesf[:], pattern=[[1, 128]],
                                                         compare_op=ALU.is_equal, fill=0.0, base=0,
                                                         channel_multiplier=-1), [c_ones])
            c_tri = P.op("pool", lambda e: e.affine_select(out=tri[:], in_=onesf[:], pattern=[[1, 128]],
                                                          compare_op=ALU.is_ge, fill=0.0, base=0,
                                                          channel_multiplier=-1), [c_ones])
            l_sm = dma("sp", sem_misc, smask[:], smask_d)
            l_bf = dma("sp", sem_misc, nbf[:], bfv)
            l_gp = dma("sp", sem_gp, gpre_sb[:], gpre.rearrange("(c p) -> p c", p=128), slow=True)
            c_nbf = ts("dve", nbf[:], nbf[:], -1.0, None, ALU.mult, deps=[l_bf])

            chk("a0")
            tm32_early = [sb("tm32_%d" % i, [128, 512], F32) for i in range(3)]
            EB = sb("EB", [128, 8, 5, 128], F32)
            if True:
                T = EB
                ea, eb = tm32_early[0], tm32_early[1]
                e0 = dma("sp", sem_e, ea[0:8, 64:256], rel)
                z1 = memset("dve", ea[0:8, 0:64], 0.0)
                z2 = memset("dve", eb[0:8, :], 0.0)
                z3 = ts("dve", ea[0:8, 0:64], ea[0:8, 0:64], ea[0:8, 64:65], None, ALU.add, deps=[e0, z1])
                z4 = ts("dve", eb[0:8, :], eb[0:8, :], ea[0:8, 255:256], None, ALU.add, deps=[e0, z2])
                e1 = dma("sp", sem_e1, ext[:, 0:256], ea[0:8, 0:256], deps=[z3])
                e3 = dma("sp", sem_e1, ext[:, 256:768], eb[0:8, :], deps=[z4])
                e4 = dma("sp", sem_e4, Erep,
                         bass.AP(tensor=ext.tensor, offset=0, ap=[[768, 8], [0, 128], [1, 768]]), deps=[e3])
                chk("a1")
                tl = None
                for h in range(8):
                    src = bass.AP(tensor=Erep.tensor, offset=h * 128 * 768 + 127, ap=[[767, 128], [128, 5], [1, 128]])
                    tl = dma("sp", sem_tl, T[:, h, :, :], src, deps=[e4])
                c_t0 = memset("dve", T[64:128, :, 0, 0:64], -BIG, deps=[tl])
                c_t4 = memset("dve", T[0:64, :, 4, 64:128], -BIG, deps=[tl])
                Tf = T[:].rearrange("p a b c -> p (a b c)")
                c_tl = act(Tf, Tf, AF.Exp, deps=[c_t0, c_t4])
            T_ready = [c_tl]
            chk("a2")
            Wb = sb("Wb", [128, DC, C1P], BF16)
            c_wpad = memset("dve", Wb[:, :, C1:C1P], 0.0, deps=[c_tl])
            CQ = C1 // 4
            wst = [sb("wst%d" % i, [128, CQ], F32) for i in range(4)]
            sem_w4 = [newsem(ph) for _ in range(4)]
            wfree = [c_tl] * 4
            W_ready = []
            k = 0
            for dc in range(DC):
                for hf in range(4):
                    c0 = hf * CQ
                    s = k % 4
                    ld = dma("sp", sem_w4[s], wst[s][:], w1[dc * 128:(dc + 1) * 128, c0:c0 + CQ],
                             deps=[wfree[s]])
                    o = ts("dve", Wb[:, dc, c0:c0 + CQ], wst[s][:], gpre_sb[:, dc:dc + 1], None, ALU.mult,
                           deps=[ld, l_gp])
                    wfree[s] = o
                    W_ready.append(o)
                    k += 1
            W_ready = W_ready[-4:] + [c_wpad]

            chk("a")
            NX = 3
            xt = [sb("xt%d" % i, [128, D], F32) for i in range(NX)]
            sem_x = [newsem(ph) for _ in range(NX)]
            junk = sb("junk", [128, D], BF16)
            ssq = sb("ssq", [128, 4], F32)
            rsd = sb("rsd", [128, 4], F32)
            hb = [sb("hb%d" % i, [128, D], BF16) for i in range(2)]
            NF = 6
            fmst = [sb("fmst%d" % i, [128, 512], BF16) for i in range(NF)]
            sem_fm = [newsem(ph) for _ in range(NF)]
            NT32 = 3
            tm32 = tm32_early
            sem_t32 = [newsem(ph) for _ in range(NT32)]
            tm16 = [sb("tm16_%d" % i, [128, 512], BF16) for i in range(NT16)]
            sem_t16 = [newsem(ph) for _ in range(NT16)]
            pbt = [[sb("pbt%d_%d" % (a_, i), [128, 512], BF16) for i in range(2)] for a_ in range(2)]
            ptmp = [[sb("ptmp%d_%d" % (a_, i), [128, 512], BF16) for i in range(2)] for a_ in range(2)]
            osbuf = [sb("osbuf%d" % i, [65, 512], F32) for i in range(2)]
            rd = sb("rd", [128, 1024], F32)
            sel = sb("sel", [128, 128], F32)
            yab = [sb("yab%d" % i, [64, 512], BF16) for i in range(2)]
            sem_ya = [newsem(ph) for _ in range(2)]
            el8 = sb("el8", [8, 512], F32)
            G8 = [sb("G8_%d" % i, [8, 512], F32) for i in range(2)]
            on8 = sb("on8", [8, 512], F32)
            r8 = sb("r8", [8, 512], F32)
            lim = sb("lim", [8, 4, 512], BF16)
            qst = sb("qst", [8, 4, 512], BF16)
            lf8 = sb("lf8", [8, 512], F32)
            sem_lim = newsem(ph)
            sem_qst = newsem(ph)
            sem_lf = newsem(ph)

            tp = [ps("tp%d" % i, [128, DC, 128], BF16) for i in range(1)] * 2
            mp = [ps("mp%d" % i, [128, 512], F32) for i in range(3)]
            sT = [ps("sT%d" % i, [128, 512], F32) for i in range(2)]
            oT = [ps("oT%d" % i, [128, 512], F32) for i in range(2)]
            c_rd = memset("pool", rd[:], 0.0)
            c_sel0 = memset("pool", sel[:], 0.0)
            c_sel = memset("pool", sel[64:65, :], 1.0, deps=[c_sel0])

            c_on8 = memset("pool", on8[:], 1.0)
            c_lim = memset("pool", lim[:, 0, :], 1.0)
            c_qst = memset("pool", qst[:, 1:4, :], 1.0)

            st = dict(xi=0, xfree=[None] * NX, hbfree=[None, None], tpfree=[None, None], hTfree=[None, None],
                      mpi=0, mpfree=[[], [], []], fmi=0, fmlast=[None] * NF, t32i=0, t32last=[e3, e3, None],
                      t16i=0, t16last=[None] * NT16, rsfree=[None] * 4, sTi=0, sTfree=[None, None],
                      pbfree=[[None, None], [None, None]], ptfree=[[None, None], [None, None]], oTfree=[[], []], yai=0, yalast=[None, None],
                      limlast=None, qstlast=None, lflast=None, Gi=0, Gprev=None, KATfree=None, VAfree=None,
                      QATfree=None, osfree=[None, None], rdfree=[None, None], pend_norm=None, elfree=None, r8free=None)

            def norm_a(src_ap, n):
                i = st["xi"]
                st["xi"] += 1
                xs_ = xt[i % NX]
                ld = dma("sp", sem_x[i % NX], xs_[0:n, :], src_ap, deps=[st["xfree"][i % NX]])
                c = i % 4
                sq = act(junk[0:n, :], xs_[0:n, :], AF.Square, deps=[ld], accum_out=ssq[0:n, c:c + 1])
                sr = act(rsd[0:n, c:c + 1], ssq[0:n, c:c + 1], AF.Ln, deps=[sq, st["rsfree"][c]],
                         scale=1.0 / D, bias=EPS)
                rc = act(rsd[0:n, c:c + 1], rsd[0:n, c:c + 1], AF.Exp, deps=[sr], scale=-0.5)
                b = i % 2
                sc = ts("dve", hb[b][0:n, :], xs_[0:n, :], rsd[0:n, c:c + 1], None, ALU.mult,
                        deps=[rc, ld, st["hbfree"][b]])
                st["rsfree"][c] = sc
                st["xfree"][i % NX] = sc
                return (b, sc, n)

            def norm_b(hnd, hTdst, deps_hT):
                b, sc, n = hnd
                t_ = None
                for dc in range(DC):
                    t_ = tr(tp[b][:, dc, 0:n], hb[b][0:n, dc * 128:(dc + 1) * 128], ident[0:n, 0:n],
                            deps=[sc, st["tpfree"][0], c_id])
                st["hbfree"][b] = t_
                co = act(hTdst, tp[b][:, :, 0:n], AF.Copy, deps=[t_] + list(deps_hT))
                st["tpfree"][0] = co
                return co

            def norm_transpose(src_ap, n, hTdst, deps_hT):
                co = norm_b(norm_a(src_ap, n), hTdst, deps_hT)
                return co, None, None

            def mm_group(pairs, M, N, deps):
                bi = st["mpi"] % 3
                st["mpi"] += 1
                bank = mp[bi]
                o = None
                n_ = len(pairs)
                for kk, (l_, r_) in enumerate(pairs):
                    o = mm(bank[0:M, 0:N], l_, r_, kk == 0, kk == n_ - 1, deps=list(deps) + st["mpfree"][bi])
                return bi, bank, o

            def fm_chunk(Wt, c0, M, hTs, N, deps):
                pairs = [(Wt[:, dc, c0:c0 + M], hTs[:, dc, 0:N]) for dc in range(DC)]
                return mm_group(pairs, M, N, deps)

            def tm_group(Wt, c0, ncol, hTs, t0, n, deps):
                pairs = [(hTs[:, dc, t0:t0 + n], Wt[:, dc, c0:c0 + ncol]) for dc in range(DC)]
                return mm_group(pairs, n, ncol, deps)

            def store_fm_heads(bank, bi, mmop, dst, hpair, col0, N, eng="dve"):
                s = st["fmi"] % NF
                st["fmi"] += 1
                ev = cp(eng, fmst[s][:, 0:N], bank[:, 0:N], deps=[mmop, st["fmlast"][s]])
                st["mpfree"][bi] = [ev]
                d_ = None
                for hh in range(2):
                    d_ = dma("pool", sem_fm[s], dst[2 * hpair + hh, 0:64, col0:col0 + N],
                             fmst[s][hh * 64:(hh + 1) * 64, 0:N], deps=[ev])
                st["fmlast"][s] = d_
                return ev

            def fpath(bank, bi, mmop, N, col0, Kdst, qcol0, own, lfdst):
                gi = st["Gi"] % 2
                st["Gi"] += 1
                G = G8[gi]
                a1 = act(el8[:, 0:N], bank[0:8, 0:N], AF.Exp, deps=[mmop, c_nbf, st["elfree"]], scale=-1.0, bias=nbf[:])
                st["mpfree"][bi] = [a1]
                a2 = act(el8[:, 0:N], el8[:, 0:N], AF.Ln, deps=[a1], bias=1.0)
                init = 0.0 if st["Gprev"] is None else st["Gprev"][0]
                sdeps = [a2, c_on8] + ([] if st["Gprev"] is None else [st["Gprev"][1]])
                sc_ = P.op("dve", lambda e: e.tensor_tensor_scan(out=G[:, 0:N], data0=on8[:, 0:N], data1=el8[:, 0:N],
                                                                 initial=init, op0=ALU.mult, op1=ALU.add),
                           sdeps + [st["limlast"], st["qstlast"]])
                st["Gprev"] = (G[:, N - 1:N], sc_)
                d1 = ts("dve", lim[:, 1, 0:N], G[:, 0:N], 8.0, None, ALU.mult, deps=[sc_, st["limlast"], c_lim])
                d2 = stt(r8[:, 0:N], G[:, 0:N], 8.0, lim[:, 1, 0:N], ALU.mult, ALU.subtract, deps=[d1, st["r8free"]])
                d3 = cp("dve", lim[:, 2, 0:N], r8[:, 0:N], deps=[d2])
                d4 = tt("dve", lim[:, 3, 0:N], r8[:, 0:N], lim[:, 2, 0:N], ALU.subtract, deps=[d3])
                st["r8free"] = d4
                st["limlast"] = dma("pool", sem_lim, Kdst[:, 64:68, col0:col0 + N], lim[:, :, 0:N], deps=[d4])
                last = d4
                if own:
                    q1 = ts("dve", qst[:, 0, 0:N], G[:, 0:N], -8.0, None, ALU.mult, deps=[sc_, st["qstlast"], c_qst])
                    st["qstlast"] = dma("pool", sem_qst, Qs[:, 64:68, qcol0:qcol0 + N], qst[:, :, 0:N], deps=[q1])
                    q2 = ts("dve", lf8[:, 0:N], el8[:, 0:N], -1.0, None, ALU.mult, deps=[a2, st["lflast"]])
                    st["lflast"] = dma("sp", sem_lf, lfdst.rearrange("t h -> h t"), lf8[:, 0:N], deps=[q2], slow=True)
                    last = q2
                st["elfree"] = last
                return last

            def band2(heads, Qaps, tiles_of, nq, ycol0, m0ctx, filler=None):
                nt = len(tiles_of[0])

                def emit_qk(a_, ti):
                    Kap, Vap, nk, q0, n, Eb, cf = tiles_of[a_][ti]
                    return mm(sT[a_][0:nk, 0:n], Kap, Qaps[a_][:, q0:q0 + n], True, True,
                              deps=[st["sTfree"][a_]] + st["Kready"] + st["Qready"])

                def emit_exp(a_, ti, qk):
                    Kap, Vap, nk, q0, n, Eb, cf = tiles_of[a_][ti]
                    kw = {"scale": SCALE}
                    if cf and m0ctx:
                        kw["bias"] = smask[0:nk, 2:3]
                    ex = act(ptmp[a_][ti % 2][0:nk, 0:n], sT[a_][0:nk, 0:n], AF.Exp,
                             deps=[qk, st["ptfree"][a_][ti % 2], l_bf], **kw)
                    st["sTfree"][a_] = ex
                    po_, pi_ = pbt[a_][ti % 2][0:nk, 0:n], ptmp[a_][ti % 2][0:nk, 0:n]
                    if len(Eb.shape) == 3:
                        po_ = po_.rearrange("p (a q) -> p a q", q=128)
                        pi_ = pi_.rearrange("p (a q) -> p a q", q=128)
                    mu = tt("dve", po_, pi_, Eb, ALU.mult, deps=[ex, st["pbfree"][a_][ti % 2]] + T_ready)
                    st["ptfree"][a_][ti % 2] = mu
                    return mu

                def emit_pv(a_, ti, ex):
                    Kap, Vap, nk, q0, n, Eb, cf = tiles_of[a_][ti]
                    pv = mm(oT[a_][0:65, q0:q0 + n], Vap, pbt[a_][ti % 2][0:nk, 0:n], ti == 0, ti == nt - 1,
                            deps=[ex] + st["oTfree"][a_] + st["Vready"], skip=True)
                    st["pbfree"][a_][ti % 2] = pv
                    return pv

                exs = {}
                pvs = {}
                qkA = emit_qk(0, 0)
                exs[(0, 0)] = emit_exp(0, 0, qkA)
                qkB = emit_qk(1, 0)
                exs[(1, 0)] = emit_exp(1, 0, qkB)
                for ti in range(nt):
                    if ti == 1 and st["pend_norm"] is not None:
                        st["pend_norm"]()
                        st["pend_norm"] = None
                    pvs[0] = emit_pv(0, ti, exs.pop((0, ti)))
                    if ti + 1 < nt:
                        q_ = emit_qk(0, ti + 1)
                        exs[(0, ti + 1)] = emit_exp(0, ti + 1, q_)
                    pvs[1] = emit_pv(1, ti, exs.pop((1, ti)))
                    if ti + 1 < nt:
                        q_ = emit_qk(1, ti + 1)
                        exs[(1, ti + 1)] = emit_exp(1, ti + 1, q_)
                    if filler is not None and ti % 2 == 1:
                        filler()
                norm_jobs = []
                c1s = []
                for a_ in range(2):
                    c1 = act(osbuf[a_][0:65, 0:nq], oT[a_][0:65, 0:nq], AF.Copy, deps=[pvs[a_], st["osfree"][a_]])
                    st["oTfree"][a_] = [c1]
                    c1s.append(c1)
                for a_ in range(2):
                    h = heads[a_]
                    l1 = act(rd[64:65, a_ * 512:a_ * 512 + nq], osbuf[a_][64:65, 0:nq], AF.Ln,
                             deps=[c1s[a_], st["rdfree"][a_], c_rd])
                    r1 = act(rd[64:65, a_ * 512:a_ * 512 + nq], rd[64:65, a_ * 512:a_ * 512 + nq], AF.Exp, deps=[l1], scale=-1.0)
                    norm_jobs.append((a_, h, r1, c1s[a_]))

                def finish(norm_jobs=norm_jobs, nq=nq, ycol0=ycol0):
                    for (a_, h, r1, c1) in norm_jobs:
                        bi = st["mpi"] % 3
                        st["mpi"] += 1
                        bc = mm(mp[bi][:, 0:nq], sel[:, :], rd[:, a_ * 512:a_ * 512 + nq], True, True,
                                deps=[r1, c_sel] + st["mpfree"][bi])
                        st["rdfree"][a_] = bc
                        yi = st["yai"] % 2
                        st["yai"] += 1
                        y1 = tt("dve", yab[yi][:, 0:nq], osbuf[a_][0:64, 0:nq], mp[bi][0:64, 0:nq], ALU.mult,
                                deps=[c1, bc, st["yalast"][yi]])
                        st["mpfree"][bi] = [y1]
                        st["osfree"][a_] = y1
                        st["yalast"][yi] = dma("pool", sem_ya[yi], YA[h * 64:(h + 1) * 64, ycol0:ycol0 + nq], yab[yi][:, 0:nq],
                                               deps=[y1])
                st["pend_norm"] = finish
                return [pvs[0], pvs[1]]

            st["Kready"] = []
            st["Qready"] = []
            st["Vready"] = []

            if sample:
                sp_ = ExitStack()
                with sp_:
                    def ssb(name, shape, dt):
                        return sp_.enter_context(nc.sbuf_tensor(name, shape, dt))
                    hTs = ssb("hTs", [128, DC, NSAMP], BF16)
                    QATs = ssb("QATs", [128, 8, NSAMP], BF16)
                    c_qz = memset("pool", QATs[:], 0.0)
                    KATs = ssb("KATs", [128, 4, 640], BF16)
                    VAs = ssb("VAs", [128, 5, 8, 65], BF16)
                    cst = [ssb("cst%d" % i, [128, 512], F32) for i in range(2)]
                    sem_c = [newsem(sp_), newsem(sp_)]
                    c16 = [ssb("c16_%d" % i, [128, 512], BF16) for i in range(2)]
                    kst = ssb("kst", [128, 4, 512], BF16)
                    sem_kst = newsem(sp_)
                    lc = ssb("lc", [8, LS], F32)
                    Gc = ssb("Gc", [8, CH2], F32)
                    on8c = ssb("on8c", [8, CH2], F32)
                    r8c = ssb("r8c", [8, CH2], F32)
                    limc2 = [ssb("limc%d" % i, [8, 4, CH2], BF16) for i in range(2)]
                    sem_lc = newsem(sp_)
                    sem_limc2 = [newsem(sp_), newsem(sp_)]
                    sem_dd = newsem(sp_)
                    sem_vc = newsem(sp_)
                    lcl = None
                    for g in range(LS // CH):
                        lcl = dma("sp", sem_lc, lc[:, g * CH:(g + 1) * CH], cbl[g * CH:(g + 1) * CH, :].rearrange("t h -> h t"),
                                  slow=True)
                    c_vas = memset("pool", VAs[:, :, :, 64:65], 1.0)
                    c_on8c = memset("pool", on8c[:], 1.0)
                    c_limc = memset("pool", limc2[0][:, 0, :], 1.0)
                    c_limc = memset("pool", limc2[1][:, 0, :], 1.0, deps=[c_limc])

                    dma("sp", sem_dd, aks[0:496, :], cak[16:512, :])
                    dma("sp", sem_dd, avs[0:496, :], cav[16:512, :])

                    co, _, _ = norm_transpose(xs_d, NSAMP, hTs[:, :, :], [])
                    wdeps = [co] + W_ready
                    chk("b0")
                    for i in range(4):
                        bi, bank, o = fm_chunk(Wb, 0 + i * 128, 128, hTs, NSAMP, wdeps)
                        ev0 = cp("dve", QATs[0:64, 2 * i, :], bank[0:64, 0:NSAMP], deps=[o, c_qz])
                        ev = cp("dve", QATs[64:128, 2 * i + 1, :], bank[64:128, 0:NSAMP], deps=[o, c_qz])
                        st["mpfree"][bi] = [ev]
                    qats_ready = ev
                    for i in range(4):
                        bi, bank, o = fm_chunk(Wb, 512 + i * 128, 128, hTs, NSAMP, wdeps)
                        ev = cp("dve", KATs[:, i, 512:512 + NSAMP], bank[:, 0:NSAMP], deps=[o])
                        st["mpfree"][bi] = [ev]
                    kats_new = ev
                    chk("b1")
                    for i in range(4):
                        bi, bank, o = fm_chunk(Wb, 1536 + i * 128, 128, hTs, NSAMP, wdeps)
                        store_fm_heads(bank, bi, o, Qs, i, NQ, NSAMP)
                    for i in range(4):
                        bi, bank, o = fm_chunk(Wb, 2048 + i * 128, 128, hTs, NSAMP, wdeps)
                        store_fm_heads(bank, bi, o, Kss, i, LS, NSAMP)
                    chk("b2")
                    vas_new = None
                    for (c0, dst32, dst16) in ((512, aks[496:512, :], None), (1024, avs[496:512, :], "va"),
                                               (2048, bks, None), (2560, bvs, "vb")):
                        bi, bank, o = tm_group(Wb, c0, 512, hTs, 0, NSAMP, wdeps)
                        s = st["t32i"] % NT32
                        st["t32i"] += 1
                        ev = cp("dve", tm32[s][0:NSAMP, :], bank[0:NSAMP, :], deps=[o] + _aslist(st["t32last"][s]))
                        st["t32last"][s] = dma("pool", sem_t32[s], dst32, tm32[s][0:NSAMP, :], deps=[ev])
                        if dst16 == "va":
                            vas_new = cp("dve", VAs[0:NSAMP, 4, :, 0:64], tm32[s][0:NSAMP, :].rearrange("p (h d) -> p h d", h=8),
                                         deps=[ev, c_vas])
                            st["t32last"][s] = _last2(st["t32last"][s], vas_new)
                            st["mpfree"][bi] = [ev]
                        elif dst16 == "vb":
                            s2 = st["t16i"] % NT16
                            st["t16i"] += 1
                            ev2 = cp("pool", tm16[s2][0:NSAMP, :], tm32[s][0:NSAMP, :], deps=[ev, st["t16last"][s2]])
                            st["t16last"][s2] = dma("pool", sem_t16[s2], Vss[LS:LS + NSAMP, :], tm16[s2][0:NSAMP, :], deps=[ev2])
                            st["t32last"][s] = _last2(st["t32last"][s], ev2)
                            st["mpfree"][bi] = [ev]
                        else:
                            st["mpfree"][bi] = [ev]
                        chk("b3_%d" % c0)
                    chk("b")
                    cfree = [None, None]
                    c16free = [None, None]
                    kk = 0

                    def load_cast(src):
                        nonlocal kk
                        s = kk % 2
                        kk += 1
                        ld = dma("sp", sem_c[s], cst[s][:], src, deps=[cfree[s]])
                        return s, ld

                    kats_c = None
                    vas_c = None
                    for kt in range(4):
                        s, ld = load_cast(cak[kt * 128:(kt + 1) * 128, :])
                        cc = cp("dve", c16[s][:], cst[s][:], deps=[ld, c16free[s]])
                        cfree[s] = cc
                        b = kt % 2
                        t_ = None
                        for i in range(4):
                            t_ = tr(tp[b][:, i, :], c16[s][:, i * 128:(i + 1) * 128], ident[:], deps=[cc, st["tpfree"][0], c_id])
                        c16free[s] = t_
                        kats_c = act(KATs[:, :, kt * 128:(kt + 1) * 128], tp[b][:, 0:4, :], AF.Copy, deps=[t_])
                        st["tpfree"][0] = kats_c
                        s, ld = load_cast(cav[kt * 128:(kt + 1) * 128, :])
                        vas_c = cp("dve", VAs[:, kt, :, 0:64], cst[s][:].rearrange("p (h d) -> p h d", h=8), deps=[ld, c_vas])
                        cfree[s] = vas_c
                    kstlast = None
                    for g in range(LS // 512):
                        evs = []
                        for q in range(4):
                            kt = g * 4 + q
                            s, ld = load_cast(cbk[kt * 128:(kt + 1) * 128, :])
                            cc = cp("dve", c16[s][:], cst[s][:], deps=[ld, c16free[s]])
                            cfree[s] = cc
                            b = kt % 2
                            t_ = None
                            for i in range(4):
                                t_ = tr(tp[b][:, i, :], c16[s][:, i * 128:(i + 1) * 128], ident[:], deps=[cc, st["tpfree"][0], c_id])
                            c16free[s] = t_
                            ev = act(kst[:, :, q * 128:(q + 1) * 128], tp[b][:, 0:4, :], AF.Copy, deps=[t_, kstlast])
                            st["tpfree"][0] = ev
                            evs.append(ev)
                        d_ = None
                        for hh in range(8):
                            d_ = dma("pool", sem_kst, Kss[hh, 0:64, g * 512:(g + 1) * 512],
                                     kst[(hh % 2) * 64:(hh % 2) * 64 + 64, hh // 2, :], deps=evs)
                        kstlast = d_
                    for g in range(8):
                        r0, r1_ = g * (LS // 8), (g + 1) * (LS // 8)
                        dma("pool", sem_vc, Vss[r0:r1_, :], cbv[r0:r1_, :])
                    chk("c")
                    limcl = [None, None]
                    limclast = None
                    gprev = None
                    n1 = ts("dve", lc[:], lc[:], -1.0, None, ALU.mult, deps=[lcl])
                    for g in range(LS // CH2):
                        limc = limc2[g % 2]
                        limclast = limcl[g % 2]
                        lcg = lc[:, g * CH2:(g + 1) * CH2]
                        n1d = [n1]
                        init = 0.0
                        if gprev is not None:
                            cz = cp("dve", r8c[:, 0:1], Gc[:, CH2 - 1:CH2], deps=[gprev])
                            init = r8c[:, 0:1]
                            n1d = [n1, cz]
                        sc_ = P.op("dve", lambda e, init=init, lcg=lcg: e.tensor_tensor_scan(out=Gc[:], data0=on8c[:], data1=lcg,
                                                                                            initial=init, op0=ALU.mult, op1=ALU.add),
                                   n1d + [c_on8c, limclast])
                        d1 = ts("dve", limc[:, 1, :], Gc[:], 8.0, None, ALU.mult, deps=[sc_, limclast, c_limc])
                        d2 = stt(r8c[:], Gc[:], 8.0, limc[:, 1, :], ALU.mult, ALU.subtract, deps=[d1])
                        d3 = cp("dve", limc[:, 2, :], r8c[:], deps=[d2])
                        d4 = tt("dve", limc[:, 3, :], r8c[:], limc[:, 2, :], ALU.subtract, deps=[d3])
                        limcl[g % 2] = dma("pool", sem_limc2[g % 2], Kss[:, 64:68, g * CH2:(g + 1) * CH2], limc[:], deps=[d4])
                        gprev = d4
                    chk("d")
                    bi, bank, o = fm_chunk(Wb, 3072, 128, hTs, NSAMP, wdeps)
                    cz = cp("dve", G8[1][:, 511:512], Gc[:, CH2 - 1:CH2], deps=[gprev])
                    st["Gprev"] = (G8[1][:, 511:512], cz)
                    st["Gi"] = 0
                    fl = fpath(bank, bi, o, NSAMP, LS, Kss, NQ, True, bls)
                    st["Gprev"] = None
                    st["Gi"] = 0
                    chk("e")
                    st["Kready"] = [kats_c, kats_new]
                    st["Qready"] = [qats_ready]
                    st["Vready"] = [vas_c, vas_new]
                    for hp in range(4):
                        heads = (2 * hp, 2 * hp + 1)
                        tiles_of = []
                        for h in heads:
                            tiles = []
                            for kt in range(5):
                                nk = 128 if kt < 4 else NSAMP
                                tiles.append((KATs[:, hp, kt * 128:kt * 128 + nk], VAs[0:nk, kt, h, :], nk, 0, NSAMP,
                                              EB[0:nk, h, 4 - kt, 0:NSAMP], False))
                            tiles_of.append(tiles)
                        band2(heads, [QATs[:, heads[0], :], QATs[:, heads[1], :]], tiles_of, NSAMP, NQ, False)
                    st["pend_norm"]()
                    st["pend_norm"] = None
                    P.finalize()
                    if stop == "p1s":
                        raise _Stop(nc)
                st["Kready"] = []
                st["Qready"] = []
                st["Vready"] = []
                st["mpfree"] = [[], [], []]
                for k_ in ("xfree", "hbfree", "tpfree", "fmlast", "t32last", "t16last", "rsfree", "sTfree",
                           "yalast", "osfree"):
                    st[k_] = [None] * len(st[k_])
                st["oTfree"] = [[], []]
                st["pbfree"] = [[None, None], [None, None]]
                st["ptfree"] = [[None, None], [None, None]]
                st["rdfree"] = [None, None]
                for k_ in ("limlast", "qstlast", "lflast", "elfree", "r8free"):
                    st[k_] = None

            hT = [sb("hT%d" % i, [128, DC, 512], BF16) for i in range(2)]
            QAT = sb("QATz", [128, 8, 512], BF16)
            c_qzp = memset("pool", QAT[:], 0.0)
            KAT = sb("KAT", [128, 4, 1024], BF16)
            VA = sb("VA", [128, 8, 8, 65], BF16)
            c_va = memset("pool", VA[:, :, :, 64:65], 1.0)
            def xsrc(p, t):
                return xw[p * 512 + t * 128:p * 512 + (t + 1) * 128, :]

            hnds = {}
            cps_of = {}

            def emit_a(p, t):
                if p < NSLOT:
                    hnds[(p, t)] = norm_a(xsrc(p, t), 128)

            def emit_b(p, t):
                if p < NSLOT:
                    co = norm_b(hnds.pop((p, t)), hT[p % 2][:, :, t * 128:(t + 1) * 128], [st["hTfree"][p % 2]])
                    cps_of.setdefault(p, []).append(co)

            for t in range(4):
                emit_a(0, t) if t < 2 else None
            emit_b(0, 0)
            emit_b(0, 1)
            emit_a(0, 2)
            emit_a(0, 3)
            emit_b(0, 2)
            emit_b(0, 3)

            WD = {}

            def make_slot(p):
                own = (p % 4 == 3)
                ctx = (p % 4 == 2)
                m = p // 4
                hTp = hT[p % 2]
                lm = {"o": None}
                groups = []

                def g_kb(i):
                    bi, bank, o = fm_chunk(Wb, 2048 + i * 128, 128, hTp, 512, WD[p])
                    store_fm_heads(bank, bi, o, Ks, i, p * 512, 512)
                    lm["o"] = o

                def g_f():
                    bi, bank, o = fm_chunk(Wb, 3072, 128, hTp, 512, WD[p])
                    fpath(bank, bi, o, 512, p * 512, Ks, m * 512, own, blf_own[m * 512:(m + 1) * 512, :] if own else None)
                    lm["o"] = o

                def g_ka(i):
                    kcol = 512 if own else 0
                    bi, bank, o = fm_chunk(Wb, 512 + i * 128, 128, hTp, 512, WD[p])
                    ev = cp("dve", KAT[:, i, kcol:kcol + 512], bank[:, :], deps=[o] + _aslist(st["KATfree"]))
                    st["mpfree"][bi] = [ev]
                    st["Kready"] = [ev]
                    lm["o"] = o

                def g_qa(i):
                    bi, bank, o = fm_chunk(Wb, 0 + i * 128, 128, hTp, 512, WD[p])
                    ev0 = cp("dve", QAT[0:64, 2 * i, :], bank[0:64, :], deps=[o, c_qzp] + _aslist(st["QATfree"]))
                    ev = cp("dve", QAT[64:128, 2 * i + 1, :], bank[64:128, :], deps=[o, c_qzp] + _aslist(st["QATfree"]))
                    st["mpfree"][bi] = [ev]
                    st["Qready"] = [ev]
                    lm["o"] = o

                def g_qb(i):
                    bi, bank, o = fm_chunk(Wb, 1536 + i * 128, 128, hTp, 512, WD[p])
                    store_fm_heads(bank, bi, o, Qs, i, m * 512, 512)
                    lm["o"] = o

                def g_vb(t):
                    tok0 = p * 512 + t * 128
                    otok0 = m * 512 + t * 128
                    bi, bank, o = tm_group(Wb, 2560, 512, hTp, t * 128, 128, WD[p])
                    lm["o"] = o
                    s2 = st["t16i"] % NT16
                    st["t16i"] += 1
                    if own:
                        s = st["t32i"] % NT32
                        st["t32i"] += 1
                        ev = cp("dve", tm32[s][:], bank[:, :], deps=[o] + _aslist(st["t32last"][s]))
                        d32 = dma("pool", sem_t32[s], bv_own[otok0:otok0 + 128, :], tm32[s][:], deps=[ev])
                        ev2 = cp("pool", tm16[s2][:], tm32[s][:], deps=[ev, st["t16last"][s2]])
                        st["t32last"][s] = [d32, ev2]
                        st["mpfree"][bi] = [ev]
                    else:
                        ev2 = cp("act", tm16[s2][:], bank[:, :], deps=[o, st["t16last"][s2]])
                        st["mpfree"][bi] = [ev2]
                    st["t16last"][s2] = dma("pool", sem_t16[s2], Vs[tok0:tok0 + 128, :], tm16[s2][:], deps=[ev2])

                def g_kbt(t):
                    otok0 = m * 512 + t * 128
                    bi, bank, o = tm_group(Wb, 2048, 512, hTp, t * 128, 128, WD[p])
                    lm["o"] = o
                    s = st["t32i"] % NT32
                    st["t32i"] += 1
                    ev = cp("dve", tm32[s][:], bank[:, :], deps=[o] + _aslist(st["t32last"][s]))
                    st["t32last"][s] = dma("pool", sem_t32[s], bk_own[otok0:otok0 + 128, :], tm32[s][:], deps=[ev])
                    st["mpfree"][bi] = [ev]

                def g_va(t):
                    kt = (4 if own else 0) + t
                    bi, bank, o = tm_group(Wb, 1024, 512, hTp, t * 128, 128, WD[p])
                    lm["o"] = o
                    ev2 = cp("dve", VA[:, kt, :, 0:64], bank[:, :].rearrange("p (h d) -> p h d", h=8),
                             deps=[o, c_va] + _aslist(st["VAfree"]))
                    st["Vready"] = [ev2]
                    st["mpfree"][bi] = [ev2]
                    if own and m == NOWN - 1:
                        s = st["t32i"] % NT32
                        st["t32i"] += 1
                        ev = cp("dve", tm32[s][:], bank[:, :], deps=[o] + _aslist(st["t32last"][s]))
                        st["t32last"][s] = dma("pool", sem_t32[s], avp[t * 128:(t + 1) * 128, :], tm32[s][:], deps=[ev])
                        st["mpfree"][bi] = [ev, ev2]

                def g_kat(t):
                    bi, bank, o = tm_group(Wb, 512, 512, hTp, t * 128, 128, WD[p])
                    lm["o"] = o
                    s = st["t32i"] % NT32
                    st["t32i"] += 1
                    ev = cp("dve", tm32[s][:], bank[:, :], deps=[o] + _aslist(st["t32last"][s]))
                    st["t32last"][s] = dma("pool", sem_t32[s], akp[t * 128:(t + 1) * 128, :], tm32[s][:], deps=[ev])
                    st["mpfree"][bi] = [ev]

                def g_band(hp, filler=None):
                    heads = (2 * hp, 2 * hp + 1)
                    tiles_of = []
                    for h in heads:
                        tiles = []
                        for kt in range(8):
                            q0 = max(0, kt - 4)
                            q1 = min(3, kt)
                            d0 = 4 + q0 - kt
                            n = (q1 - q0 + 1) * 128
                            tiles.append((KAT[:, hp, kt * 128:(kt + 1) * 128], VA[:, kt, h, :], 128, q0 * 128, n,
                                          EB[:, h, d0:d0 + (q1 - q0 + 1), :], kt < 4))
                        tiles_of.append(tiles)
                    pvl = band2(heads, [QAT[:, heads[0], :], QAT[:, heads[1], :]], tiles_of, 512, m * 512, m == 0, filler)
                    if hp == 3:
                        st["KATfree"] = pvl
                        st["VAfree"] = pvl
                        st["QATfree"] = pvl

                def mk(f, a_):
                    return lambda: f(a_)

                for i in range(4):
                    groups.append(mk(g_kb, i))
                groups.append(g_f)
                if own or ctx:
                    for i in range(4):
                        groups.append(mk(g_ka, i))
                if own:
                    for i in range(4):
                        groups.append(mk(g_qa, i))
                    for i in range(4):
                        groups.append(mk(g_qb, i))
                for t in range(4):
                    groups.append(mk(g_vb, t))
                    if own:
                        groups.append(mk(g_kbt, t))
                    if own or ctx:
                        groups.append(mk(g_va, t))
                    if own and m == NOWN - 1:
                        groups.append(mk(g_kat, t))
                nproj = len(groups)
                ng = nproj
                a_at = {0: [0, 1], max(2, (2 * ng) // 8 + 1): [2], max(3, (4 * ng) // 8): [3]}
                b_at = {max(1, ng // 8): 0, max(2, (2 * ng) // 8 + 1): 1, max(3, (4 * ng) // 8): 2, max(4, (5 * ng) // 8 + 1): 3}
                done_a, done_b = set(), set()
                units = []

                def unit(gi, g):
                    def run():
                        if gi == 0:
                            WD[p] = cps_of.pop(p) + W_ready
                        if gi in b_at:
                            t = b_at[gi]
                            if t in done_a:
                                emit_b(p + 1, t)
                                done_b.add(t)
                        for t in a_at.get(gi, []):
                            emit_a(p + 1, t)
                            done_a.add(t)
                        g()
                        if gi == 2 and st["pend_norm"] is not None and not own:
                            st["pend_norm"]()
                            st["pend_norm"] = None
                        if gi == nproj - 1:
                            st["hTfree"][p % 2] = lm["o"]
                            for t in range(4):
                                if t not in done_a:
                                    emit_a(p + 1, t)
                                if t not in done_b:
                                    emit_b(p + 1, t)
                    return run

                for gi, g in enumerate(groups):
                    units.append(("plain", unit(gi, g), p))
                if own:
                    for hp in range(4):
                        units.append(("band", (lambda f, hp=hp: g_band(hp, f)), p))
                return units

            seq = []
            for p in range(NSLOT):
                seq.extend(make_slot(p))
            pos = [0]
            while pos[0] < len(seq):
                kind, fn, slot = seq[pos[0]]
                pos[0] += 1
                if kind == "band":
                    def filler(slot=slot):
                        i = pos[0]
                        while i < len(seq) and seq[i][0] == "band":
                            i += 1
                        if i < len(seq) and seq[i][2] % 4 in (0, 1) and seq[i][2] <= slot + 2:
                            u = seq.pop(i)
                            u[1]()
                    fn(filler)
                else:
                    fn()
                    if st["pend_norm"] is not None:
                        st["pend_norm"]()
                        st["pend_norm"] = None
            if st["pend_norm"] is not None:
                st["pend_norm"]()
                st["pend_norm"] = None
            P.finalize()
            if stop == "p1":
                raise _Stop(nc)

        ph = ExitStack()
        with ph:
            def sb(name, shape, dt):
                return ph.enter_context(nc.sbuf_tensor(name, shape, dt))

            def ps(name, shape, dt):
                return ph.enter_context(nc.psum_tensor(name, shape, dt))

            NKT = L // 128
            Kb = [sb("Kb%d" % i, [68, L], BF16) for i in range(2)]
            Vb = [sb("Vb%d" % i, [128, NKT, 65], BF16) for i in range(2)]
            Qb = [sb("Qb%d" % i, [68, NQA], BF16) for i in range(2)]
            sem_kc = [[newsem(ph) for _ in range(4)] for _ in range(2)]
            sem_ks = [newsem(ph), newsem(ph)]
            if sample:
                Ksb = [sb("Ksb%d" % i, [68, LSP], BF16) for i in range(2)]
                Vsb = [sb("Vsb%d" % i, [128, LSP // 128, 65], BF16) for i in range(2)]
            NPB = 4
            pb2 = [sb("pb2_%d" % i, [128, 512], BF16) for i in range(NPB)]
            osb2 = sb("osb2", [65, 512], F32)
            rd2 = sb("rd2", [65, 512], F32)
            yb2 = [sb("yb2_%d" % i, [64, 512], BF16) for i in range(2)]
            sem_yb = [newsem(ph), newsem(ph)]
            sT2 = [ps("sT2_%d" % i, [128, 512], F32) for i in range(NPB)]
            oT2 = [ps("oT2_%d" % i, [128, 512], F32) for i in range(2)]
            bc2 = ps("bc2", [128, 512], F32)

            wq32 = [sb("wq32_%d" % i, [128, C3], F32) for i in range(3)]
            wq16 = [sb("wq16_%d" % i, [128, C3], BF16) for i in range(3)]
            sem_wq = [newsem(ph) for _ in range(3)]
            sem_wqs = [newsem(ph) for _ in range(3)]
            wq = dict(i=0, free32=[None] * 3, last16=[None] * 3, pend=[])
            wtasks = [(w3, W3s, C3, dc, True) for dc in range(DC)] + [(wbr, Wbrs, D, dc, False) for dc in range(DC)] + \
                     [(wout, Wouts, D, dc, False) for dc in range(DC)]

            def wtask_load():
                if not wtasks:
                    return
                src, dst, W_, dc, scaled = wtasks.pop(0)
                k_ = wq["i"] % 3
                wq["i"] += 1
                ld = dma("sp", sem_wq[k_], wq32[k_][:, 0:W_], src[dc * 128:(dc + 1) * 128, :], deps=[wq["free32"][k_]])
                wq["pend"].append((k_, ld, dst, W_, dc, scaled))

            def wtask_convert():
                if not wq["pend"]:
                    return
                k_, ld, dst, W_, dc, scaled = wq["pend"].pop(0)
                if scaled:
                    cv = ts("dve", wq16[k_][:, 0:W_], wq32[k_][:, 0:W_], gpre_sb[:, dc:dc + 1], None, ALU.mult,
                            deps=[ld, wq["last16"][k_]])
                else:
                    cv = cp("dve", wq16[k_][:, 0:W_], wq32[k_][:, 0:W_], deps=[ld, wq["last16"][k_]])
                wq["free32"][k_] = cv
                wq["last16"][k_] = dma("pool", sem_wqs[k_], dst[dc * 128:(dc + 1) * 128, :], wq16[k_][:, 0:W_], deps=[cv])

            for _ in range(3):
                wtask_load()

            c_v = [memset("pool", Vb[i][:, :, 64:65], 1.0) for i in range(2)]
            if sample:
                c_vs = [memset("pool", Vsb[i][:, :, 64:65], 1.0) for i in range(2)]
            bufree = [None, None]
            s2 = dict(sTfree=[None] * NPB, pbfree=[None] * NPB, oTfree=[None, None], oTfree_b=[None, None], pend=None,
                      ji=0, oi=0, rdfree=None, osfree=None, bcfree=None, yi=0, ylast=[None, None])

            for h in range(8):
                par = h % 2
                fdep = [bufree[par]]
                NCH = 4
                TPC = NKT // NCH
                chunk_ld = []
                for c_ in range(NCH):
                    k0, k1 = c_ * TPC, (c_ + 1) * TPC
                    l_ = None
                    if c_ == 0:
                        l_ = dma("sp", sem_kc[par][c_], Qb[par][:], Qs[h], deps=fdep)
                    l_ = dma("sp", sem_kc[par][c_], Kb[par][:, k0 * 128:k1 * 128], Ks[h, :, k0 * 128:k1 * 128], deps=fdep)
                    for g in range(k0, k1, 16):
                        g1 = min(k1, g + 16)
                        l_ = dma("sp", sem_kc[par][c_], Vb[par][:, g:g1, 0:64],
                                 Vs[g * 128:g1 * 128, h * 64:(h + 1) * 64].rearrange("(k p) d -> p k d", p=128), deps=fdep)
                    chunk_ld.append(l_)
                samp_ld = None
                if sample:
                    samp_ld = dma("sp", sem_ks[par], Ksb[par][:, 0:LS + NSAMP], Kss[h, :, 0:LS + NSAMP], deps=fdep)
                    for g in range(0, LS // 128, 16):
                        g1 = min(LS // 128, g + 16)
                        samp_ld = dma("sp", sem_ks[par], Vsb[par][:, g:g1, 0:64],
                                      Vss[g * 128:g1 * 128, h * 64:(h + 1) * 64].rearrange("(k p) d -> p k d", p=128),
                                      deps=fdep)
                    samp_ld = dma("sp", sem_ks[par], Vsb[par][0:NSAMP, LS // 128, 0:64],
                                  Vss[LS:LS + NSAMP, h * 64:(h + 1) * 64], deps=fdep)
                if h > 0:
                    for _ in range(3):
                        wtask_convert()
                    for _ in range(3):
                        wtask_load()
                def kdeps(kt):
                    c_ = kt // TPC
                    return [chunk_ld[0], c_v[par]] + ([chunk_ld[c_]] if c_ > 0 else [])
                sdeps_ = [chunk_ld[0], samp_ld, c_vs[par]] if sample else []

                groups = []
                for m in range(NOWN):
                    jobs = []
                    nfull = (4 * m + 3) * 4
                    for kt in range(nfull):
                        bias = smask[:, kt // 4:kt // 4 + 1] if kt < 12 else None
                        jobs.append((Kb[par][:, kt * 128:(kt + 1) * 128], Qb[par][:, m * 512:(m + 1) * 512],
                                     Vb[par][:, kt, :], 128, 0, 512, bias, False, kdeps(kt)))
                    for b in range(4):
                        kt = nfull + b
                        jobs.append((Kb[par][:, kt * 128:(kt + 1) * 128], Qb[par][:, m * 512 + b * 128:(m + 1) * 512],
                                     Vb[par][:, kt, :], 128, b * 128, 512 - b * 128, None, True, kdeps(kt)))
                    groups.append((jobs, 512, m * 512))
                if sample:
                    jobs = []
                    for kt in range(LS // 128):
                        jobs.append((Ksb[par][:, kt * 128:(kt + 1) * 128], Qb[par][:, NQ:NQ + NSAMP],
                                     Vsb[par][:, kt, :], 128, 0, NSAMP, None, False, sdeps_))
                    kt = LS // 128
                    jobs.append((Ksb[par][:, LS:LS + NSAMP], Qb[par][:, NQ:NQ + NSAMP],
                                 Vsb[par][0:NSAMP, kt, :], NSAMP, 0, NSAMP, None, True, sdeps_))
                    groups.append((jobs, NSAMP, NQ))

                flat = []
                for gi, (jobs, nq, ycol) in enumerate(groups):
                    for ji, jb in enumerate(jobs):
                        flat.append((gi, ji, len(jobs), jb))
                nflat = len(flat)
                LOOK = 3
                qk_pend = {}

                def emit_qk(fi):
                    gi, ji, nj, (Kap, Qap, Vap, nk, c0, n, bias, trif, jd) = flat[fi]
                    b = s2["ji"] % NPB
                    s2["ji"] += 1
                    o = mm(sT2[b][0:nk, 0:n], Kap, Qap, True, True, deps=jd + [s2["sTfree"][b]])
                    qk_pend[fi] = (b, o)

                for fi in range(min(LOOK, nflat)):
                    emit_qk(fi)
                cur_o = None
                last_pv = None
                for fi in range(nflat):
                    gi, ji, nj, (Kap, Qap, Vap, nk, c0, n, bias, trif, jd) = flat[fi]
                    if fi + LOOK < nflat:
                        emit_qk(fi + LOOK)
                    b, qk = qk_pend.pop(fi)
                    if ji == 0:
                        cur_o = s2["oi"] % 2
                        s2["oi"] += 1
                    kw = {"scale": SCALE}
                    if bias is not None:
                        kw["bias"] = bias
                    ex = act(pb2[b][0:nk, 0:n], sT2[b][0:nk, 0:n], AF.Exp, deps=[qk, s2["pbfree"][b], l_bf], **kw)
                    s2["sTfree"][b] = ex
                    pdep = ex
                    if trif:
                        w_ = min(128, n)
                        pdep = tt("pool", pb2[b][0:nk, 0:w_], pb2[b][0:nk, 0:w_], tri[0:nk, 0:w_], ALU.mult, deps=[ex, c_tri])
                    pv = mm(oT2[cur_o][0:65, c0:c0 + n], Vap, pb2[b][0:nk, 0:n], ji == 0, ji == nj - 1,
                            deps=[pdep, s2["oTfree"][cur_o], s2["oTfree_b"][cur_o]] + jd, skip=True)
                    s2["pbfree"][b] = pv
                    last_pv = pv
                    if ji == nj - 1:
                        jobs, nq, ycol = groups[gi]
                        o_ = oT2[cur_o]
                        c1 = act(osb2[0:65, 0:nq], o_[0:65, 0:nq], AF.Copy, deps=[pv, s2["osfree"]])
                        s2["oTfree"][cur_o] = c1
                        r1 = P.op("dve", lambda e, nq=nq: e.reciprocal(out=rd2[64:65, 0:nq], in_=osb2[64:65, 0:nq]),
                                  [c1, s2["rdfree"]])

                        def finish(r1=r1, c1=c1, nq=nq, ycol=ycol, h=h):
                            bc = mm(bc2[0:64, 0:nq], onesf[64:65, 0:64], rd2[64:65, 0:nq], True, True,
                                    deps=[r1, s2["bcfree"]])
                            s2["rdfree"] = bc
                            yi = s2["yi"] % 2
                            s2["yi"] += 1
                            y1 = tt("dve", yb2[yi][:, 0:nq], osb2[0:64, 0:nq], bc2[0:64, 0:nq], ALU.mult,
                                    deps=[c1, bc, s2["ylast"][yi]])
                            s2["bcfree"] = y1
                            s2["osfree"] = y1
                            s2["ylast"][yi] = dma("pool", sem_yb[yi], YB[h * 64:(h + 1) * 64, ycol:ycol + nq], yb2[yi][:, 0:nq],
                                                  deps=[y1])
                        s2["pend"] = [finish, 6]
                    elif s2["pend"] is not None:
                        s2["pend"][1] -= 1
                        if s2["pend"][1] <= 0:
                            s2["pend"][0]()
                            s2["pend"] = None
                if s2["pend"] is not None:
                    s2["pend"][0]()
                    s2["pend"] = None
                bufree[par] = last_pv
            while wq["pend"] or wtasks:
                for _ in range(3):
                    wtask_convert()
                for _ in range(3):
                    wtask_load()
            P.finalize()
            if stop == "p2":
                raise _Stop(nc)

        ph = ExitStack()
        with ph:
            def sb(name, shape, dt):
                return ph.enter_context(nc.sbuf_tensor(name, shape, dt))

            def ps(name, shape, dt):
                return ph.enter_context(nc.psum_tensor(name, shape, dt))

            W3b = sb("W3b", [128, DC, C3], BF16)
            Wbrb = sb("Wbrb", [128, DC, D], BF16)
            Woutb = sb("Woutb", [128, DC, D], BF16)
            gpb = sb("gpb", [128, D], F32)
            xs4 = [sb("xs4_%d" % i, [128, 4, D], F32) for i in range(2)]
            junk3 = sb("junk3", [128, D], BF16)
            ssq3 = sb("ssq3", [128, 8], F32)
            rsd3 = sb("rsd3", [128, 8], F32)
            hb3 = [sb("hb3_%d" % i, [128, D], BF16) for i in range(2)]
            hT3 = [sb("hT3_%d" % i, [128, DC, 512], BF16) for i in range(2)]
            GZ = sb("GZ", [128, 8, 512], BF16)
            GM = sb("GM", [128, 16, 512], BF16)
            yl = sb("yl", [128, 8, 512], BF16)
            gmul = sb("gmul", [128, 8, 512], BF16)
            mrg = sb("mrg", [128, DC, 512], BF16)
            t1 = [sb("t1_%d" % i, [128, 512], F32) for i in range(2)]
            t2 = [sb("t2_%d" % i, [128, 512], F32) for i in range(2)]
            osb3 = [sb("osb3_%d" % i, [128, D], F32) for i in range(2)]
            ss2 = sb("ss2", [128, 4], F32)
            rs2 = sb("rs2", [128, 2], F32)
            sem_m3 = newsem(ph)
            sem_w3 = [newsem(ph) for _ in range(8)]
            sem_x3 = [newsem(ph), newsem(ph)]
            sem_y3 = newsem(ph)
            sem_o3 = [newsem(ph), newsem(ph)]
            tp3 = [ps("tp3_%d" % i, [128, DC, 128], BF16) for i in range(2)]
            NM3 = 5
            mp3 = [ps("mp3_%d" % i, [128, 512], F32) for i in range(NM3)]

            mhalf = sb("mhalf", [128, 1], F32)
            c_mh = memset("pool", mhalf[:], -0.5)
            l_gpb = dma("sp", sem_m3, gpb[:], gpost.partition_broadcast(128))
            w3ld = []
            for cb in range(6):
                w3ld.append(dma("sp", sem_w3[cb], W3b[:, :, cb * 512:(cb + 1) * 512],
                                W3s[:, cb * 512:(cb + 1) * 512].rearrange("(c p) n -> p c n", p=128)))
            wbrld = dma("sp", sem_w3[6], Wbrb[:], Wbrs.rearrange("(c p) n -> p c n", p=128))
            woutld = dma("sp", sem_w3[7], Woutb[:], Wouts.rearrange("(c p) n -> p c n", p=128))

            s3 = dict(mpi=0, mpfree=[[] for _ in range(NM3)], junkfree=None, ssfree=None, rs2free=None, hbfree=[None, None],
                      tpfree=[None, None], xfree=[None, None], hTfree=[None, None],
                      gzfree=None, gmfree=None, ylfree=None, gmulfree=None, mrgfree=None, t1free=[None, None],
                      t2free=[None, None], oi=0, olast=[None, None], rsfree=[None] * 8, hi=0)

            def mm3(pairs, M, N, deps):
                bi = s3["mpi"] % NM3
                s3["mpi"] += 1
                bank = mp3[bi]
                o = None
                for kk, (l_, r_) in enumerate(pairs):
                    o = mm(bank[0:M, 0:N], l_, r_, kk == 0, kk == len(pairs) - 1, deps=list(deps) + s3["mpfree"][bi])
                return bi, bank, o

            units = [(xw, (4 * m + 3) * 512, m * 512, 512, y_own, m * 512) for m in range(NOWN)]
            if sample:
                units.append((xs_d, 0, NQ, NSAMP, ys_o, 0))
            NU = len(units)
            xld = {}
            hnd3 = {}
            cps3 = {}

            def u_geom(u):
                xsrc, x0, ycol, ntok, ydst, ybase = units[u]
                return (ntok + 127) // 128, min(128, ntok)

            def p3_load(u):
                if u >= NU:
                    return
                xsrc, x0, ycol, ntok, ydst, ybase = units[u]
                ntile, tn = u_geom(u)
                ld = None
                for t in range(ntile):
                    ld = dma("sp", sem_x3[u % 2], xs4[u % 2][0:tn, t, :], xsrc[x0 + t * 128:x0 + t * 128 + tn, :],
                             deps=[s3["xfree"][u % 2]])
                xld[u] = ld

            def p3_a(u, t):
                if u >= NU:
                    return
                ntile, tn = u_geom(u)
                if t >= ntile:
                    return
                c = (u % 2) * 4 + t
                ld = xld[u]
                xin = xs4[u % 2][0:tn, t, :]
                sq = act(junk3[0:tn, :], xin, AF.Square, deps=[ld, s3["junkfree"]], accum_out=ssq3[0:tn, c:c + 1])
                s3["junkfree"] = sq
                sr = ts("pool", rsd3[0:tn, c:c + 1], ssq3[0:tn, c:c + 1], 1.0 / D, EPS, ALU.mult, ALU.add,
                        deps=[sq, s3["rsfree"][c]])
                rc = tt("pool", rsd3[0:tn, c:c + 1], rsd3[0:tn, c:c + 1], mhalf[0:tn, :], ALU.pow, deps=[sr, c_mh])
                b = s3["hi"] % 2
                s3["hi"] += 1
                sc = ts("dve", hb3[b][0:tn, :], xin, rsd3[0:tn, c:c + 1], None, ALU.mult, deps=[rc, s3["hbfree"][b]])
                s3["rsfree"][c] = sc
                hnd3[(u, t)] = (b, sc)

            def p3_b(u, t):
                if u >= NU:
                    return
                ntile, tn = u_geom(u)
                if t >= ntile:
                    return
                b, sc = hnd3.pop((u, t))
                t_ = None
                for dc in range(DC):
                    t_ = tr(tp3[b][:, dc, 0:tn], hb3[b][0:tn, dc * 128:(dc + 1) * 128], ident[0:tn, 0:tn],
                            deps=[sc, s3["tpfree"][b]])
                s3["hbfree"][b] = t_
                co = act(hT3[u % 2][:, :, t * 128:t * 128 + tn], tp3[b][:, :, 0:tn], AF.Copy,
                         deps=[t_, s3["hTfree"][u % 2]])
                s3["tpfree"][b] = co
                cps3.setdefault(u, []).append(co)

            p3_load(0)
            p3_load(1)
            for t in range(4):
                p3_a(0, t) if t < 2 else None
            p3_b(0, 0)
            p3_b(0, 1)
            p3_a(0, 2)
            p3_a(0, 3)
            p3_b(0, 2)
            p3_b(0, 3)

            for u in range(NU):
                xsrc, x0, ycol, ntok, ydst, ybase = units[u]
                ntile, tn = u_geom(u)
                N = ntok
                hTu = hT3[u % 2]
                cps = cps3.pop(u)
                dma("sp", sem_y3, yl[:, 0:4, 0:N], YA[:, ycol:ycol + N].rearrange("(c p) t -> p c t", p=128),
                    deps=[s3["ylfree"]])
                yld = dma("sp", sem_y3, yl[:, 4:8, 0:N], YB[:, ycol:ycol + N].rearrange("(c p) t -> p c t", p=128),
                          deps=[s3["ylfree"]])
                lastmm = None
                gz_last = None
                gm_last = None
                g1 = None
                for c in range(24):
                    pairs = [(W3b[:, dc, c * 128:(c + 1) * 128], hTu[:, dc, 0:N]) for dc in range(DC)]
                    bi, bank, o = mm3(pairs, 128, N, cps + [w3ld[c // 4]])
                    lastmm = o
                    if c < 8:
                        ev = act(GZ[:, c, 0:N], bank[:, 0:N], AF.Silu, deps=[o, s3["gzfree"]])
                        gz_last = ev
                    else:
                        ev = act(GM[:, c - 8, 0:N], bank[:, 0:N], AF.Sigmoid, deps=[o, s3["gmfree"]])
                        gm_last = ev
                    s3["mpfree"][bi] = [ev]
                    if c == 7:
                        g1 = tt("dve", gmul[:, :, 0:N], yl[:, :, 0:N], GZ[:, :, 0:N], ALU.mult,
                                deps=[yld, gz_last, s3["gmulfree"]])
                        s3["ylfree"] = g1
                        s3["gzfree"] = g1
                        p3_a(u + 1, 0)
                        p3_a(u + 1, 1)
                    if c == 15:
                        p3_b(u + 1, 0)
                        p3_b(u + 1, 1)
                        p3_a(u + 1, 2)
                        p3_a(u + 1, 3)
                s3["hTfree"][u % 2] = lastmm
                lastbr = None
                mr_ops = []
                for dc in range(DC):
                    pa = [(Wbrb[:, c, dc * 128:(dc + 1) * 128], gmul[:, c, 0:N]) for c in range(4)]
                    bia, banka, oa = mm3(pa, 128, N, [g1, wbrld])
                    pbb = [(Wbrb[:, 4 + c, dc * 128:(dc + 1) * 128], gmul[:, 4 + c, 0:N]) for c in range(4)]
                    bib, bankb, ob = mm3(pbb, 128, N, [g1, wbrld])
                    lastbr = ob
                    k2 = dc % 2
                    m1 = tt("dve", t1[k2][:, 0:N], banka[:, 0:N], GM[:, dc, 0:N], ALU.mult, deps=[oa, gm_last, s3["t1free"][k2]])
                    s3["mpfree"][bia] = [m1]
                    m2 = tt("dve", t2[k2][:, 0:N], bankb[:, 0:N], GM[:, 8 + dc, 0:N], ALU.mult, deps=[ob, gm_last, s3["t2free"][k2]])
                    s3["mpfree"][bib] = [m2]
                    m3_ = tt("pool", mrg[:, dc, 0:N], t1[k2][:, 0:N], t2[k2][:, 0:N], ALU.add, deps=[m1, m2, s3["mrgfree"]])
                    s3["t1free"][k2] = m3_
                    s3["t2free"][k2] = m3_
                    mr_ops.append(m3_)
                    if dc == 3:
                        p3_b(u + 1, 2)
                        p3_b(u + 1, 3)
                s3["gmulfree"] = lastbr
                s3["gmfree"] = mr_ops[-1]
                lastout = None
                fin = None
                for t in range(ntile):
                    oi = s3["oi"] % 2
                    s3["oi"] += 1
                    ob_ = osb3[oi]
                    sqs = []
                    for hf in range(2):
                        pairs = [(mrg[:, dc, t * 128:t * 128 + tn], Woutb[:, dc, hf * 512:(hf + 1) * 512]) for dc in range(DC)]
                        bi, bank, o = mm3(pairs, tn, 512, mr_ops + [woutld])
                        lastout = o
                        ev = act(ob_[0:tn, hf * 512:(hf + 1) * 512], bank[0:tn, :], AF.Copy, deps=[o, s3["olast"][oi]])
                        s3["mpfree"][bi] = [ev]
                        sq = act(junk3[0:tn, 0:512], ob_[0:tn, hf * 512:(hf + 1) * 512], AF.Square,
                                 deps=[ev, s3["ssfree"], s3["junkfree"]], accum_out=ss2[0:tn, hf:hf + 1])
                        s3["junkfree"] = sq
                        sqs.append(sq)
                    a_ = tt("dve", ss2[0:tn, 2:3], ss2[0:tn, 0:1], ss2[0:tn, 1:2], ALU.add, deps=sqs)
                    s3["ssfree"] = a_
                    sr = ts("pool", rs2[0:tn, 0:1], ss2[0:tn, 2:3], 1.0 / D, EPS, ALU.mult, ALU.add, deps=[a_, s3["rs2free"]])
                    rc = tt("pool", rs2[0:tn, 0:1], rs2[0:tn, 0:1], mhalf[0:tn, :], ALU.pow, deps=[sr, c_mh])
                    f1 = stt(ob_[0:tn, :], ob_[0:tn, :], rs2[0:tn, 0:1], gpb[0:tn, :], ALU.mult, ALU.mult, deps=[rc, l_gpb])
                    s3["rs2free"] = f1
                    f2 = tt("pool", ob_[0:tn, :], ob_[0:tn, :], xs4[u % 2][0:tn, t, :], ALU.add, deps=[f1])
                    s3["olast"][oi] = dma("pool", sem_o3[oi], ydst[ybase + t * 128:ybase + t * 128 + tn, :],
                                          ob_[0:tn, :], deps=[f2])
                    fin = f2
                s3["mrgfree"] = lastout
                s3["xfree"][u % 2] = fin
                p3_load(u + 2)
                for t in range(4):
                    if (u + 1, t) in hnd3:
                        p3_b(u + 1, t)
            P.finalize()
    return nc


_CACHE = {}


def _get_nc(NSLOT=32, sample=True):
    key = (NSLOT, sample)
    if key not in _CACHE:
        _CACHE[key] = build(NSLOT, sample)
    return _CACHE[key]


def make_in_maps(inputs, NSLOT=32):
    f32 = np.float32
    xp = np.asarray(inputs["x_prompt"], f32)
    B, S, _ = xp.shape
    L = NSLOT * 512
    w_in = np.asarray(inputs["w_in"], f32)[0]
    cols1 = np.r_[0:512, 512:1024, 1024:1536, 2048:2560, 2560:3072, 3072:3584, 4096:4104]
    cols3 = np.r_[1536:2048, 3584:4096, 4104:5128, 5128:6152]
    w1 = np.ascontiguousarray(w_in[:, cols1])
    w3 = np.ascontiguousarray(w_in[:, cols3])
    wbr = np.ascontiguousarray(np.concatenate([np.asarray(inputs["w_br_a"], f32)[0], np.asarray(inputs["w_br_b"], f32)[0]], 0))
    wout = np.ascontiguousarray(np.asarray(inputs["w_out"], f32)[0])
    maps = []
    for c in range(8):
        b, j = c // 4, c % 4
        start = (j - 3) * 512
        win = np.zeros((L, D), f32)
        lo = max(start, 0)
        hi = min(start + L, S)
        win[lo - start:hi - start] = xp[b, lo:hi]
        sm = np.zeros((128, 3), f32)
        for s_ in range(3):
            if s_ + j - 3 < 0:
                sm[:, s_] = -BIG
        maps.append({
            "xw": win, "w1": w1, "w3": w3, "wbr": wbr, "wout": wout,
            "gpre": np.ascontiguousarray(np.asarray(inputs["g_pre"], f32)[0]),
            "gpost": np.ascontiguousarray(np.asarray(inputs["g_post"], f32)[0]),
            "bfv": np.ascontiguousarray(np.asarray(inputs["b_f"], f32)[0].reshape(8, 1)),
            "rel": np.ascontiguousarray(np.asarray(inputs["rel_table"], f32)[0]),
            "smask": sm,
            "xs": np.ascontiguousarray(np.asarray(inputs["x_sample"], f32)[c]),
            "cak": np.ascontiguousarray(np.asarray(inputs["cache_a_k"], f32)[0, c].reshape(512, 512)),
            "cav": np.ascontiguousarray(np.asarray(inputs["cache_a_v"], f32)[0, c].reshape(512, 512)),
            "cbk": np.ascontiguousarray(np.asarray(inputs["cache_b_k"], f32)[0, c].reshape(LS, 512)),
            "cbv": np.ascontiguousarray(np.asarray(inputs["cache_b_v"], f32)[0, c].reshape(LS, 512)),
            "cbl": np.ascontiguousarray(np.asarray(inputs["cache_b_logf"], f32)[0, c].reshape(LS, 8)),
        })
    return maps


def assemble(results, B, S, NSLOT=32):
    f32 = np.float32
    NOWN = NSLOT // 4
    y = np.zeros((B, S, D), f32)
    bk = np.zeros((1, B, S, 8, 64), f32)
    bv = np.zeros((1, B, S, 8, 64), f32)
    blf = np.zeros((1, B, S, 8), f32)
    akp = np.zeros((1, B, 512, 8, 64), f32)
    avp = np.zeros((1, B, 512, 8, 64), f32)
    ys = np.zeros((8, NSAMP, D), f32)
    aks = np.zeros((1, 8, 512, 8, 64), f32)
    avs = np.zeros((1, 8, 512, 8, 64), f32)
    bks = np.zeros((1, 8, NSAMP, 8, 64), f32)
    bvs = np.zeros((1, 8, NSAMP, 8, 64), f32)
    bls = np.zeros((1, 8, NSAMP, 8), f32)
    for c in range(8):
        r = results[c]
        b, j = c // 4, c % 4
        for m in range(NOWN):
            s0 = (4 * m + j) * 512
            y[b, s0:s0 + 512] = r["y_own"][m * 512:(m + 1) * 512]
            bk[0, b, s0:s0 + 512] = r["bk_own"][m * 512:(m + 1) * 512].reshape(512, 8, 64)
            bv[0, b, s0:s0 + 512] = r["bv_own"][m * 512:(m + 1) * 512].reshape(512, 8, 64)
            blf[0, b, s0:s0 + 512] = r["blf_own"][m * 512:(m + 1) * 512]
        if j == 3:
            akp[0, b] = r["akp"].reshape(512, 8, 64)
            avp[0, b] = r["avp"].reshape(512, 8, 64)
        ys[c] = r["ys"]
        aks[0, c] = r["aks"].reshape(512, 8, 64)
        avs[0, c] = r["avs"].reshape(512, 8, 64)
        bks[0, c] = r["bks"].reshape(NSAMP, 8, 64)
        bvs[0, c] = r["bvs"].reshape(NSAMP, 8, 64)
        bls[0, c] = r["bls"]
    return (y, ys, akp, avp, bk, bv, blf, aks, avs, bks, bvs, bls)


def kernel(**inputs):
    nc = _get_nc(32, True)
    maps = make_in_maps(inputs, 32)
    res = run_bass_kernel_spmd(nc, maps, core_ids=list(range(8)))
    B, S, _ = np.asarray(inputs["x_prompt"]).shape
    return assemble(res.results, B, S, 32)
```

```python
import numpy as np
from contextlib import ExitStack
import concourse.bass as bass
import concourse.mybir as mybir
from concourse.bass_utils import run_bass_kernel_spmd

F32 = mybir.dt.float32
BF16 = mybir.dt.bfloat16
AF = mybir.ActivationFunctionType
ALU = mybir.AluOpType

D = 1024
DC = 8
C1 = 3080
NT16 = 4
C1P = 3200
C3 = 3072
SCALE = 0.125
BIG = 30000.0
EPS = 1e-6
NSAMP = 16
LS = 4096
LSP = 4224
CH = 512
CH2 = 512


class _Stop(Exception):
    pass


def _aslist(x):
    if x is None:
        return []
    return list(x) if isinstance(x, (list, tuple)) else [x]


def _last2(a, b):
    return _aslist(a) + [b]


class Op:
    __slots__ = ("eng", "fn", "deps", "sig", "val", "dsem", "dval")

    def __init__(self, eng, fn, deps, dsem=None, dval=0):
        self.eng = eng
        self.fn = fn
        self.deps = [d for d in deps if d is not None]
        self.sig = False
        self.val = 0
        self.dsem = dsem
        self.dval = dval


class Prog:
    ENGS = ("pe", "act", "dve", "pool", "sp")

    def __init__(self, nc, sems):
        self.nc = nc
        self.sems = sems
        self.lists = {e: [] for e in self.ENGS}
        self.dma_cnt = {}
        self.dma_sems = {}
        self.base = {e: 0 for e in self.ENGS}

    def op(self, eng, fn, deps=()):
        o = Op(eng, fn, deps)
        self.lists[eng].append(o)
        return o

    def dma(self, eng, sem, fn, deps=()):
        k = id(sem)
        self.dma_cnt[k] = self.dma_cnt.get(k, 0) + 16
        self.dma_sems[k] = sem
        o = Op(eng, fn, deps, dsem=sem, dval=self.dma_cnt[k])
        self.lists[eng].append(o)
        return o

    def finalize(self):
        sems = self.sems
        for e in self.ENGS:
            for o in self.lists[e]:
                for d in o.deps:
                    if d.dsem is None and not (d.eng == "pe" and e == "pe"):
                        d.sig = True
        for e in self.ENGS:
            c = self.base[e]
            for o in self.lists[e]:
                if o.dsem is None and o.sig:
                    c += 1
                o.val = c
            self.base[e] = c
        nc = self.nc
        lists = self.lists
        final = [(self.dma_sems[k], v) for k, v in self.dma_cnt.items()]
        with nc.Block() as block:
            def run(e, engine):
                waited = {}
                for o in lists[e]:
                    for d in o.deps:
                        if d.dsem is not None:
                            s, v = d.dsem, d.dval
                        else:
                            if d.eng == "pe" and e == "pe":
                                continue
                            s, v = sems[d.eng], d.val
                        key = id(s)
                        if waited.get(key, 0) >= v:
                            continue
                        waited[key] = v
                        engine.wait_ge(s, v)
                    ins = o.fn(engine)
                    if o.dsem is not None:
                        ins.then_inc(o.dsem, 16)
                    elif o.sig:
                        ins.then_inc(sems[e], 1)
                if e == "sp":
                    for (s, v) in final:
                        engine.wait_ge(s, v)

            @block.tensor
            def _(eng):
                run("pe", eng)

            @block.scalar
            def _(eng):
                run("act", eng)

            @block.vector
            def _(eng):
                run("dve", eng)

            @block.gpsimd
            def _(eng):
                run("pool", eng)

            @block.sync
            def _(eng):
                run("sp", eng)
        self.lists = {e: [] for e in self.ENGS}


def build(NSLOT=32, sample=True, stop=None):
    try:
        return _build(NSLOT, sample, stop)
    except _Stop as e:
        return e.args[0]


def _build(NSLOT=32, sample=True, stop=None):
    NOWN = NSLOT // 4
    L = NSLOT * 512
    NQ = NOWN * 512
    NQA = NQ + NSAMP
    nc = bass.Bass("TRN2", target_bir_lowering=False)

    def din(name, shape, dt=F32):
        return nc.dram_tensor(name, shape, dt, kind="ExternalInput").ap()

    def dout(name, shape, dt=F32):
        return nc.dram_tensor(name, shape, dt, kind="ExternalOutput").ap()

    def dscr(name, shape, dt):
        return nc.dram_tensor(name, shape, dt, kind="Internal").ap()

    xw = din("xw", [L, D])
    w1 = din("w1", [D, C1])
    w3 = din("w3", [D, C3])
    wbr = din("wbr", [D, D])
    wout = din("wout", [D, D])
    gpre = din("gpre", [D])
    gpost = din("gpost", [D])
    bfv = din("bfv", [8, 1])
    rel = din("rel", [8, 192])
    smask_d = din("smask", [128, 3])
    xs_d = din("xs", [NSAMP, D])
    cak = din("cak", [512, 512])
    cav = din("cav", [512, 512])
    cbk = din("cbk", [LS, 512])
    cbv = din("cbv", [LS, 512])
    cbl = din("cbl", [LS, 8])

    y_own = dout("y_own", [NQ, D])
    bk_own = dout("bk_own", [NQ, 512])
    bv_own = dout("bv_own", [NQ, 512])
    blf_own = dout("blf_own", [NQ, 8])
    akp = dout("akp", [512, 512])
    avp = dout("avp", [512, 512])
    ys_o = dout("ys", [NSAMP, D])
    aks = dout("aks", [512, 512])
    avs = dout("avs", [512, 512])
    bks = dout("bks", [NSAMP, 512])
    bvs = dout("bvs", [NSAMP, 512])
    bls = dout("bls", [NSAMP, 8])

    Ks = dscr("Ks", [8, 68, L], BF16)
    Vs = dscr("Vs", [L, 512], BF16)
    Qs = dscr("Qs", [8, 68, NQA], BF16)
    YA = dscr("YA", [512, NQA], BF16)
    YB = dscr("YB", [512, NQA], BF16)
    ext = dscr("ext", [8, 768], F32)
    Erep = dscr("Erep", [8, 128, 768], F32)
    W3s = dscr("W3s", [D, C3], BF16)
    Wbrs = dscr("Wbrs", [D, D], BF16)
    Wouts = dscr("Wouts", [D, D], BF16)
    Kss = dscr("Kss", [8, 68, LSP], BF16)
    Vss = dscr("Vss", [LSP, 512], BF16)

    outer = ExitStack()
    with outer:
        def osb(name, shape, dt):
            return outer.enter_context(nc.sbuf_tensor(name, shape, dt))

        sems = {e: outer.enter_context(nc.semaphore("s_" + e)) for e in ("pe", "act", "dve", "pool")}
        nsem = [0]

        def newsem(stack):
            nsem[0] += 1
            return outer.enter_context(nc.semaphore("d%d" % nsem[0]))

        P = Prog(nc, sems)

        def chk(tag):
            if stop == tag:
                P.finalize()
                raise _Stop(nc)

        def mm(out, lhsT, rhs, start, stop, deps=(), skip=False):
            return P.op("pe", lambda e: e.matmul(out, lhsT=lhsT, rhs=rhs, start=start, stop=stop,
                                                 skip_group_check=skip), deps)

        def tr(out, in_, idn, deps=()):
            return P.op("pe", lambda e: e.transpose(out=out, in_=in_, identity=idn), deps)

        def act(out, in_, func, deps=(), **kw):
            return P.op("act", lambda e: e.activation(out=out, in_=in_, func=func, **kw), deps)

        def ts(eng, out, in0, s1, s2, op0, op1=None, deps=()):
            if op1 is None:
                return P.op(eng, lambda e: e.tensor_scalar(out=out, in0=in0, scalar1=s1, scalar2=None, op0=op0), deps)
            return P.op(eng, lambda e: e.tensor_scalar(out=out, in0=in0, scalar1=s1, scalar2=s2, op0=op0, op1=op1), deps)

        def tt(eng, out, in0, in1, op, deps=()):
            return P.op(eng, lambda e: e.tensor_tensor(out=out, in0=in0, in1=in1, op=op), deps)

        def stt(out, in0, scalar, in1, op0, op1, deps=()):
            return P.op("dve", lambda e: e.scalar_tensor_tensor(out=out, in0=in0, scalar=scalar, in1=in1,
                                                                op0=op0, op1=op1), deps)

        def cp(eng, out, in_, deps=()):
            if eng == "act":
                return act(out, in_, AF.Copy, deps)
            return P.op(eng, lambda e: e.tensor_copy(out=out, in_=in_), deps)

        def memset(eng, ap, val, deps=()):
            return P.op(eng, lambda e: e.memset(ap, val), deps)

        def dma(q, sem, out, in_, deps=(), slow=False):
            if slow:
                q = "pool"
            return P.dma(q, sem, lambda e: e.dma_start(out=out, in_=in_, allow_slow_non_contiguous=slow), deps)

        ident = osb("ident", [128, 128], BF16)
        tri = osb("tri", [128, 128], BF16)
        onesf = osb("onesf", [128, 128], F32)
        smask = osb("smask_sb", [128, 3], F32)
        nbf = osb("nbf", [8, 1], F32)
        gpre_sb = osb("gpre_sb", [128, DC], F32)

        ph = ExitStack()
        with ph:
            def sb(name, shape, dt):
                return ph.enter_context(nc.sbuf_tensor(name, shape, dt))

            def ps(name, shape, dt):
                return ph.enter_context(nc.psum_tensor(name, shape, dt))

            sem_misc = newsem(ph)
            sem_gp = newsem(ph)
            sem_e = newsem(ph)
            sem_e4 = newsem(ph)
            sem_e1 = newsem(ph)
            sem_tl = newsem(ph)
            sem_w = [newsem(ph), newsem(ph)]
            c_ones = memset("pool", onesf[:], 1.0)
            c_id = P.op("pool", lambda e: e.affine_select(out=ident[:], in_=onesf[:], pattern=[[1, 128]],
                                                         compare_op=ALU.is_equal, fill=0.0, base=0,
                                                         channel_multiplier=-1), [c_ones])
            c_tri = P.op("pool", lambda e: e.affine_select(out=tri[:], in_=onesf[:], pattern=[[1, 128]],
                                                          compare_op=ALU.is_ge, fill=0.0, base=0,
                                                          channel_multiplier=-1), [c_ones])
            l_sm = dma("sp", sem_misc, smask[:], smask_d)
            l_bf = dma("sp", sem_misc, nbf[:], bfv)
            l_gp = dma("sp", sem_gp, gpre_sb[:], gpre.rearrange("(c p) -> p c", p=128), slow=True)
            c_nbf = ts("dve", nbf[:], nbf[:], -1.0, None, ALU.mult, deps=[l_bf])

            chk("a0")
            tm32_early = [sb("tm32_%d" % i, [128, 512], F32) for i in range(3)]
            EB = sb("EB", [128, 8, 5, 128], F32)
            if True:
                T = EB
                ea, eb = tm32_early[0], tm32_early[1]
                e0 = dma("sp", sem_e, ea[0:8, 64:256], rel)
                z1 = memset("dve", ea[0:8, 0:64], 0.0)
                z2 = memset("dve", eb[0:8, :], 0.0)
                z3 = ts("dve", ea[0:8, 0:64], ea[0:8, 0:64], ea[0:8, 64:65], None, ALU.add, deps=[e0, z1])
                z4 = ts("dve", eb[0:8, :], eb[0:8, :], ea[0:8, 255:256], None, ALU.add, deps=[e0, z2])
                e1 = dma("sp", sem_e1, ext[:, 0:256], ea[0:8, 0:256], deps=[z3])
                e3 = dma("sp", sem_e1, ext[:, 256:768], eb[0:8, :], deps=[z4])
                e4 = dma("sp", sem_e4, Erep,
                         bass.AP(tensor=ext.tensor, offset=0, ap=[[768, 8], [0, 128], [1, 768]]), deps=[e3])
                chk("a1")
                tl = None
                for h in range(8):
                    src = bass.AP(tensor=Erep.tensor, offset=h * 128 * 768 + 127, ap=[[767, 128], [128, 5], [1, 128]])
                    tl = dma("sp", sem_tl, T[:, h, :, :], src, deps=[e4])
                c_t0 = memset("dve", T[64:128, :, 0, 0:64], -BIG, deps=[tl])
                c_t4 = memset("dve", T[0:64, :, 4, 64:128], -BIG, deps=[tl])
                Tf = T[:].rearrange("p a b c -> p (a b c)")
                c_tl = act(Tf, Tf, AF.Exp, deps=[c_t0, c_t4])
            T_ready = [c_tl]
            chk("a2")
            Wb = sb("Wb", [128, DC, C1P], BF16)
            c_wpad = memset("dve", Wb[:, :, C1:C1P], 0.0, deps=[c_tl])
            CQ = C1 // 4
            wst = [sb("wst%d" % i, [128, CQ], F32) for i in range(4)]
            sem_w4 = [newsem(ph) for _ in range(4)]
            wfree = [c_tl] * 4
            W_ready = []
            k = 0
            for dc in range(DC):
                for hf in range(4):
                    c0 = hf * CQ
                    s = k % 4
                    ld = dma("sp", sem_w4[s], wst[s][:], w1[dc * 128:(dc + 1) * 128, c0:c0 + CQ],
                             deps=[wfree[s]])
                    o = ts("dve", Wb[:, dc, c0:c0 + CQ], wst[s][:], gpre_sb[:, dc:dc + 1], None, ALU.mult,
                           deps=[ld, l_gp])
                    wfree[s] = o
                    W_ready.append(o)
                    k += 1
            W_ready = W_ready[-4:] + [c_wpad]

            chk("a")
            NX = 3
            xt = [sb("xt%d" % i, [128, D], F32) for i in range(NX)]
            sem_x = [newsem(ph) for _ in range(NX)]
            junk = sb("junk", [128, D], BF16)
            ssq = sb("ssq", [128, 4], F32)
            rsd = sb("rsd", [128, 4], F32)
            hb = [sb("hb%d" % i, [128, D], BF16) for i in range(2)]
            NF = 6
            fmst = [sb("fmst%d" % i, [128, 512], BF16) for i in range(NF)]
            sem_fm = [newsem(ph) for _ in range(NF)]
            NT32 = 3
            tm32 = tm32_early
            sem_t32 = [newsem(ph) for _ in range(NT32)]
            tm16 = [sb("tm16_%d" % i, [128, 512], BF16) for i in range(NT16)]
            sem_t16 = [newsem(ph) for _ in range(NT16)]
            pbt = [[sb("pbt%d_%d" % (a_, i), [128, 512], BF16) for i in range(2)] for a_ in range(2)]
            ptmp = [[sb("ptmp%d_%d" % (a_, i), [128, 512], BF16) for i in range(2)] for a_ in range(2)]
            osbuf = [sb("osbuf%d" % i, [65, 512], F32) for i in range(2)]
            rd = sb("rd", [128, 1024], F32)
            sel = sb("sel", [128, 128], F32)
            yab = [sb("yab%d" % i, [64, 512], BF16) for i in range(2)]
            sem_ya = [newsem(ph) for _ in range(2)]
            el8 = sb("el8", [8, 512], F32)
            G8 = [sb("G8_%d" % i, [8, 512], F32) for i in range(2)]
            on8 = sb("on8", [8, 512], F32)
            r8 = sb("r8", [8, 512], F32)
            lim = sb("lim", [8, 4, 512], BF16)
            qst = sb("qst", [8, 4, 512], BF16)
            lf8 = sb("lf8", [8, 512], F32)
            sem_lim = newsem(ph)
            sem_qst = newsem(ph)
            sem_lf = newsem(ph)

            tp = [ps("tp%d" % i, [128, DC, 128], BF16) for i in range(1)] * 2
            mp = [ps("mp%d" % i, [128, 512], F32) for i in range(3)]
            sT = [ps("sT%d" % i, [128, 512], F32) for i in range(2)]
            oT = [ps("oT%d" % i, [128, 512], F32) for i in range(2)]
            c_rd = memset("pool", rd[:], 0.0)
            c_sel0 = memset("pool", sel[:], 0.0)
            c_sel = memset("pool", sel[64:65, :], 1.0, deps=[c_sel0])

            c_on8 = memset("pool", on8[:], 1.0)
            c_lim = memset("pool", lim[:, 0, :], 1.0)
            c_qst = memset("pool", qst[:, 1:4, :], 1.0)

            st = dict(xi=0, xfree=[None] * NX, hbfree=[None, None], tpfree=[None, None], hTfree=[None, None],
                      mpi=0, mpfree=[[], [], []], fmi=0, fmlast=[None] * NF, t32i=0, t32last=[e3, e3, None],
                      t16i=0, t16last=[None] * NT16, rsfree=[None] * 4, sTi=0, sTfree=[None, None],
                      pbfree=[[None, None], [None, None]], ptfree=[[None, None], [None, None]], oTfree=[[], []], yai=0, yalast=[None, None],
                      limlast=None, qstlast=None, lflast=None, Gi=0, Gprev=None, KATfree=None, VAfree=None,
                      QATfree=None, osfree=[None, None], rdfree=[None, None], pend_norm=None, elfree=None, r8free=None)

            def norm_a(src_ap, n):
                i = st["xi"]
                st["xi"] += 1
                xs_ = xt[i % NX]
                ld = dma("sp", sem_x[i % NX], xs_[0:n, :], src_ap, deps=[st["xfree"][i % NX]])
                c = i % 4
                sq = act(junk[0:n, :], xs_[0:n, :], AF.Square, deps=[ld], accum_out=ssq[0:n, c:c + 1])
                sr = act(rsd[0:n, c:c + 1], ssq[0:n, c:c + 1], AF.Ln, deps=[sq, st["rsfree"][c]],
                         scale=1.0 / D, bias=EPS)
                rc = act(rsd[0:n, c:c + 1], rsd[0:n, c:c + 1], AF.Exp, deps=[sr], scale=-0.5)
                b = i % 2
                sc = ts("dve", hb[b][0:n, :], xs_[0:n, :], rsd[0:n, c:c + 1], None, ALU.mult,
                        deps=[rc, ld, st["hbfree"][b]])
                st["rsfree"][c] = sc
                st["xfree"][i % NX] = sc
                return (b, sc, n)

            def norm_b(hnd, hTdst, deps_hT):
                b, sc, n = hnd
                t_ = None
                for dc in range(DC):
                    t_ = tr(tp[b][:, dc, 0:n], hb[b][0:n, dc * 128:(dc + 1) * 128], ident[0:n, 0:n],
                            deps=[sc, st["tpfree"][0], c_id])
                st["hbfree"][b] = t_
                co = act(hTdst, tp[b][:, :, 0:n], AF.Copy, deps=[t_] + list(deps_hT))
                st["tpfree"][0] = co
                return co

            def norm_transpose(src_ap, n, hTdst, deps_hT):
                co = norm_b(norm_a(src_ap, n), hTdst, deps_hT)
                return co, None, None

            def mm_group(pairs, M, N, deps):
                bi = st["mpi"] % 3
                st["mpi"] += 1
                bank = mp[bi]
                o = None
                n_ = len(pairs)
                for kk, (l_, r_) in enumerate(pairs):
                    o = mm(bank[0:M, 0:N], l_, r_, kk == 0, kk == n_ - 1, deps=list(deps) + st["mpfree"][bi])
                return bi, bank, o

            def fm_chunk(Wt, c0, M, hTs, N, deps):
                pairs = [(Wt[:, dc, c0:c0 + M], hTs[:, dc, 0:N]) for dc in range(DC)]
                return mm_group(pairs, M, N, deps)

            def tm_group(Wt, c0, ncol, hTs, t0, n, deps):
                pairs = [(hTs[:, dc, t0:t0 + n], Wt[:, dc, c0:c0 + ncol]) for dc in range(DC)]
                return mm_group(pairs, n, ncol, deps)

            def store_fm_heads(bank, bi, mmop, dst, hpair, col0, N, eng="dve"):
                s = st["fmi"] % NF
                st["fmi"] += 1
                ev = cp(eng, fmst[s][:, 0:N], bank[:, 0:N], deps=[mmop, st["fmlast"][s]])
                st["mpfree"][bi] = [ev]
                d_ = None
                for hh in range(2):
                    d_ = dma("pool", sem_fm[s], dst[2 * hpair + hh, 0:64, col0:col0 + N],
                             fmst[s][hh * 64:(hh + 1) * 64, 0:N], deps=[ev])
                st["fmlast"][s] = d_
                return ev

            def fpath(bank, bi, mmop, N, col0, Kdst, qcol0, own, lfdst):
                gi = st["Gi"] % 2
                st["Gi"] += 1
                G = G8[gi]
                a1 = act(el8[:, 0:N], bank[0:8, 0:N], AF.Exp, deps=[mmop, c_nbf, st["elfree"]], scale=-1.0, bias=nbf[:])
                st["mpfree"][bi] = [a1]
                a2 = act(el8[:, 0:N], el8[:, 0:N], AF.Ln, deps=[a1], bias=1.0)
                init = 0.0 if st["Gprev"] is None else st["Gprev"][0]
                sdeps = [a2, c_on8] + ([] if st["Gprev"] is None else [st["Gprev"][1]])
                sc_ = P.op("dve", lambda e: e.tensor_tensor_scan(out=G[:, 0:N], data0=on8[:, 0:N], data1=el8[:, 0:N],
                                                                 initial=init, op0=ALU.mult, op1=ALU.add),
                           sdeps + [st["limlast"], st["qstlast"]])
                st["Gprev"] = (G[:, N - 1:N], sc_)
                d1 = ts("dve", lim[:, 1, 0:N], G[:, 0:N], 8.0, None, ALU.mult, deps=[sc_, st["limlast"], c_lim])
                d2 = stt(r8[:, 0:N], G[:, 0:N], 8.0, lim[:, 1, 0:N], ALU.mult, ALU.subtract, deps=[d1, st["r8free"]])
                d3 = cp("dve", lim[:, 2, 0:N], r8[:, 0:N], deps=[d2])
                d4 = tt("dve", lim[:, 3, 0:N], r8[:, 0:N], lim[:, 2, 0:N], ALU.subtract, deps=[d3])
                st["r8free"] = d4
                st["limlast"] = dma("pool", sem_lim, Kdst[:, 64:68, col0:col0 + N], lim[:, :, 0:N], deps=[d4])
                last = d4
                if own:
                    q1 = ts("dve", qst[:, 0, 0:N], G[:, 0:N], -8.0, None, ALU.mult, deps=[sc_, st["qstlast"], c_qst])
                    st["qstlast"] = dma("pool", sem_qst, Qs[:, 64:68, qcol0:qcol0 + N], qst[:, :, 0:N], deps=[q1])
                    q2 = ts("dve", lf8[:, 0:N], el8[:, 0:N], -1.0, None, ALU.mult, deps=[a2, st["lflast"]])
                    st["lflast"] = dma("sp", sem_lf, lfdst.rearrange("t h -> h t"), lf8[:, 0:N], deps=[q2], slow=True)
                    last = q2
                st["elfree"] = last
                return last

            def band2(heads, Qaps, tiles_of, nq, ycol0, m0ctx, filler=None):
                nt = len(tiles_of[0])

                def emit_qk(a_, ti):
                    Kap, Vap, nk, q0, n, Eb, cf = tiles_of[a_][ti]
                    return mm(sT[a_][0:nk, 0:n], Kap, Qaps[a_][:, q0:q0 + n], True, True,
                              deps=[st["sTfree"][a_]] + st["Kready"] + st["Qready"])

                def emit_exp(a_, ti, qk):
                    Kap, Vap, nk, q0, n, Eb, cf = tiles_of[a_][ti]
                    kw = {"scale": SCALE}
                    if cf and m0ctx:
                        kw["bias"] = smask[0:nk, 2:3]
                    ex = act(ptmp[a_][ti % 2][0:nk, 0:n], sT[a_][0:nk, 0:n], AF.Exp,
                             deps=[qk, st["ptfree"][a_][ti % 2], l_bf], **kw)
                    st["sTfree"][a_] = ex
                    po_, pi_ = pbt[a_][ti % 2][0:nk, 0:n], ptmp[a_][ti % 2][0:nk, 0:n]
                    if len(Eb.shape) == 3:
                        po_ = po_.rearrange("p (a q) -> p a q", q=128)
                        pi_ = pi_.rearrange("p (a q) -> p a q", q=128)
                    mu = tt("dve", po_, pi_, Eb, ALU.mult, deps=[ex, st["pbfree"][a_][ti % 2]] + T_ready)
                    st["ptfree"][a_][ti % 2] = mu
                    return mu

                def emit_pv(a_, ti, ex):
                    Kap, Vap, nk, q0, n, Eb, cf = tiles_of[a_][ti]
                    pv = mm(oT[a_][0:65, q0:q0 + n], Vap, pbt[a_][ti % 2][0:nk, 0:n], ti == 0, ti == nt - 1,
                            deps=[ex] + st["oTfree"][a_] + st["Vready"], skip=True)
                    st["pbfree"][a_][ti % 2] = pv
                    return pv

                exs = {}
                pvs = {}
                qkA = emit_qk(0, 0)
                exs[(0, 0)] = emit_exp(0, 0, qkA)
                qkB = emit_qk(1, 0)
                exs[(1, 0)] = emit_exp(1, 0, qkB)
                for ti in range(nt):
                    if ti == 1 and st["pend_norm"] is not None:
                        st["pend_norm"]()
                        st["pend_norm"] = None
                    pvs[0] = emit_pv(0, ti, exs.pop((0, ti)))
                    if ti + 1 < nt:
                        q_ = emit_qk(0, ti + 1)
                        exs[(0, ti + 1)] = emit_exp(0, ti + 1, q_)
                    pvs[1] = emit_pv(1, ti, exs.pop((1, ti)))
                    if ti + 1 < nt:
                        q_ = emit_qk(1, ti + 1)
                        exs[(1, ti + 1)] = emit_exp(1, ti + 1, q_)
                    if filler is not None and ti % 2 == 1:
                        filler()
                norm_jobs = []
                c1s = []
                for a_ in range(2):
                    c1 = act(osbuf[a_][0:65, 0:nq], oT[a_][0:65, 0:nq], AF.Copy, deps=[pvs[a_], st["osfree"][a_]])
                    st["oTfree"][a_] = [c1]
                    c1s.append(c1)
                for a_ in range(2):
                    h = heads[a_]
                    l1 = act(rd[64:65, a_ * 512:a_ * 512 + nq], osbuf[a_][64:65, 0:nq], AF.Ln,
                             deps=[c1s[a_], st["rdfree"][a_], c_rd])
                    r1 = act(rd[64:65, a_ * 512:a_ * 512 + nq], rd[64:65, a_ * 512:a_ * 512 + nq], AF.Exp, deps=[l1], scale=-1.0)
                    norm_jobs.append((a_, h, r1, c1s[a_]))

                def finish(norm_jobs=norm_jobs, nq=nq, ycol0=ycol0):
                    for (a_, h, r1, c1) in norm_jobs:
                        bi = st["mpi"] % 3
                        st["mpi"] += 1
                        bc = mm(mp[bi][:, 0:nq], sel[:, :], rd[:, a_ * 512:a_ * 512 + nq], True, True,
                                deps=[r1, c_sel] + st["mpfree"][bi])
                        st["rdfree"][a_] = bc
                        yi = st["yai"] % 2
                        st["yai"] += 1
                        y1 = tt("dve", yab[yi][:, 0:nq], osbuf[a_][0:64, 0:nq], mp[bi][0:64, 0:nq], ALU.mult,
                                deps=[c1, bc, st["yalast"][yi]])
                        st["mpfree"][bi] = [y1]
                        st["osfree"][a_] = y1
                        st["yalast"][yi] = dma("pool", sem_ya[yi], YA[h * 64:(h + 1) * 64, ycol0:ycol0 + nq], yab[yi][:, 0:nq],
                                               deps=[y1])
                st["pend_norm"] = finish
                return [pvs[0], pvs[1]]

            st["Kready"] = []
            st["Qready"] = []
            st["Vready"] = []

            if sample:
                sp_ = ExitStack()
                with sp_:
                    def ssb(name, shape, dt):
                        return sp_.enter_context(nc.sbuf_tensor(name, shape, dt))
                    hTs = ssb("hTs", [128, DC, NSAMP], BF16)
                    QATs = ssb("QATs", [128, 8, NSAMP], BF16)
                    c_qz = memset("pool", QATs[:], 0.0)
                    KATs = ssb("KATs", [128, 4, 640], BF16)
                    VAs = ssb("VAs", [128, 5, 8, 65], BF16)
                    cst = [ssb("cst%d" % i, [128, 512], F32) for i in range(2)]
                    sem_c = [newsem(sp_), newsem(sp_)]
                    c16 = [ssb("c16_%d" % i, [128, 512], BF16) for i in range(2)]
                    kst = ssb("kst", [128, 4, 512], BF16)
                    sem_kst = newsem(sp_)
                    lc = ssb("lc", [8, LS], F32)
                    Gc = ssb("Gc", [8, CH2], F32)
                    on8c = ssb("on8c", [8, CH2], F32)
                    r8c = ssb("r8c", [8, CH2], F32)
                    limc2 = [ssb("limc%d" % i, [8, 4, CH2], BF16) for i in range(2)]
                    sem_lc = newsem(sp_)
                    sem_limc2 = [newsem(sp_), newsem(sp_)]
                    sem_dd = newsem(sp_)
                    sem_vc = newsem(sp_)
                    lcl = None
                    for g in range(LS // CH):
                        lcl = dma("sp", sem_lc, lc[:, g * CH:(g + 1) * CH], cbl[g * CH:(g + 1) * CH, :].rearrange("t h -> h t"),
                                  slow=True)
                    c_vas = memset("pool", VAs[:, :, :, 64:65], 1.0)
                    c_on8c = memset("pool", on8c[:], 1.0)
                    c_limc = memset("pool", limc2[0][:, 0, :], 1.0)
                    c_limc = memset("pool", limc2[1][:, 0, :], 1.0, deps=[c_limc])

                    dma("sp", sem_dd, aks[0:496, :], cak[16:512, :])
                    dma("sp", sem_dd, avs[0:496, :], cav[16:512, :])

                    co, _, _ = norm_transpose(xs_d, NSAMP, hTs[:, :, :], [])
                    wdeps = [co] + W_ready
                    chk("b0")
                    for i in range(4):
                        bi, bank, o = fm_chunk(Wb, 0 + i * 128, 128, hTs, NSAMP, wdeps)
                        ev0 = cp("dve", QATs[0:64, 2 * i, :], bank[0:64, 0:NSAMP], deps=[o, c_qz])
                        ev = cp("dve", QATs[64:128, 2 * i + 1, :], bank[64:128, 0:NSAMP], deps=[o, c_qz])
                        st["mpfree"][bi] = [ev]
                    qats_ready = ev
                    for i in range(4):
                        bi, bank, o = fm_chunk(Wb, 512 + i * 128, 128, hTs, NSAMP, wdeps)
                        ev = cp("dve", KATs[:, i, 512:512 + NSAMP], bank[:, 0:NSAMP], deps=[o])
                        st["mpfree"][bi] = [ev]
                    kats_new = ev
                    chk("b1")
                    for i in range(4):
                        bi, bank, o = fm_chunk(Wb, 1536 + i * 128, 128, hTs, NSAMP, wdeps)
                        store_fm_heads(bank, bi, o, Qs, i, NQ, NSAMP)
                    for i in range(4):
                        bi, bank, o = fm_chunk(Wb, 2048 + i * 128, 128, hTs, NSAMP, wdeps)
                        store_fm_heads(bank, bi, o, Kss, i, LS, NSAMP)
                    chk("b2")
                    vas_new = None
                    for (c0, dst32, dst16) in ((512, aks[496:512, :], None), (1024, avs[496:512, :], "va"),
                                               (2048, bks, None), (2560, bvs, "vb")):
                        bi, bank, o = tm_group(Wb, c0, 512, hTs, 0, NSAMP, wdeps)
                        s = st["t32i"] % NT32
                        st["t32i"] += 1
                        ev = cp("dve", tm32[s][0:NSAMP, :], bank[0:NSAMP, :], deps=[o] + _aslist(st["t32last"][s]))
                        st["t32last"][s] = dma("pool", sem_t32[s], dst32, tm32[s][0:NSAMP, :], deps=[ev])
                        if dst16 == "va":
                            vas_new = cp("dve", VAs[0:NSAMP, 4, :, 0:64], tm32[s][0:NSAMP, :].rearrange("p (h d) -> p h d", h=8),
                                         deps=[ev, c_vas])
                            st["t32last"][s] = _last2(st["t32last"][s], vas_new)
                            st["mpfree"][bi] = [ev]
                        elif dst16 == "vb":
                            s2 = st["t16i"] % NT16
                            st["t16i"] += 1
                            ev2 = cp("pool", tm16[s2][0:NSAMP, :], tm32[s][0:NSAMP, :], deps=[ev, st["t16last"][s2]])
                            st["t16last"][s2] = dma("pool", sem_t16[s2], Vss[LS:LS + NSAMP, :], tm16[s2][0:NSAMP, :], deps=[ev2])
                            st["t32last"][s] = _last2(st["t32last"][s], ev2)
                            st["mpfree"][bi] = [ev]
                        else:
                            st["mpfree"][bi] = [ev]
                        chk("b3_%d" % c0)
                    chk("b")
                    cfree = [None, None]
                    c16free = [None, None]
                    kk = 0

                    def load_cast(src):
                        nonlocal kk
                        s = kk % 2
                        kk += 1
                        ld = dma("sp", sem_c[s], cst[s][:], src, deps=[cfree[s]])
                        return s, ld

                    kats_c = None
                    vas_c = None
                    for kt in range(4):
                        s, ld = load_cast(cak[kt * 128:(kt + 1) * 128, :])
                        cc = cp("dve", c16[s][:], cst[s][:], deps=[ld, c16free[s]])
                        cfree[s] = cc
                        b = kt % 2
                        t_ = None
                        for i in range(4):
                            t_ = tr(tp[b][:, i, :], c16[s][:, i * 128:(i + 1) * 128], ident[:], deps=[cc, st["tpfree"][0], c_id])
                        c16free[s] = t_
                        kats_c = act(KATs[:, :, kt * 128:(kt + 1) * 128], tp[b][:, 0:4, :], AF.Copy, deps=[t_])
                        st["tpfree"][0] = kats_c
                        s, ld = load_cast(cav[kt * 128:(kt + 1) * 128, :])
                        vas_c = cp("dve", VAs[:, kt, :, 0:64], cst[s][:].rearrange("p (h d) -> p h d", h=8), deps=[ld, c_vas])
                        cfree[s] = vas_c
                    kstlast = None
                    for g in range(LS // 512):
                        evs = []
                        for q in range(4):
                            kt = g * 4 + q
                            s, ld = load_cast(cbk[kt * 128:(kt + 1) * 128, :])
                            cc = cp("dve", c16[s][:], cst[s][:], deps=[ld, c16free[s]])
                            cfree[s] = cc
                            b = kt % 2
                            t_ = None
                            for i in range(4):
                                t_ = tr(tp[b][:, i, :], c16[s][:, i * 128:(i + 1) * 128], ident[:], deps=[cc, st["tpfree"][0], c_id])
                            c16free[s] = t_
                            ev = act(kst[:, :, q * 128:(q + 1) * 128], tp[b][:, 0:4, :], AF.Copy, deps=[t_, kstlast])
                            st["tpfree"][0] = ev
                            evs.append(ev)
                        d_ = None
                        for hh in range(8):
                            d_ = dma("pool", sem_kst, Kss[hh, 0:64, g * 512:(g + 1) * 512],
                                     kst[(hh % 2) * 64:(hh % 2) * 64 + 64, hh // 2, :], deps=evs)
                        kstlast = d_
                    for g in range(8):
                        r0, r1_ = g * (LS // 8), (g + 1) * (LS // 8)
                        dma("pool", sem_vc, Vss[r0:r1_, :], cbv[r0:r1_, :])
                    chk("c")
                    limcl = [None, None]
                    limclast = None
                    gprev = None
                    n1 = ts("dve", lc[:], lc[:], -1.0, None, ALU.mult, deps=[lcl])
                    for g in range(LS // CH2):
                        limc = limc2[g % 2]
                        limclast = limcl[g % 2]
                        lcg = lc[:, g * CH2:(g + 1) * CH2]
                        n1d = [n1]
                        init = 0.0
                        if gprev is not None:
                            cz = cp("dve", r8c[:, 0:1], Gc[:, CH2 - 1:CH2], deps=[gprev])
                            init = r8c[:, 0:1]
                            n1d = [n1, cz]
                        sc_ = P.op("dve", lambda e, init=init, lcg=lcg: e.tensor_tensor_scan(out=Gc[:], data0=on8c[:], data1=lcg,
                                                                                            initial=init, op0=ALU.mult, op1=ALU.add),
                                   n1d + [c_on8c, limclast])
                        d1 = ts("dve", limc[:, 1, :], Gc[:], 8.0, None, ALU.mult, deps=[sc_, limclast, c_limc])
                        d2 = stt(r8c[:], Gc[:], 8.0, limc[:, 1, :], ALU.mult, ALU.subtract, deps=[d1])
                        d3 = cp("dve", limc[:, 2, :], r8c[:], deps=[d2])
                        d4 = tt("dve", limc[:, 3, :], r8c[:], limc[:, 2, :], ALU.subtract, deps=[d3])
                        limcl[g % 2] = dma("pool", sem_limc2[g % 2], Kss[:, 64:68, g * CH2:(g + 1) * CH2], limc[:], deps=[d4])
                        gprev = d4
                    chk("d")
                    bi, bank, o = fm_chunk(Wb, 3072, 128, hTs, NSAMP, wdeps)
                    cz = cp("dve", G8[1][:, 511:512], Gc[:, CH2 - 1:CH2], deps=[gprev])
                    st["Gprev"] = (G8[1][:, 511:512], cz)
                    st["Gi"] = 0
                    fl = fpath(bank, bi, o, NSAMP, LS, Kss, NQ, True, bls)
                    st["Gprev"] = None
                    st["Gi"] = 0
                    chk("e")
                    st["Kready"] = [kats_c, kats_new]
                    st["Qready"] = [qats_ready]
                    st["Vready"] = [vas_c, vas_new]
                    for hp in range(4):
                        heads = (2 * hp, 2 * hp + 1)
                        tiles_of = []
                        for h in heads:
                            tiles = []
                            for kt in range(5):
                                nk = 128 if kt < 4 else NSAMP
                                tiles.append((KATs[:, hp, kt * 128:kt * 128 + nk], VAs[0:nk, kt, h, :], nk, 0, NSAMP,
                                              EB[0:nk, h, 4 - kt, 0:NSAMP], False))
                            tiles_of.append(tiles)
                        band2(heads, [QATs[:, heads[0], :], QATs[:, heads[1], :]], tiles_of, NSAMP, NQ, False)
                    st["pend_norm"]()
                    st["pend_norm"] = None
                    P.finalize()
                    if stop == "p1s":
                        raise _Stop(nc)
                st["Kready"] = []
                st["Qready"] = []
                st["Vready"] = []
                st["mpfree"] = [[], [], []]
                for k_ in ("xfree", "hbfree", "tpfree", "fmlast", "t32last", "t16last", "rsfree", "sTfree",
                           "yalast", "osfree"):
                    st[k_] = [None] * len(st[k_])
                st["oTfree"] = [[], []]
                st["pbfree"] = [[None, None], [None, None]]
                st["ptfree"] = [[None, None], [None, None]]
                st["rdfree"] = [None, None]
                for k_ in ("limlast", "qstlast", "lflast", "elfree", "r8free"):
                    st[k_] = None

            hT = [sb("hT%d" % i, [128, DC, 512], BF16) for i in range(2)]
            QAT = sb("QATz", [128, 8, 512], BF16)
            c_qzp = memset("pool", QAT[:], 0.0)
            KAT = sb("KAT", [128, 4, 1024], BF16)
            VA = sb("VA", [128, 8, 8, 65], BF16)
            c_va = memset("pool", VA[:, :, :, 64:65], 1.0)
            def xsrc(p, t):
                return xw[p * 512 + t * 128:p * 512 + (t + 1) * 128, :]

            hnds = {}
            cps_of = {}

            def emit_a(p, t):
                if p < NSLOT:
                    hnds[(p, t)] = norm_a(xsrc(p, t), 128)

            def emit_b(p, t):
                if p < NSLOT:
                    co = norm_b(hnds.pop((p, t)), hT[p % 2][:, :, t * 128:(t + 1) * 128], [st["hTfree"][p % 2]])
                    cps_of.setdefault(p, []).append(co)

            for t in range(4):
                emit_a(0, t) if t < 2 else None
            emit_b(0, 0)
            emit_b(0, 1)
            emit_a(0, 2)
            emit_a(0, 3)
            emit_b(0, 2)
            emit_b(0, 3)

            WD = {}

            def make_slot(p):
                own = (p % 4 == 3)
                ctx = (p % 4 == 2)
                m = p // 4
                hTp = hT[p % 2]
                lm = {"o": None}
                groups = []

                def g_kb(i):
                    bi, bank, o = fm_chunk(Wb, 2048 + i * 128, 128, hTp, 512, WD[p])
                    store_fm_heads(bank, bi, o, Ks, i, p * 512, 512)
                    lm["o"] = o

                def g_f():
                    bi, bank, o = fm_chunk(Wb, 3072, 128, hTp, 512, WD[p])
                    fpath(bank, bi, o, 512, p * 512, Ks, m * 512, own, blf_own[m * 512:(m + 1) * 512, :] if own else None)
                    lm["o"] = o

                def g_ka(i):
                    kcol = 512 if own else 0
                    bi, bank, o = fm_chunk(Wb, 512 + i * 128, 128, hTp, 512, WD[p])
                    ev = cp("dve", KAT[:, i, kcol:kcol + 512], bank[:, :], deps=[o] + _aslist(st["KATfree"]))
                    st["mpfree"][bi] = [ev]
                    st["Kready"] = [ev]
                    lm["o"] = o

                def g_qa(i):
                    bi, bank, o = fm_chunk(Wb, 0 + i * 128, 128, hTp, 512, WD[p])
                    ev0 = cp("dve", QAT[0:64, 2 * i, :], bank[0:64, :], deps=[o, c_qzp] + _aslist(st["QATfree"]))
                    ev = cp("dve", QAT[64:128, 2 * i + 1, :], bank[64:128, :], deps=[o, c_qzp] + _aslist(st["QATfree"]))
                    st["mpfree"][bi] = [ev]
                    st["Qready"] = [ev]
                    lm["o"] = o

                def g_qb(i):
                    bi, bank, o = fm_chunk(Wb, 1536 + i * 128, 128, hTp, 512, WD[p])
                    store_fm_heads(bank, bi, o, Qs, i, m * 512, 512)
                    lm["o"] = o

                def g_vb(t):
                    tok0 = p * 512 + t * 128
                    otok0 = m * 512 + t * 128
                    bi, bank, o = tm_group(Wb, 2560, 512, hTp, t * 128, 128, WD[p])
                    lm["o"] = o
                    s2 = st["t16i"] % NT16
                    st["t16i"] += 1
                    if own:
                        s = st["t32i"] % NT32
                        st["t32i"] += 1
                        ev = cp("dve", tm32[s][:], bank[:, :], deps=[o] + _aslist(st["t32last"][s]))
                        d32 = dma("pool", sem_t32[s], bv_own[otok0:otok0 + 128, :], tm32[s][:], deps=[ev])
                        ev2 = cp("pool", tm16[s2][:], tm32[s][:], deps=[ev, st["t16last"][s2]])
                        st["t32last"][s] = [d32, ev2]
                        st["mpfree"][bi] = [ev]
                    else:
                        ev2 = cp("act", tm16[s2][:], bank[:, :], deps=[o, st["t16last"][s2]])
                        st["mpfree"][bi] = [ev2]
                    st["t16last"][s2] = dma("pool", sem_t16[s2], Vs[tok0:tok0 + 128, :], tm16[s2][:], deps=[ev2])

                def g_kbt(t):
                    otok0 = m * 512 + t * 128
                    bi, bank, o = tm_group(Wb, 2048, 512, hTp, t * 128, 128, WD[p])
                    lm["o"] = o
                    s = st["t32i"] % NT32
                    st["t32i"] += 1
                    ev = cp("dve", tm32[s][:], bank[:, :], deps=[o] + _aslist(st["t32last"][s]))
                    st["t32last"][s] = dma("pool", sem_t32[s], bk_own[otok0:otok0 + 128, :], tm32[s][:], deps=[ev])
                    st["mpfree"][bi] = [ev]

                def g_va(t):
                    kt = (4 if own else 0) + t
                    bi, bank, o = tm_group(Wb, 1024, 512, hTp, t * 128, 128, WD[p])
                    lm["o"] = o
                    ev2 = cp("dve", VA[:, kt, :, 0:64], bank[:, :].rearrange("p (h d) -> p h d", h=8),
                             deps=[o, c_va] + _aslist(st["VAfree"]))
                    st["Vready"] = [ev2]
                    st["mpfree"][bi] = [ev2]
                    if own and m == NOWN - 1:
                        s = st["t32i"] % NT32
                        st["t32i"] += 1
                        ev = cp("dve", tm32[s][:], bank[:, :], deps=[o] + _aslist(st["t32last"][s]))
                        st["t32last"][s] = dma("pool", sem_t32[s], avp[t * 128:(t + 1) * 128, :], tm32[s][:], deps=[ev])
                        st["mpfree"][bi] = [ev, ev2]

                def g_kat(t):
                    bi, bank, o = tm_group(Wb, 512, 512, hTp, t * 128, 128, WD[p])
                    lm["o"] = o
                    s = st["t32i"] % NT32
                    st["t32i"] += 1
                    ev = cp("dve", tm32[s][:], bank[:, :], deps=[o] + _aslist(st["t32last"][s]))
                    st["t32last"][s] = dma("pool", sem_t32[s], akp[t * 128:(t + 1) * 128, :], tm32[s][:], deps=[ev])
                    st["mpfree"][bi] = [ev]

                def g_band(hp, filler=None):
                    heads = (2 * hp, 2 * hp + 1)
                    tiles_of = []
                    for h in heads:
                        tiles = []
                        for kt in range(8):
                            q0 = max(0, kt - 4)
                            q1 = min(3, kt)
                            d0 = 4 + q0 - kt
                            n = (q1 - q0 + 1) * 128
                            tiles.append((KAT[:, hp, kt * 128:(kt + 1) * 128], VA[:, kt, h, :], 128, q0 * 128, n,
                                          EB[:, h, d0:d0 + (q1 - q0 + 1), :], kt < 4))
                        tiles_of.append(tiles)
                    pvl = band2(heads, [QAT[:, heads[0], :], QAT[:, heads[1], :]], tiles_of, 512, m * 512, m == 0, filler)
                    if hp == 3:
                        st["KATfree"] = pvl
                        st["VAfree"] = pvl
                        st["QATfree"] = pvl

                def mk(f, a_):
                    return lambda: f(a_)

                for i in range(4):
                    groups.append(mk(g_kb, i))
                groups.append(g_f)
                if own or ctx:
                    for i in range(4):
                        groups.append(mk(g_ka, i))
                if own:
                    for i in range(4):
                        groups.append(mk(g_qa, i))
                    for i in range(4):
                        groups.append(mk(g_qb, i))
                for t in range(4):
                    groups.append(mk(g_vb, t))
                    if own:
                        groups.append(mk(g_kbt, t))
                    if own or ctx:
                        groups.append(mk(g_va, t))
                    if own and m == NOWN - 1:
                        groups.append(mk(g_kat, t))
                nproj = len(groups)
                ng = nproj
                a_at = {0: [0, 1], max(1, ng // 4): [2], max(2, ng // 2): [3]}
                b_at = {max(1, ng // 8): 0, max(2, (3 * ng) // 8): 1, max(3, (5 * ng) // 8): 2, max(4, (7 * ng) // 8): 3}
                done_a, done_b = set(), set()
                units = []

                def unit(gi, g):
                    def run():
                        if gi == 0:
                            WD[p] = cps_of.pop(p) + W_ready
                        for t in a_at.get(gi, []):
                            emit_a(p + 1, t)
                            done_a.add(t)
                        if gi in b_at:
                            t = b_at[gi]
                            if t in done_a:
                                emit_b(p + 1, t)
                                done_b.add(t)
                        g()
                        if gi == 2 and st["pend_norm"] is not None and not own:
                            st["pend_norm"]()
                            st["pend_norm"] = None
                        if gi == nproj - 1:
                            st["hTfree"][p % 2] = lm["o"]
                            for t in range(4):
                                if t not in done_a:
                                    emit_a(p + 1, t)
                                if t not in done_b:
                                    emit_b(p + 1, t)
                    return run

                for gi, g in enumerate(groups):
                    units.append(("plain", unit(gi, g), p))
                if own:
                    for hp in range(4):
                        units.append(("band", (lambda f, hp=hp: g_band(hp, f)), p))
                return units

            seq = []
            for p in range(NSLOT):
                seq.extend(make_slot(p))
            pos = [0]
            while pos[0] < len(seq):
                kind, fn, slot = seq[pos[0]]
                pos[0] += 1
                if kind == "band":
                    def filler(slot=slot):
                        i = pos[0]
                        while i < len(seq) and seq[i][0] == "band":
                            i += 1
                        if i < len(seq) and seq[i][2] % 4 in (0, 1) and seq[i][2] <= slot + 2:
                            u = seq.pop(i)
                            u[1]()
                    fn(filler)
                else:
                    fn()
                    if st["pend_norm"] is not None:
                        st["pend_norm"]()
                        st["pend_norm"] = None
            if st["pend_norm"] is not None:
                st["pend_norm"]()
                st["pend_norm"] = None
            P.finalize()
            if stop == "p1":
                raise _Stop(nc)

        ph = ExitStack()
        with ph:
            def sb(name, shape, dt):
                return ph.enter_context(nc.sbuf_tensor(name, shape, dt))

            def ps(name, shape, dt):
                return ph.enter_context(nc.psum_tensor(name, shape, dt))

            NKT = L // 128
            Kb = [sb("Kb%d" % i, [68, L], BF16) for i in range(2)]
            VP = 66
            Vb = [sb("Vb%d" % i, [128, NKT + 2, VP], BF16) for i in range(2)]
            Qb = [sb("Qb%d" % i, [68, NQA], BF16) for i in range(2)]
            sem_kc = [[newsem(ph) for _ in range(4)] for _ in range(2)]
            sem_ks = [newsem(ph), newsem(ph)]
            if sample:
                Ksb = [sb("Ksb%d" % i, [68, LSP], BF16) for i in range(2)]
                Vsb = [sb("Vsb%d" % i, [128, LSP // 128 + 2, VP], BF16) for i in range(2)]
            NPB = 4
            pb2 = [sb("pb2_%d" % i, [128, 512], BF16) for i in range(NPB)]
            osb2 = sb("osb2", [65, 512], F32)
            rd2 = sb("rd2", [65, 512], F32)
            yb2 = [sb("yb2_%d" % i, [64, 512], BF16) for i in range(2)]
            sem_yb = [newsem(ph), newsem(ph)]
            sT2 = [ps("sT2_%d" % i, [128, 512], F32) for i in range(NPB)]
            oT2 = [ps("oT2_%d" % i, [128, 512], F32) for i in range(2)]
            bc2 = ps("bc2", [128, 512], F32)

            wq32 = [sb("wq32_%d" % i, [128, C3], F32) for i in range(3)]
            wq16 = [sb("wq16_%d" % i, [128, C3], BF16) for i in range(3)]
            sem_wq = [newsem(ph) for _ in range(3)]
            sem_wqs = [newsem(ph) for _ in range(3)]
            wq = dict(i=0, free32=[None] * 3, last16=[None] * 3, pend=[])
            wtasks = [(w3, W3s, C3, dc, True) for dc in range(DC)] + [(wbr, Wbrs, D, dc, False) for dc in range(DC)] + \
                     [(wout, Wouts, D, dc, False) for dc in range(DC)]

            def wtask_load():
                if not wtasks:
                    return
                src, dst, W_, dc, scaled = wtasks.pop(0)
                k_ = wq["i"] % 3
                wq["i"] += 1
                ld = dma("sp", sem_wq[k_], wq32[k_][:, 0:W_], src[dc * 128:(dc + 1) * 128, :], deps=[wq["free32"][k_]])
                wq["pend"].append((k_, ld, dst, W_, dc, scaled))

            def wtask_convert():
                if not wq["pend"]:
                    return
                k_, ld, dst, W_, dc, scaled = wq["pend"].pop(0)
                if scaled:
                    cv = ts("dve", wq16[k_][:, 0:W_], wq32[k_][:, 0:W_], gpre_sb[:, dc:dc + 1], None, ALU.mult,
                            deps=[ld, wq["last16"][k_]])
                else:
                    cv = cp("dve", wq16[k_][:, 0:W_], wq32[k_][:, 0:W_], deps=[ld, wq["last16"][k_]])
                wq["free32"][k_] = cv
                wq["last16"][k_] = dma("pool", sem_wqs[k_], dst[dc * 128:(dc + 1) * 128, :], wq16[k_][:, 0:W_], deps=[cv])

            for _ in range(3):
                wtask_load()

            def vwide(Vt, kt, nk):
                flat = Vt[:].rearrange("p k d -> p (k d)")
                return flat[0:nk, kt * VP:kt * VP + 128]

            c_v0 = [memset("pool", Vb[i][:], 0.0) for i in range(2)]
            c_v = [memset("pool", Vb[i][:, :, 64:65], 1.0, deps=[c_v0[i]]) for i in range(2)]
            if sample:
                c_vs0 = [memset("pool", Vsb[i][:], 0.0) for i in range(2)]
                c_vs = [memset("pool", Vsb[i][:, :, 64:65], 1.0, deps=[c_vs0[i]]) for i in range(2)]
            bufree = [None, None]
            s2 = dict(sTfree=[None] * NPB, pbfree=[None] * NPB, oTfree=[None, None], oTfree_b=[None, None], pend=None,
                      ji=0, oi=0, rdfree=None, osfree=None, bcfree=None, yi=0, ylast=[None, None])

            for h in range(8):
                par = h % 2
                fdep = [bufree[par], c_v[par]] + ([c_vs[par]] if sample else [])
                NCH = 4
                TPC = NKT // NCH
                chunk_ld = []
                for c_ in range(NCH):
                    k0, k1 = c_ * TPC, (c_ + 1) * TPC
                    l_ = None
                    if c_ == 0:
                        l_ = dma("sp", sem_kc[par][c_], Qb[par][:], Qs[h], deps=fdep)
                    l_ = dma("sp", sem_kc[par][c_], Kb[par][:, k0 * 128:k1 * 128], Ks[h, :, k0 * 128:k1 * 128], deps=fdep)
                    for g in range(k0, k1, 16):
                        g1 = min(k1, g + 16)
                        l_ = dma("sp", sem_kc[par][c_], Vb[par][:, g:g1, 0:64],
                                 Vs[g * 128:g1 * 128, h * 64:(h + 1) * 64].rearrange("(k p) d -> p k d", p=128), deps=fdep)
                    chunk_ld.append(l_)
                samp_ld = None
                if sample:
                    samp_ld = dma("sp", sem_ks[par], Ksb[par][:, 0:LS + NSAMP], Kss[h, :, 0:LS + NSAMP], deps=fdep)
                    for g in range(0, LS // 128, 16):
                        g1 = min(LS // 128, g + 16)
                        samp_ld = dma("sp", sem_ks[par], Vsb[par][:, g:g1, 0:64],
                                      Vss[g * 128:g1 * 128, h * 64:(h + 1) * 64].rearrange("(k p) d -> p k d", p=128),
                                      deps=fdep)
                    samp_ld = dma("sp", sem_ks[par], Vsb[par][0:NSAMP, LS // 128, 0:64],
                                  Vss[LS:LS + NSAMP, h * 64:(h + 1) * 64], deps=fdep)
                if h > 0:
                    for _ in range(3):
                        wtask_convert()
                    for _ in range(3):
                        wtask_load()
                def kdeps(kt):
                    c_ = kt // TPC
                    d_ = [chunk_ld[0], c_v[par]] + ([chunk_ld[c_]] if c_ > 0 else [])
                    if (kt + 1) // TPC != c_ and c_ + 1 < NCH:
                        d_.append(chunk_ld[c_ + 1])
                    return d_
                sdeps_ = [chunk_ld[0], samp_ld, c_vs[par]] if sample else []

                groups = []
                for m in range(NOWN):
                    jobs = []
                    nfull = (4 * m + 3) * 4
                    for kt in range(nfull):
                        bias = smask[:, kt // 4:kt // 4 + 1] if kt < 12 else None
                        jobs.append((Kb[par][:, kt * 128:(kt + 1) * 128], Qb[par][:, m * 512:(m + 1) * 512],
                                     vwide(Vb[par], kt, 128), 128, 0, 512, bias, False, kdeps(kt)))
                    for b in range(4):
                        kt = nfull + b
                        jobs.append((Kb[par][:, kt * 128:(kt + 1) * 128], Qb[par][:, m * 512 + b * 128:(m + 1) * 512],
                                     vwide(Vb[par], kt, 128), 128, b * 128, 512 - b * 128, None, True, kdeps(kt)))
                    groups.append((jobs, 512, m * 512))
                if sample:
                    jobs = []
                    for kt in range(LS // 128):
                        jobs.append((Ksb[par][:, kt * 128:(kt + 1) * 128], Qb[par][:, NQ:NQ + NSAMP],
                                     vwide(Vsb[par], kt, 128), 128, 0, NSAMP, None, False, sdeps_))
                    kt = LS // 128
                    jobs.append((Ksb[par][:, LS:LS + NSAMP], Qb[par][:, NQ:NQ + NSAMP],
                                 vwide(Vsb[par], kt, NSAMP), NSAMP, 0, NSAMP, None, True, sdeps_))
                    groups.append((jobs, NSAMP, NQ))

                flat = []
                for gi, (jobs, nq, ycol) in enumerate(groups):
                    for ji, jb in enumerate(jobs):
                        flat.append((gi, ji, len(jobs), jb))
                nflat = len(flat)
                LOOK = 3
                qk_pend = {}

                def emit_qk(fi):
                    gi, ji, nj, (Kap, Qap, Vap, nk, c0, n, bias, trif, jd) = flat[fi]
                    b = s2["ji"] % NPB
                    s2["ji"] += 1
                    o = mm(sT2[b][0:nk, 0:n], Kap, Qap, True, True, deps=jd + [s2["sTfree"][b]])
                    qk_pend[fi] = (b, o)

                for fi in range(min(LOOK, nflat)):
                    emit_qk(fi)
                cur_o = None
                last_pv = None
                for fi in range(nflat):
                    gi, ji, nj, (Kap, Qap, Vap, nk, c0, n, bias, trif, jd) = flat[fi]
                    if fi + LOOK < nflat:
                        emit_qk(fi + LOOK)
                    b, qk = qk_pend.pop(fi)
                    if ji == 0:
                        cur_o = s2["oi"] % 2
                        s2["oi"] += 1
                    kw = {"scale": SCALE}
                    if bias is not None:
                        kw["bias"] = bias
                    ex = act(pb2[b][0:nk, 0:n], sT2[b][0:nk, 0:n], AF.Exp, deps=[qk, s2["pbfree"][b], l_bf], **kw)
                    s2["sTfree"][b] = ex
                    pdep = ex
                    if trif:
                        w_ = min(128, n)
                        pdep = tt("pool", pb2[b][0:nk, 0:w_], pb2[b][0:nk, 0:w_], tri[0:nk, 0:w_], ALU.mult, deps=[ex, c_tri])
                    pv = mm(oT2[cur_o][0:128, c0:c0 + n], Vap, pb2[b][0:nk, 0:n], ji == 0, ji == nj - 1,
                            deps=[pdep, s2["oTfree"][cur_o], s2["oTfree_b"][cur_o]] + jd, skip=True)
                    s2["pbfree"][b] = pv
                    last_pv = pv
                    if ji == nj - 1:
                        jobs, nq, ycol = groups[gi]
                        o_ = oT2[cur_o]
                        c1 = act(osb2[0:65, 0:nq], o_[0:65, 0:nq], AF.Copy, deps=[pv, s2["osfree"]])
                        s2["oTfree"][cur_o] = c1
                        r1 = P.op("dve", lambda e, nq=nq: e.reciprocal(out=rd2[64:65, 0:nq], in_=osb2[64:65, 0:nq]),
                                  [c1, s2["rdfree"]])

                        def finish(r1=r1, c1=c1, nq=nq, ycol=ycol, h=h):
                            bc = mm(bc2[0:64, 0:nq], onesf[64:65, 0:64], rd2[64:65, 0:nq], True, True,
                                    deps=[r1, s2["bcfree"]])
                            s2["rdfree"] = bc
                            yi = s2["yi"] % 2
                            s2["yi"] += 1
                            y1 = tt("dve", yb2[yi][:, 0:nq], osb2[0:64, 0:nq], bc2[0:64, 0:nq], ALU.mult,
                                    deps=[c1, bc, s2["ylast"][yi]])
                            s2["bcfree"] = y1
                            s2["osfree"] = y1
                            s2["ylast"][yi] = dma("pool", sem_yb[yi], YB[h * 64:(h + 1) * 64, ycol:ycol + nq], yb2[yi][:, 0:nq],
                                                  deps=[y1])
                        s2["pend"] = [finish, 6]
                    elif s2["pend"] is not None:
                        s2["pend"][1] -= 1
                        if s2["pend"][1] <= 0:
                            s2["pend"][0]()
                            s2["pend"] = None
                if s2["pend"] is not None:
                    s2["pend"][0]()
                    s2["pend"] = None
                bufree[par] = last_pv
            while wq["pend"] or wtasks:
                for _ in range(3):
                    wtask_convert()
                for _ in range(3):
                    wtask_load()
            P.finalize()
            if stop == "p2":
                raise _Stop(nc)

        ph = ExitStack()
        with ph:
            def sb(name, shape, dt):
                return ph.enter_context(nc.sbuf_tensor(name, shape, dt))

            def ps(name, shape, dt):
                return ph.enter_context(nc.psum_tensor(name, shape, dt))

            W3b = sb("W3b", [128, DC, C3], BF16)
            Wbrb = sb("Wbrb", [128, DC, D], BF16)
            Woutb = sb("Woutb", [128, DC, D], BF16)
            gpb = sb("gpb", [128, D], F32)
            xs4 = [sb("xs4_%d" % i, [128, 4, D], F32) for i in range(2)]
            junk3 = sb("junk3", [128, D], BF16)
            ssq3 = sb("ssq3", [128, 8], F32)
            rsd3 = sb("rsd3", [128, 8], F32)
            hb3 = [sb("hb3_%d" % i, [128, D], BF16) for i in range(2)]
            hT3 = [sb("hT3_%d" % i, [128, DC, 512], BF16) for i in range(2)]
            GZ = sb("GZ", [128, 8, 512], BF16)
            GM = sb("GM", [128, 16, 512], BF16)
            yl = sb("yl", [128, 8, 512], BF16)
            gmul = sb("gmul", [128, 8, 512], BF16)
            mrg = sb("mrg", [128, DC, 512], BF16)
            t1 = [sb("t1_%d" % i, [128, 512], F32) for i in range(2)]
            t2 = [sb("t2_%d" % i, [128, 512], F32) for i in range(2)]
            osb3 = [sb("osb3_%d" % i, [128, D], F32) for i in range(2)]
            ss2 = sb("ss2", [128, 4], F32)
            rs2 = sb("rs2", [128, 2], F32)
            sem_m3 = newsem(ph)
            sem_w3 = [newsem(ph) for _ in range(8)]
            sem_x3 = [newsem(ph), newsem(ph)]
            sem_y3 = newsem(ph)
            sem_o3 = [newsem(ph), newsem(ph)]
            tp3 = [ps("tp3_%d" % i, [128, DC, 128], BF16) for i in range(2)]
            NM3 = 5
            mp3 = [ps("mp3_%d" % i, [128, 512], F32) for i in range(NM3)]

            mhalf = sb("mhalf", [128, 1], F32)
            c_mh = memset("pool", mhalf[:], -0.5)
            l_gpb = dma("sp", sem_m3, gpb[:], gpost.partition_broadcast(128))
            w3ld = []
            for cb in range(6):
                w3ld.append(dma("sp", sem_w3[cb], W3b[:, :, cb * 512:(cb + 1) * 512],
                                W3s[:, cb * 512:(cb + 1) * 512].rearrange("(c p) n -> p c n", p=128)))
            wbrld = dma("sp", sem_w3[6], Wbrb[:], Wbrs.rearrange("(c p) n -> p c n", p=128))
            woutld = dma("sp", sem_w3[7], Woutb[:], Wouts.rearrange("(c p) n -> p c n", p=128))

            s3 = dict(mpi=0, mpfree=[[] for _ in range(NM3)], junkfree=None, ssfree=None, rs2free=None, hbfree=[None, None],
                      tpfree=[None, None], xfree=[None, None], hTfree=[None, None],
                      gzfree=None, gmfree=None, ylfree=None, gmulfree=None, mrgfree=None, t1free=[None, None],
                      t2free=[None, None], oi=0, olast=[None, None], rsfree=[None] * 8, hi=0)

            def mm3(pairs, M, N, deps):
                bi = s3["mpi"] % NM3
                s3["mpi"] += 1
                bank = mp3[bi]
                o = None
                for kk, (l_, r_) in enumerate(pairs):
                    o = mm(bank[0:M, 0:N], l_, r_, kk == 0, kk == len(pairs) - 1, deps=list(deps) + s3["mpfree"][bi])
                return bi, bank, o

            units = [(xw, (4 * m + 3) * 512, m * 512, 512, y_own, m * 512) for m in range(NOWN)]
            if sample:
                units.append((xs_d, 0, NQ, NSAMP, ys_o, 0))
            NU = len(units)
            xld = {}
            hnd3 = {}
            cps3 = {}

            def u_geom(u):
                xsrc, x0, ycol, ntok, ydst, ybase = units[u]
                return (ntok + 127) // 128, min(128, ntok)

            def p3_load(u):
                if u >= NU:
                    return
                xsrc, x0, ycol, ntok, ydst, ybase = units[u]
                ntile, tn = u_geom(u)
                ld = None
                for t in range(ntile):
                    ld = dma("sp", sem_x3[u % 2], xs4[u % 2][0:tn, t, :], xsrc[x0 + t * 128:x0 + t * 128 + tn, :],
                             deps=[s3["xfree"][u % 2]])
                xld[u] = ld

            def p3_a(u, t):
                if u >= NU:
                    return
                ntile, tn = u_geom(u)
                if t >= ntile:
                    return
                c = (u % 2) * 4 + t
                ld = xld[u]
                xin = xs4[u % 2][0:tn, t, :]
                sq = act(junk3[0:tn, :], xin, AF.Square, deps=[ld, s3["junkfree"]], accum_out=ssq3[0:tn, c:c + 1])
                s3["junkfree"] = sq
                sr = ts("pool", rsd3[0:tn, c:c + 1], ssq3[0:tn, c:c + 1], 1.0 / D, EPS, ALU.mult, ALU.add,
                        deps=[sq, s3["rsfree"][c]])
                rc = tt("pool", rsd3[0:tn, c:c + 1], rsd3[0:tn, c:c + 1], mhalf[0:tn, :], ALU.pow, deps=[sr, c_mh])
                b = s3["hi"] % 2
                s3["hi"] += 1
                sc = ts("dve", hb3[b][0:tn, :], xin, rsd3[0:tn, c:c + 1], None, ALU.mult, deps=[rc, s3["hbfree"][b]])
                s3["rsfree"][c] = sc
                hnd3[(u, t)] = (b, sc)

            def p3_b(u, t):
                if u >= NU:
                    return
                ntile, tn = u_geom(u)
                if t >= ntile:
                    return
                b, sc = hnd3.pop((u, t))
                t_ = None
                for dc in range(DC):
                    t_ = tr(tp3[b][:, dc, 0:tn], hb3[b][0:tn, dc * 128:(dc + 1) * 128], ident[0:tn, 0:tn],
                            deps=[sc, s3["tpfree"][b]])
                s3["hbfree"][b] = t_
                co = act(hT3[u % 2][:, :, t * 128:t * 128 + tn], tp3[b][:, :, 0:tn], AF.Copy,
                         deps=[t_, s3["hTfree"][u % 2]])
                s3["tpfree"][b] = co
                cps3.setdefault(u, []).append(co)

            p3_load(0)
            p3_load(1)
            for t in range(4):
                p3_a(0, t) if t < 2 else None
            p3_b(0, 0)
            p3_b(0, 1)
            p3_a(0, 2)
            p3_a(0, 3)
            p3_b(0, 2)
            p3_b(0, 3)

            for u in range(NU):
                xsrc, x0, ycol, ntok, ydst, ybase = units[u]
                ntile, tn = u_geom(u)
                N = ntok
                hTu = hT3[u % 2]
                cps = cps3.pop(u)
                dma("sp", sem_y3, yl[:, 0:4, 0:N], YA[:, ycol:ycol + N].rearrange("(c p) t -> p c t", p=128),
                    deps=[s3["ylfree"]])
                yld = dma("sp", sem_y3, yl[:, 4:8, 0:N], YB[:, ycol:ycol + N].rearrange("(c p) t -> p c t", p=128),
                          deps=[s3["ylfree"]])
                lastmm = None
                gz_last = None
                gm_last = None
                g1 = None
                for c in range(24):
                    pairs = [(W3b[:, dc, c * 128:(c + 1) * 128], hTu[:, dc, 0:N]) for dc in range(DC)]
                    bi, bank, o = mm3(pairs, 128, N, cps + [w3ld[c // 4]])
                    lastmm = o
                    if c < 8:
                        ev = act(GZ[:, c, 0:N], bank[:, 0:N], AF.Silu, deps=[o, s3["gzfree"]])
                        gz_last = ev
                    else:
                        ev = act(GM[:, c - 8, 0:N], bank[:, 0:N], AF.Sigmoid, deps=[o, s3["gmfree"]])
                        gm_last = ev
                    s3["mpfree"][bi] = [ev]
                    if c == 7:
                        g1 = tt("dve", gmul[:, :, 0:N], yl[:, :, 0:N], GZ[:, :, 0:N], ALU.mult,
                                deps=[yld, gz_last, s3["gmulfree"]])
                        s3["ylfree"] = g1
                        s3["gzfree"] = g1
                        p3_a(u + 1, 0)
                        p3_a(u + 1, 1)
                    if c == 15:
                        p3_b(u + 1, 0)
                        p3_b(u + 1, 1)
                        p3_a(u + 1, 2)
                        p3_a(u + 1, 3)
                s3["hTfree"][u % 2] = lastmm
                lastbr = None
                mr_ops = []
                for dc in range(DC):
                    pa = [(Wbrb[:, c, dc * 128:(dc + 1) * 128], gmul[:, c, 0:N]) for c in range(4)]
                    bia, banka, oa = mm3(pa, 128, N, [g1, wbrld])
                    pbb = [(Wbrb[:, 4 + c, dc * 128:(dc + 1) * 128], gmul[:, 4 + c, 0:N]) for c in range(4)]
                    bib, bankb, ob = mm3(pbb, 128, N, [g1, wbrld])
                    lastbr = ob
                    k2 = dc % 2
                    m1 = tt("dve", t1[k2][:, 0:N], banka[:, 0:N], GM[:, dc, 0:N], ALU.mult, deps=[oa, gm_last, s3["t1free"][k2]])
                    s3["mpfree"][bia] = [m1]
                    m2 = tt("dve", t2[k2][:, 0:N], bankb[:, 0:N], GM[:, 8 + dc, 0:N], ALU.mult, deps=[ob, gm_last, s3["t2free"][k2]])
                    s3["mpfree"][bib] = [m2]
                    m3_ = tt("pool", mrg[:, dc, 0:N], t1[k2][:, 0:N], t2[k2][:, 0:N], ALU.add, deps=[m1, m2, s3["mrgfree"]])
                    s3["t1free"][k2] = m3_
                    s3["t2free"][k2] = m3_
                    mr_ops.append(m3_)
                    if dc == 3:
                        p3_b(u + 1, 2)
                        p3_b(u + 1, 3)
                s3["gmulfree"] = lastbr
                s3["gmfree"] = mr_ops[-1]
                lastout = None
                fin = None
                for t in range(ntile):
                    oi = s3["oi"] % 2
                    s3["oi"] += 1
                    ob_ = osb3[oi]
                    sqs = []
                    for hf in range(2):
                        pairs = [(mrg[:, dc, t * 128:t * 128 + tn], Woutb[:, dc, hf * 512:(hf + 1) * 512]) for dc in range(DC)]
                        bi, bank, o = mm3(pairs, tn, 512, mr_ops + [woutld])
                        lastout = o
                        ev = act(ob_[0:tn, hf * 512:(hf + 1) * 512], bank[0:tn, :], AF.Copy, deps=[o, s3["olast"][oi]])
                        s3["mpfree"][bi] = [ev]
                        sq = act(junk3[0:tn, 0:512], ob_[0:tn, hf * 512:(hf + 1) * 512], AF.Square,
                                 deps=[ev, s3["ssfree"], s3["junkfree"]], accum_out=ss2[0:tn, hf:hf + 1])
                        s3["junkfree"] = sq
                        sqs.append(sq)
                    a_ = tt("dve", ss2[0:tn, 2:3], ss2[0:tn, 0:1], ss2[0:tn, 1:2], ALU.add, deps=sqs)
                    s3["ssfree"] = a_
                    sr = ts("pool", rs2[0:tn, 0:1], ss2[0:tn, 2:3], 1.0 / D, EPS, ALU.mult, ALU.add, deps=[a_, s3["rs2free"]])
                    rc = tt("pool", rs2[0:tn, 0:1], rs2[0:tn, 0:1], mhalf[0:tn, :], ALU.pow, deps=[sr, c_mh])
                    f1 = stt(ob_[0:tn, :], ob_[0:tn, :], rs2[0:tn, 0:1], gpb[0:tn, :], ALU.mult, ALU.mult, deps=[rc, l_gpb])
                    s3["rs2free"] = f1
                    f2 = tt("pool", ob_[0:tn, :], ob_[0:tn, :], xs4[u % 2][0:tn, t, :], ALU.add, deps=[f1])
                    s3["olast"][oi] = dma("pool", sem_o3[oi], ydst[ybase + t * 128:ybase + t * 128 + tn, :],
                                          ob_[0:tn, :], deps=[f2])
                    fin = f2
                s3["mrgfree"] = lastout
                s3["xfree"][u % 2] = fin
                p3_load(u + 2)
                for t in range(4):
                    if (u + 1, t) in hnd3:
                        p3_b(u + 1, t)
            P.finalize()
    return nc


_CACHE = {}


def _get_nc(NSLOT=32, sample=True):
    key = (NSLOT, sample)
    if key not in _CACHE:
        _CACHE[key] = build(NSLOT, sample)
    return _CACHE[key]


def make_in_maps(inputs, NSLOT=32):
    f32 = np.float32
    xp = np.asarray(inputs["x_prompt"], f32)
    B, S, _ = xp.shape
    L = NSLOT * 512
    w_in = np.asarray(inputs["w_in"], f32)[0]
    cols1 = np.r_[0:512, 512:1024, 1024:1536, 2048:2560, 2560:3072, 3072:3584, 4096:4104]
    cols3 = np.r_[1536:2048, 3584:4096, 4104:5128, 5128:6152]
    w1 = np.ascontiguousarray(w_in[:, cols1])
    w3 = np.ascontiguousarray(w_in[:, cols3])
    wbr = np.ascontiguousarray(np.concatenate([np.asarray(inputs["w_br_a"], f32)[0], np.asarray(inputs["w_br_b"], f32)[0]], 0))
    wout = np.ascontiguousarray(np.asarray(inputs["w_out"], f32)[0])
    maps = []
    for c in range(8):
        b, j = c // 4, c % 4
        start = (j - 3) * 512
        win = np.zeros((L, D), f32)
        lo = max(start, 0)
        hi = min(start + L, S)
        win[lo - start:hi - start] = xp[b, lo:hi]
        sm = np.zeros((128, 3), f32)
        for s_ in range(3):
            if s_ + j - 3 < 0:
                sm[:, s_] = -BIG
        maps.append({
            "xw": win, "w1": w1, "w3": w3, "wbr": wbr, "wout": wout,
            "gpre": np.ascontiguousarray(np.asarray(inputs["g_pre"], f32)[0]),
            "gpost": np.ascontiguousarray(np.asarray(inputs["g_post"], f32)[0]),
            "bfv": np.ascontiguousarray(np.asarray(inputs["b_f"], f32)[0].reshape(8, 1)),
            "rel": np.ascontiguousarray(np.asarray(inputs["rel_table"], f32)[0]),
            "smask": sm,
            "xs": np.ascontiguousarray(np.asarray(inputs["x_sample"], f32)[c]),
            "cak": np.ascontiguousarray(np.asarray(inputs["cache_a_k"], f32)[0, c].reshape(512, 512)),
            "cav": np.ascontiguousarray(np.asarray(inputs["cache_a_v"], f32)[0, c].reshape(512, 512)),
            "cbk": np.ascontiguousarray(np.asarray(inputs["cache_b_k"], f32)[0, c].reshape(LS, 512)),
            "cbv": np.ascontiguousarray(np.asarray(inputs["cache_b_v"], f32)[0, c].reshape(LS, 512)),
            "cbl": np.ascontiguousarray(np.asarray(inputs["cache_b_logf"], f32)[0, c].reshape(LS, 8)),
        })
    return maps


def assemble(results, B, S, NSLOT=32):
    f32 = np.float32
    NOWN = NSLOT // 4
    y = np.zeros((B, S, D), f32)
    bk = np.zeros((1, B, S, 8, 64), f32)
    bv = np.zeros((1, B, S, 8, 64), f32)
    blf = np.zeros((1, B, S, 8), f32)
    akp = np.zeros((1, B, 512, 8, 64), f32)
    avp = np.zeros((1, B, 512, 8, 64), f32)
    ys = np.zeros((8, NSAMP, D), f32)
    aks = np.zeros((1, 8, 512, 8, 64), f32)
    avs = np.zeros((1, 8, 512, 8, 64), f32)
    bks = np.zeros((1, 8, NSAMP, 8, 64), f32)
    bvs = np.zeros((1, 8, NSAMP, 8, 64), f32)
    bls = np.zeros((1, 8, NSAMP, 8), f32)
    for c in range(8):
        r = results[c]
        b, j = c // 4, c % 4
        for m in range(NOWN):
            s0 = (4 * m + j) * 512
            y[b, s0:s0 + 512] = r["y_own"][m * 512:(m + 1) * 512]
            bk[0, b, s0:s0 + 512] = r["bk_own"][m * 512:(m + 1) * 512].reshape(512, 8, 64)
            bv[0, b, s0:s0 + 512] = r["bv_own"][m * 512:(m + 1) * 512].reshape(512, 8, 64)
            blf[0, b, s0:s0 + 512] = r["blf_own"][m * 512:(m + 1) * 512]
        if j == 3:
            akp[0, b] = r["akp"].reshape(512, 8, 64)
            avp[0, b] = r["avp"].reshape(512, 8, 64)
        ys[c] = r["ys"]
        aks[0, c] = r["aks"].reshape(512, 8, 64)
        avs[0, c] = r["avs"].reshape(512, 8, 64)
        bks[0, c] = r["bks"].reshape(NSAMP, 8, 64)
        bvs[0, c] = r["bvs"].reshape(NSAMP, 8, 64)
        bls[0, c] = r["bls"]
    return (y, ys, akp, avp, bk, bv, blf, aks, avs, bks, bvs, bls)


def kernel(**inputs):
    nc = _get_nc(32, True)
    maps = make_in_maps(inputs, 32)
    res = run_bass_kernel_spmd(nc, maps, core_ids=list(range(8)))
    B, S, _ = np.asarray(inputs["x_prompt"]).shape
    return assemble(res.results, B, S, 32)
```

```python
import numpy as np
from contextlib import ExitStack
import concourse.bass as bass
import concourse.mybir as mybir
from concourse.bass_utils import run_bass_kernel_spmd

F32 = mybir.dt.float32
BF16 = mybir.dt.bfloat16
AF = mybir.ActivationFunctionType
ALU = mybir.AluOpType

D = 1024
DC = 8
C1 = 3080
NT16 = 4
C1P = 3200
C3 = 3072
SCALE = 0.125
BIG = 30000.0
EPS = 1e-6
NSAMP = 16
LS = 4096
LSP = 4224
CH = 512
CH2 = 512


class _Stop(Exception):
    pass


def _aslist(x):
    if x is None:
        return []
    return list(x) if isinstance(x, (list, tuple)) else [x]


def _last2(a, b):
    return _aslist(a) + [b]


class Op:
    __slots__ = ("eng", "fn", "deps", "sig", "val", "dsem", "dval")

    def __init__(self, eng, fn, deps, dsem=None, dval=0):
        self.eng = eng
        self.fn = fn
        self.deps = [d for d in deps if d is not None]
        self.sig = False
        self.val = 0
        self.dsem = dsem
        self.dval = dval


class Prog:
    ENGS = ("pe", "act", "dve", "pool", "sp")

    def __init__(self, nc, sems):
        self.nc = nc
        self.sems = sems
        self.lists = {e: [] for e in self.ENGS}
        self.dma_cnt = {}
        self.dma_sems = {}
        self.base = {e: 0 for e in self.ENGS}

    def op(self, eng, fn, deps=()):
        o = Op(eng, fn, deps)
        self.lists[eng].append(o)
        return o

    def dma(self, eng, sem, fn, deps=()):
        k = id(sem)
        self.dma_cnt[k] = self.dma_cnt.get(k, 0) + 16
        self.dma_sems[k] = sem
        o = Op(eng, fn, deps, dsem=sem, dval=self.dma_cnt[k])
        self.lists[eng].append(o)
        return o

    def finalize(self):
        sems = self.sems
        for e in self.ENGS:
            for o in self.lists[e]:
                for d in o.deps:
                    if d.dsem is None and not (d.eng == "pe" and e == "pe"):
                        d.sig = True
        for e in self.ENGS:
            c = self.base[e]
            for o in self.lists[e]:
                if o.dsem is None and o.sig:
                    c += 1
                o.val = c
            self.base[e] = c
        nc = self.nc
        lists = self.lists
        final = [(self.dma_sems[k], v) for k, v in self.dma_cnt.items()]
        with nc.Block() as block:
            def run(e, engine):
                waited = {}
                for o in lists[e]:
                    for d in o.deps:
                        if d.dsem is not None:
                            s, v = d.dsem, d.dval
                        else:
                            if d.eng == "pe" and e == "pe":
                                continue
                            s, v = sems[d.eng], d.val
                        key = id(s)
                        if waited.get(key, 0) >= v:
                            continue
                        waited[key] = v
                        engine.wait_ge(s, v)
                    ins = o.fn(engine)
                    if o.dsem is not None:
                        ins.then_inc(o.dsem, 16)
                    elif o.sig:
                        ins.then_inc(sems[e], 1)
                if e == "sp":
                    for (s, v) in final:
                        engine.wait_ge(s, v)

            @block.tensor
            def _(eng):
                run("pe", eng)

            @block.scalar
            def _(eng):
                run("act", eng)

            @block.vector
            def _(eng):
                run("dve", eng)

            @block.gpsimd
            def _(eng):
                run("pool", eng)

            @block.sync
            def _(eng):
                run("sp", eng)
        self.lists = {e: [] for e in self.ENGS}


def build(NSLOT=32, sample=True, stop=None):
    try:
        return _build(NSLOT, sample, stop)
    except _Stop as e:
        return e.args[0]


def _build(NSLOT=32, sample=True, stop=None):
    NOWN = NSLOT // 4
    L = NSLOT * 512
    NQ = NOWN * 512
    NQA = NQ + NSAMP
    nc = bass.Bass("TRN2", target_bir_lowering=False)

    def din(name, shape, dt=F32):
        return nc.dram_tensor(name, shape, dt, kind="ExternalInput").ap()

    def dout(name, shape, dt=F32):
        return nc.dram_tensor(name, shape, dt, kind="ExternalOutput").ap()

    def dscr(name, shape, dt):
        return nc.dram_tensor(name, shape, dt, kind="Internal").ap()

    xw = din("xw", [L, D])
    w1 = din("w1", [D, C1])
    w3 = din("w3", [D, C3])
    wbr = din("wbr", [D, D])
    wout = din("wout", [D, D])
    gpre = din("gpre", [D])
    gpost = din("gpost", [D])
    bfv = din("bfv", [8, 1])
    rel = din("rel", [8, 192])
    smask_d = din("smask", [128, 3])
    xs_d = din("xs", [NSAMP, D])
    cak = din("cak", [512, 512])
    cav = din("cav", [512, 512])
    cbk = din("cbk", [LS, 512])
    cbv = din("cbv", [LS, 512])
    cbl = din("cbl", [LS, 8])

    y_own = dout("y_own", [NQ, D])
    bk_own = dout("bk_own", [NQ, 512])
    bv_own = dout("bv_own", [NQ, 512])
    blf_own = dout("blf_own", [NQ, 8])
    akp = dout("akp", [512, 512])
    avp = dout("avp", [512, 512])
    ys_o = dout("ys", [NSAMP, D])
    aks = dout("aks", [512, 512])
    avs = dout("avs", [512, 512])
    bks = dout("bks", [NSAMP, 512])
    bvs = dout("bvs", [NSAMP, 512])
    bls = dout("bls", [NSAMP, 8])

    Ks = dscr("Ks", [8, 68, L], BF16)
    Vs = dscr("Vs", [L, 512], BF16)
    Qs = dscr("Qs", [8, 68, NQA], BF16)
    YA = dscr("YA", [512, NQA], BF16)
    YB = dscr("YB", [512, NQA], BF16)
    ext = dscr("ext", [8, 768], F32)
    Erep = dscr("Erep", [8, 128, 768], F32)
    W3s = dscr("W3s", [D, C3], BF16)
    Wbrs = dscr("Wbrs", [D, D], BF16)
    Wouts = dscr("Wouts", [D, D], BF16)
    Kss = dscr("Kss", [8, 68, LSP], BF16)
    Vss = dscr("Vss", [LSP, 512], BF16)

    outer = ExitStack()
    with outer:
        def osb(name, shape, dt):
            return outer.enter_context(nc.sbuf_tensor(name, shape, dt))

        sems = {e: outer.enter_context(nc.semaphore("s_" + e)) for e in ("pe", "act", "dve", "pool")}
        nsem = [0]

        def newsem(stack):
            nsem[0] += 1
            return outer.enter_context(nc.semaphore("d%d" % nsem[0]))

        P = Prog(nc, sems)

        def chk(tag):
            if stop == tag:
                P.finalize()
                raise _Stop(nc)

        def mm(out, lhsT, rhs, start, stop, deps=(), skip=False):
            return P.op("pe", lambda e: e.matmul(out, lhsT=lhsT, rhs=rhs, start=start, stop=stop,
                                                 skip_group_check=skip), deps)

        def tr(out, in_, idn, deps=()):
            return P.op("pe", lambda e: e.transpose(out=out, in_=in_, identity=idn), deps)

        def act(out, in_, func, deps=(), **kw):
            return P.op("act", lambda e: e.activation(out=out, in_=in_, func=func, **kw), deps)

        def ts(eng, out, in0, s1, s2, op0, op1=None, deps=()):
            if op1 is None:
                return P.op(eng, lambda e: e.tensor_scalar(out=out, in0=in0, scalar1=s1, scalar2=None, op0=op0), deps)
            return P.op(eng, lambda e: e.tensor_scalar(out=out, in0=in0, scalar1=s1, scalar2=s2, op0=op0, op1=op1), deps)

        def tt(eng, out, in0, in1, op, deps=()):
            return P.op(eng, lambda e: e.tensor_tensor(out=out, in0=in0, in1=in1, op=op), deps)

        def stt(out, in0, scalar, in1, op0, op1, deps=()):
            return P.op("dve", lambda e: e.scalar_tensor_tensor(out=out, in0=in0, scalar=scalar, in1=in1,
                                                                op0=op0, op1=op1), deps)

        def cp(eng, out, in_, deps=()):
            if eng == "act":
                return act(out, in_, AF.Copy, deps)
            return P.op(eng, lambda e: e.tensor_copy(out=out, in_=in_), deps)

        def memset(eng, ap, val, deps=()):
            return P.op(eng, lambda e: e.memset(ap, val), deps)

        def dma(q, sem, out, in_, deps=(), slow=False):
            if slow:
                q = "pool"
            return P.dma(q, sem, lambda e: e.dma_start(out=out, in_=in_, allow_slow_non_contiguous=slow), deps)

        ident = osb("ident", [128, 128], BF16)
        tri = osb("tri", [128, 128], BF16)
        onesf = osb("onesf", [128, 128], F32)
        smask = osb("smask_sb", [128, 3], F32)
        nbf = osb("nbf", [8, 1], F32)
        gpre_sb = osb("gpre_sb", [128, DC], F32)

        ph = ExitStack()
        with ph:
            def sb(name, shape, dt):
                return ph.enter_context(nc.sbuf_tensor(name, shape, dt))

            def ps(name, shape, dt):
                return ph.enter_context(nc.psum_tensor(name, shape, dt))

            sem_misc = newsem(ph)
            sem_gp = newsem(ph)
            sem_e = newsem(ph)
            sem_e4 = newsem(ph)
            sem_e1 = newsem(ph)
            sem_tl = newsem(ph)
            sem_w = [newsem(ph), newsem(ph)]
            c_ones = memset("pool", onesf[:], 1.0)
            c_id = P.op("pool", lambda e: e.affine_select(out=ident[:], in_=onesf[:], pattern=[[1, 128]],
                                                         compare_op=ALU.is_equal, fill=0.0, base=0,
                                                         channel_multiplier=-1), [c_ones])
            c_tri = P.op("pool", lambda e: e.affine_select(out=tri[:], in_=onesf[:], pattern=[[1, 128]],
                                                          compare_op=ALU.is_ge, fill=0.0, base=0,
                                                          channel_multiplier=-1), [c_ones])
            l_sm = dma("sp", sem_misc, smask[:], smask_d)
            l_bf = dma("sp", sem_misc, nbf[:], bfv)
            l_gp = dma("sp", sem_gp, gpre_sb[:], gpre.rearrange("(c p) -> p c", p=128), slow=True)
            c_nbf = ts("dve", nbf[:], nbf[:], -1.0, None, ALU.mult, deps=[l_bf])

            chk("a0")
            tm32_early = [sb("tm32_%d" % i, [128, 512], F32) for i in range(3)]
            EB = sb("EB", [128, 8, 5, 128], F32)
            if True:
                T = EB
                ea, eb = tm32_early[0], tm32_early[1]
                e0 = dma("sp", sem_e, ea[0:8, 64:256], rel)
                z1 = memset("dve", ea[0:8, 0:64], 0.0)
                z2 = memset("dve", eb[0:8, :], 0.0)
                z3 = ts("dve", ea[0:8, 0:64], ea[0:8, 0:64], ea[0:8, 64:65], None, ALU.add, deps=[e0, z1])
                z4 = ts("dve", eb[0:8, :], eb[0:8, :], ea[0:8, 255:256], None, ALU.add, deps=[e0, z2])
                e1 = dma("sp", sem_e1, ext[:, 0:256], ea[0:8, 0:256], deps=[z3])
                e3 = dma("sp", sem_e1, ext[:, 256:768], eb[0:8, :], deps=[z4])
                e4 = dma("sp", sem_e4, Erep,
                         bass.AP(tensor=ext.tensor, offset=0, ap=[[768, 8], [0, 128], [1, 768]]), deps=[e3])
                chk("a1")
                tl = None
                for h in range(8):
                    src = bass.AP(tensor=Erep.tensor, offset=h * 128 * 768 + 127, ap=[[767, 128], [128, 5], [1, 128]])
                    tl = dma("sp", sem_tl, T[:, h, :, :], src, deps=[e4])
                c_t0 = memset("dve", T[64:128, :, 0, 0:64], -BIG, deps=[tl])
                c_t4 = memset("dve", T[0:64, :, 4, 64:128], -BIG, deps=[tl])
                Tf = T[:].rearrange("p a b c -> p (a b c)")
                c_tl = act(Tf, Tf, AF.Exp, deps=[c_t0, c_t4])
            T_ready = [c_tl]
            chk("a2")
            Wb = sb("Wb", [128, DC, C1P], BF16)
            c_wpad = memset("dve", Wb[:, :, C1:C1P], 0.0, deps=[c_tl])
            CQ = C1 // 4
            wst = [sb("wst%d" % i, [128, CQ], F32) for i in range(4)]
            sem_w4 = [newsem(ph) for _ in range(4)]
            wfree = [c_tl] * 4
            W_ready = []
            k = 0
            for dc in range(DC):
                for hf in range(4):
                    c0 = hf * CQ
                    s = k % 4
                    ld = dma("sp", sem_w4[s], wst[s][:], w1[dc * 128:(dc + 1) * 128, c0:c0 + CQ],
                             deps=[wfree[s]])
                    o = ts("dve", Wb[:, dc, c0:c0 + CQ], wst[s][:], gpre_sb[:, dc:dc + 1], None, ALU.mult,
                           deps=[ld, l_gp])
                    wfree[s] = o
                    W_ready.append(o)
                    k += 1
            W_ready = W_ready[-4:] + [c_wpad]

            chk("a")
            NX = 3
            xt = [sb("xt%d" % i, [128, D], F32) for i in range(NX)]
            sem_x = [newsem(ph) for _ in range(NX)]
            junk = sb("junk", [128, D], BF16)
            ssq = sb("ssq", [128, 4], F32)
            rsd = sb("rsd", [128, 4], F32)
            hb = [sb("hb%d" % i, [128, D], BF16) for i in range(2)]
            NF = 6
            fmst = [sb("fmst%d" % i, [128, 512], BF16) for i in range(NF)]
            sem_fm = [newsem(ph) for _ in range(NF)]
            NT32 = 3
            tm32 = tm32_early
            sem_t32 = [newsem(ph) for _ in range(NT32)]
            tm16 = [sb("tm16_%d" % i, [128, 512], BF16) for i in range(NT16)]
            sem_t16 = [newsem(ph) for _ in range(NT16)]
            pbt = [[sb("pbt%d_%d" % (a_, i), [128, 512], BF16) for i in range(2)] for a_ in range(2)]
            ptmp = [[sb("ptmp%d_%d" % (a_, i), [128, 512], BF16) for i in range(2)] for a_ in range(2)]
            osbuf = [sb("osbuf%d" % i, [65, 512], F32) for i in range(2)]
            rd = sb("rd", [128, 1024], F32)
            sel = sb("sel", [128, 128], F32)
            yab = [sb("yab%d" % i, [64, 512], BF16) for i in range(2)]
            sem_ya = [newsem(ph) for _ in range(2)]
            el8 = sb("el8", [8, 512], F32)
            G8 = [sb("G8_%d" % i, [8, 512], F32) for i in range(2)]
            on8 = sb("on8", [8, 512], F32)
            r8 = sb("r8", [8, 512], F32)
            lim = sb("lim", [8, 4, 512], BF16)
            qst = sb("qst", [8, 4, 512], BF16)
            lf8 = sb("lf8", [8, 512], F32)
            sem_lim = newsem(ph)
            sem_qst = newsem(ph)
            sem_lf = newsem(ph)

            tp = [ps("tp%d" % i, [128, DC, 128], BF16) for i in range(1)] * 2
            mp = [ps("mp%d" % i, [128, 512], F32) for i in range(3)]
            sT = [ps("sT%d" % i, [128, 512], F32) for i in range(2)]
            oT = [ps("oT%d" % i, [128, 512], F32) for i in range(2)]
            c_rd = memset("pool", rd[:], 0.0)
            c_sel0 = memset("pool", sel[:], 0.0)
            c_sel = memset("pool", sel[64:65, :], 1.0, deps=[c_sel0])

            c_on8 = memset("pool", on8[:], 1.0)
            c_lim = memset("pool", lim[:, 0, :], 1.0)
            c_qst = memset("pool", qst[:, 1:4, :], 1.0)

            st = dict(xi=0, xfree=[None] * NX, hbfree=[None, None], tpfree=[None, None], hTfree=[None, None],
                      mpi=0, mpfree=[[], [], []], fmi=0, fmlast=[None] * NF, t32i=0, t32last=[e3, e3, None],
                      t16i=0, t16last=[None] * NT16, rsfree=[None] * 4, sTi=0, sTfree=[None, None],
                      pbfree=[[None, None], [None, None]], ptfree=[[None, None], [None, None]], oTfree=[[], []], yai=0, yalast=[None, None],
                      limlast=None, qstlast=None, lflast=None, Gi=0, Gprev=None, KATfree=None, VAfree=None,
                      QATfree=None, osfree=[None, None], rdfree=[None, None], pend_norm=None, elfree=None, r8free=None)

            def norm_a(src_ap, n):
                i = st["xi"]
                st["xi"] += 1
                xs_ = xt[i % NX]
                ld = dma("sp", sem_x[i % NX], xs_[0:n, :], src_ap, deps=[st["xfree"][i % NX]])
                c = i % 4
                sq = act(junk[0:n, :], xs_[0:n, :], AF.Square, deps=[ld], accum_out=ssq[0:n, c:c + 1])
                sr = act(rsd[0:n, c:c + 1], ssq[0:n, c:c + 1], AF.Ln, deps=[sq, st["rsfree"][c]],
                         scale=1.0 / D, bias=EPS)
                rc = act(rsd[0:n, c:c + 1], rsd[0:n, c:c + 1], AF.Exp, deps=[sr], scale=-0.5)
                b = i % 2
                sc = ts("dve", hb[b][0:n, :], xs_[0:n, :], rsd[0:n, c:c + 1], None, ALU.mult,
                        deps=[rc, ld, st["hbfree"][b]])
                st["rsfree"][c] = sc
                st["xfree"][i % NX] = sc
                return (b, sc, n)

            def norm_b(hnd, hTdst, deps_hT):
                b, sc, n = hnd
                t_ = None
                for dc in range(DC):
                    t_ = tr(tp[b][:, dc, 0:n], hb[b][0:n, dc * 128:(dc + 1) * 128], ident[0:n, 0:n],
                            deps=[sc, st["tpfree"][0], c_id])
                st["hbfree"][b] = t_
                co = act(hTdst, tp[b][:, :, 0:n], AF.Copy, deps=[t_] + list(deps_hT))
                st["tpfree"][0] = co
                return co

            def norm_transpose(src_ap, n, hTdst, deps_hT):
                co = norm_b(norm_a(src_ap, n), hTdst, deps_hT)
                return co, None, None

            def mm_group(pairs, M, N, deps):
                bi = st["mpi"] % 3
                st["mpi"] += 1
                bank = mp[bi]
                o = None
                n_ = len(pairs)
                for kk, (l_, r_) in enumerate(pairs):
                    o = mm(bank[0:M, 0:N], l_, r_, kk == 0, kk == n_ - 1, deps=list(deps) + st["mpfree"][bi])
                return bi, bank, o

            def fm_chunk(Wt, c0, M, hTs, N, deps):
                pairs = [(Wt[:, dc, c0:c0 + M], hTs[:, dc, 0:N]) for dc in range(DC)]
                return mm_group(pairs, M, N, deps)

            def tm_group(Wt, c0, ncol, hTs, t0, n, deps):
                pairs = [(hTs[:, dc, t0:t0 + n], Wt[:, dc, c0:c0 + ncol]) for dc in range(DC)]
                return mm_group(pairs, n, ncol, deps)

            def store_fm_heads(bank, bi, mmop, dst, hpair, col0, N, eng="dve"):
                s = st["fmi"] % NF
                st["fmi"] += 1
                ev = cp(eng, fmst[s][:, 0:N], bank[:, 0:N], deps=[mmop, st["fmlast"][s]])
                st["mpfree"][bi] = [ev]
                d_ = None
                for hh in range(2):
                    d_ = dma("pool", sem_fm[s], dst[2 * hpair + hh, 0:64, col0:col0 + N],
                             fmst[s][hh * 64:(hh + 1) * 64, 0:N], deps=[ev])
                st["fmlast"][s] = d_
                return ev

            def fpath(bank, bi, mmop, N, col0, Kdst, qcol0, own, lfdst):
                gi = st["Gi"] % 2
                st["Gi"] += 1
                G = G8[gi]
                a1 = act(el8[:, 0:N], bank[0:8, 0:N], AF.Exp, deps=[mmop, c_nbf, st["elfree"]], scale=-1.0, bias=nbf[:])
                st["mpfree"][bi] = [a1]
                a2 = act(el8[:, 0:N], el8[:, 0:N], AF.Ln, deps=[a1], bias=1.0)
                init = 0.0 if st["Gprev"] is None else st["Gprev"][0]
                sdeps = [a2, c_on8] + ([] if st["Gprev"] is None else [st["Gprev"][1]])
                sc_ = P.op("dve", lambda e: e.tensor_tensor_scan(out=G[:, 0:N], data0=on8[:, 0:N], data1=el8[:, 0:N],
                                                                 initial=init, op0=ALU.mult, op1=ALU.add),
                           sdeps + [st["limlast"], st["qstlast"]])
                st["Gprev"] = (G[:, N - 1:N], sc_)
                d1 = ts("dve", lim[:, 1, 0:N], G[:, 0:N], 8.0, None, ALU.mult, deps=[sc_, st["limlast"], c_lim])
                d2 = stt(r8[:, 0:N], G[:, 0:N], 8.0, lim[:, 1, 0:N], ALU.mult, ALU.subtract, deps=[d1, st["r8free"]])
                d3 = cp("dve", lim[:, 2, 0:N], r8[:, 0:N], deps=[d2])
                d4 = tt("dve", lim[:, 3, 0:N], r8[:, 0:N], lim[:, 2, 0:N], ALU.subtract, deps=[d3])
                st["r8free"] = d4
                st["limlast"] = dma("pool", sem_lim, Kdst[:, 64:68, col0:col0 + N], lim[:, :, 0:N], deps=[d4])
                last = d4
                if own:
                    q1 = ts("dve", qst[:, 0, 0:N], G[:, 0:N], -8.0, None, ALU.mult, deps=[sc_, st["qstlast"], c_qst])
                    st["qstlast"] = dma("pool", sem_qst, Qs[:, 64:68, qcol0:qcol0 + N], qst[:, :, 0:N], deps=[q1])
                    q2 = ts("dve", lf8[:, 0:N], el8[:, 0:N], -1.0, None, ALU.mult, deps=[a2, st["lflast"]])
                    st["lflast"] = dma("sp", sem_lf, lfdst.rearrange("t h -> h t"), lf8[:, 0:N], deps=[q2], slow=True)
                    last = q2
                st["elfree"] = last
                return last

            def band2(heads, Qaps, tiles_of, nq, ycol0, m0ctx, filler=None):
                nt = len(tiles_of[0])

                def emit_qk(a_, ti):
                    Kap, Vap, nk, q0, n, Eb, cf = tiles_of[a_][ti]
                    return mm(sT[a_][0:nk, 0:n], Kap, Qaps[a_][:, q0:q0 + n], True, True,
                              deps=[st["sTfree"][a_]] + st["Kready"] + st["Qready"])

                def emit_exp(a_, ti, qk):
                    Kap, Vap, nk, q0, n, Eb, cf = tiles_of[a_][ti]
                    kw = {"scale": SCALE}
                    if cf and m0ctx:
                        kw["bias"] = smask[0:nk, 2:3]
                    ex = act(ptmp[a_][ti % 2][0:nk, 0:n], sT[a_][0:nk, 0:n], AF.Exp,
                             deps=[qk, st["ptfree"][a_][ti % 2], l_bf], **kw)
                    st["sTfree"][a_] = ex
                    po_, pi_ = pbt[a_][ti % 2][0:nk, 0:n], ptmp[a_][ti % 2][0:nk, 0:n]
                    if len(Eb.shape) == 3:
                        po_ = po_.rearrange("p (a q) -> p a q", q=128)
                        pi_ = pi_.rearrange("p (a q) -> p a q", q=128)
                    mu = tt("dve", po_, pi_, Eb, ALU.mult, deps=[ex, st["pbfree"][a_][ti % 2]] + T_ready)
                    st["ptfree"][a_][ti % 2] = mu
                    return mu

                def emit_pv(a_, ti, ex):
                    Kap, Vap, nk, q0, n, Eb, cf = tiles_of[a_][ti]
                    pv = mm(oT[a_][0:65, q0:q0 + n], Vap, pbt[a_][ti % 2][0:nk, 0:n], ti == 0, ti == nt - 1,
                            deps=[ex] + st["oTfree"][a_] + st["Vready"], skip=True)
                    st["pbfree"][a_][ti % 2] = pv
                    return pv

                exs = {}
                pvs = {}
                qkA = emit_qk(0, 0)
                exs[(0, 0)] = emit_exp(0, 0, qkA)
                qkB = emit_qk(1, 0)
                exs[(1, 0)] = emit_exp(1, 0, qkB)
                for ti in range(nt):
                    if ti == 1 and st["pend_norm"] is not None:
                        st["pend_norm"]()
                        st["pend_norm"] = None
                    pvs[0] = emit_pv(0, ti, exs.pop((0, ti)))
                    if ti + 1 < nt:
                        q_ = emit_qk(0, ti + 1)
                        exs[(0, ti + 1)] = emit_exp(0, ti + 1, q_)
                    pvs[1] = emit_pv(1, ti, exs.pop((1, ti)))
                    if ti + 1 < nt:
                        q_ = emit_qk(1, ti + 1)
                        exs[(1, ti + 1)] = emit_exp(1, ti + 1, q_)
                    if filler is not None and ti % 2 == 1:
                        filler()
                norm_jobs = []
                c1s = []
                for a_ in range(2):
                    c1 = act(osbuf[a_][0:65, 0:nq], oT[a_][0:65, 0:nq], AF.Copy, deps=[pvs[a_], st["osfree"][a_]])
                    st["oTfree"][a_] = [c1]
                    c1s.append(c1)
                for a_ in range(2):
                    h = heads[a_]
                    l1 = act(rd[64:65, a_ * 512:a_ * 512 + nq], osbuf[a_][64:65, 0:nq], AF.Ln,
                             deps=[c1s[a_], st["rdfree"][a_], c_rd])
                    r1 = act(rd[64:65, a_ * 512:a_ * 512 + nq], rd[64:65, a_ * 512:a_ * 512 + nq], AF.Exp, deps=[l1], scale=-1.0)
                    norm_jobs.append((a_, h, r1, c1s[a_]))

                def finish(norm_jobs=norm_jobs, nq=nq, ycol0=ycol0):
                    for (a_, h, r1, c1) in norm_jobs:
                        bi = st["mpi"] % 3
                        st["mpi"] += 1
                        bc = mm(mp[bi][:, 0:nq], sel[:, :], rd[:, a_ * 512:a_ * 512 + nq], True, True,
                                deps=[r1, c_sel] + st["mpfree"][bi])
                        st["rdfree"][a_] = bc
                        yi = st["yai"] % 2
                        st["yai"] += 1
                        y1 = tt("dve", yab[yi][:, 0:nq], osbuf[a_][0:64, 0:nq], mp[bi][0:64, 0:nq], ALU.mult,
                                deps=[c1, bc, st["yalast"][yi]])
                        st["mpfree"][bi] = [y1]
                        st["osfree"][a_] = y1
                        st["yalast"][yi] = dma("pool", sem_ya[yi], YA[h * 64:(h + 1) * 64, ycol0:ycol0 + nq], yab[yi][:, 0:nq],
                                               deps=[y1])
                st["pend_norm"] = finish
                return [pvs[0], pvs[1]]

            st["Kready"] = []
            st["Qready"] = []
            st["Vready"] = []

            if sample:
                sp_ = ExitStack()
                with sp_:
                    def ssb(name, shape, dt):
                        return sp_.enter_context(nc.sbuf_tensor(name, shape, dt))
                    hTs = ssb("hTs", [128, DC, NSAMP], BF16)
                    QATs = ssb("QATs", [128, 8, NSAMP], BF16)
                    c_qz = memset("pool", QATs[:], 0.0)
                    KATs = ssb("KATs", [128, 4, 640], BF16)
                    VAs = ssb("VAs", [128, 5, 8, 65], BF16)
                    cst = [ssb("cst%d" % i, [128, 512], F32) for i in range(2)]
                    sem_c = [newsem(sp_), newsem(sp_)]
                    c16 = [ssb("c16_%d" % i, [128, 512], BF16) for i in range(2)]
                    kst = ssb("kst", [128, 4, 512], BF16)
                    sem_kst = newsem(sp_)
                    lc = ssb("lc", [8, LS], F32)
                    Gc = ssb("Gc", [8, CH2], F32)
                    on8c = ssb("on8c", [8, CH2], F32)
                    r8c = ssb("r8c", [8, CH2], F32)
                    limc2 = [ssb("limc%d" % i, [8, 4, CH2], BF16) for i in range(2)]
                    sem_lc = newsem(sp_)
                    sem_limc2 = [newsem(sp_), newsem(sp_)]
                    sem_dd = newsem(sp_)
                    sem_vc = newsem(sp_)
                    lcl = None
                    for g in range(LS // CH):
                        lcl = dma("sp", sem_lc, lc[:, g * CH:(g + 1) * CH], cbl[g * CH:(g + 1) * CH, :].rearrange("t h -> h t"),
                                  slow=True)
                    c_vas = memset("pool", VAs[:, :, :, 64:65], 1.0)
                    c_on8c = memset("pool", on8c[:], 1.0)
                    c_limc = memset("pool", limc2[0][:, 0, :], 1.0)
                    c_limc = memset("pool", limc2[1][:, 0, :], 1.0, deps=[c_limc])

                    dma("sp", sem_dd, aks[0:496, :], cak[16:512, :])
                    dma("sp", sem_dd, avs[0:496, :], cav[16:512, :])

                    co, _, _ = norm_transpose(xs_d, NSAMP, hTs[:, :, :], [])
                    wdeps = [co] + W_ready
                    chk("b0")
                    for i in range(4):
                        bi, bank, o = fm_chunk(Wb, 0 + i * 128, 128, hTs, NSAMP, wdeps)
                        ev0 = cp("dve", QATs[0:64, 2 * i, :], bank[0:64, 0:NSAMP], deps=[o, c_qz])
                        ev = cp("dve", QATs[64:128, 2 * i + 1, :], bank[64:128, 0:NSAMP], deps=[o, c_qz])
                        st["mpfree"][bi] = [ev]
                    qats_ready = ev
                    for i in range(4):
                        bi, bank, o = fm_chunk(Wb, 512 + i * 128, 128, hTs, NSAMP, wdeps)
                        ev = cp("dve", KATs[:, i, 512:512 + NSAMP], bank[:, 0:NSAMP], deps=[o])
                        st["mpfree"][bi] = [ev]
                    kats_new = ev
                    chk("b1")
                    for i in range(4):
                        bi, bank, o = fm_chunk(Wb, 1536 + i * 128, 128, hTs, NSAMP, wdeps)
                        store_fm_heads(bank, bi, o, Qs, i, NQ, NSAMP)
                    for i in range(4):
                        bi, bank, o = fm_chunk(Wb, 2048 + i * 128, 128, hTs, NSAMP, wdeps)
                        store_fm_heads(bank, bi, o, Kss, i, LS, NSAMP)
                    chk("b2")
                    vas_new = None
                    for (c0, dst32, dst16) in ((512, aks[496:512, :], None), (1024, avs[496:512, :], "va"),
                                               (2048, bks, None), (2560, bvs, "vb")):
                        bi, bank, o = tm_group(Wb, c0, 512, hTs, 0, NSAMP, wdeps)
                        s = st["t32i"] % NT32
                        st["t32i"] += 1
                        ev = cp("dve", tm32[s][0:NSAMP, :], bank[0:NSAMP, :], deps=[o] + _aslist(st["t32last"][s]))
                        st["t32last"][s] = dma("pool", sem_t32[s], dst32, tm32[s][0:NSAMP, :], deps=[ev])
                        if dst16 == "va":
                            vas_new = cp("dve", VAs[0:NSAMP, 4, :, 0:64], tm32[s][0:NSAMP, :].rearrange("p (h d) -> p h d", h=8),
                                         deps=[ev, c_vas])
                            st["t32last"][s] = _last2(st["t32last"][s], vas_new)
                            st["mpfree"][bi] = [ev]
                        elif dst16 == "vb":
                            s2 = st["t16i"] % NT16
                            st["t16i"] += 1
                            ev2 = cp("pool", tm16[s2][0:NSAMP, :], tm32[s][0:NSAMP, :], deps=[ev, st["t16last"][s2]])
                            st["t16last"][s2] = dma("pool", sem_t16[s2], Vss[LS:LS + NSAMP, :], tm16[s2][0:NSAMP, :], deps=[ev2])
                            st["t32last"][s] = _last2(st["t32last"][s], ev2)
                            st["mpfree"][bi] = [ev]
                        else:
                            st["mpfree"][bi] = [ev]
                        chk("b3_%d" % c0)
                    chk("b")
                    cfree = [None, None]
                    c16free = [None, None]
                    kk = 0

                    def load_cast(src):
                        nonlocal kk
                        s = kk % 2
                        kk += 1
                        ld = dma("sp", sem_c[s], cst[s][:], src, deps=[cfree[s]])
                        return s, ld

                    kats_c = None
                    vas_c = None
                    for kt in range(4):
                        s, ld = load_cast(cak[kt * 128:(kt + 1) * 128, :])
                        cc = cp("dve", c16[s][:], cst[s][:], deps=[ld, c16free[s]])
                        cfree[s] = cc
                        b = kt % 2
                        t_ = None
                        for i in range(4):
                            t_ = tr(tp[b][:, i, :], c16[s][:, i * 128:(i + 1) * 128], ident[:], deps=[cc, st["tpfree"][0], c_id])
                        c16free[s] = t_
                        kats_c = act(KATs[:, :, kt * 128:(kt + 1) * 128], tp[b][:, 0:4, :], AF.Copy, deps=[t_])
                        st["tpfree"][0] = kats_c
                        s, ld = load_cast(cav[kt * 128:(kt + 1) * 128, :])
                        vas_c = cp("dve", VAs[:, kt, :, 0:64], cst[s][:].rearrange("p (h d) -> p h d", h=8), deps=[ld, c_vas])
                        cfree[s] = vas_c
                    kstlast = None
                    for g in range(LS // 512):
                        evs = []
                        for q in range(4):
                            kt = g * 4 + q
                            s, ld = load_cast(cbk[kt * 128:(kt + 1) * 128, :])
                            cc = cp("dve", c16[s][:], cst[s][:], deps=[ld, c16free[s]])
                            cfree[s] = cc
                            b = kt % 2
                            t_ = None
                            for i in range(4):
                                t_ = tr(tp[b][:, i, :], c16[s][:, i * 128:(i + 1) * 128], ident[:], deps=[cc, st["tpfree"][0], c_id])
                            c16free[s] = t_
                            ev = act(kst[:, :, q * 128:(q + 1) * 128], tp[b][:, 0:4, :], AF.Copy, deps=[t_, kstlast])
                            st["tpfree"][0] = ev
                            evs.append(ev)
                        d_ = None
                        for hh in range(8):
                            d_ = dma("pool", sem_kst, Kss[hh, 0:64, g * 512:(g + 1) * 512],
                                     kst[(hh % 2) * 64:(hh % 2) * 64 + 64, hh // 2, :], deps=evs)
                        kstlast = d_
                    for g in range(8):
                        r0, r1_ = g * (LS // 8), (g + 1) * (LS // 8)
                        dma("pool", sem_vc, Vss[r0:r1_, :], cbv[r0:r1_, :])
                    chk("c")
                    limcl = [None, None]
                    limclast = None
                    gprev = None
                    n1 = ts("dve", lc[:], lc[:], -1.0, None, ALU.mult, deps=[lcl])
                    for g in range(LS // CH2):
                        limc = limc2[g % 2]
                        limclast = limcl[g % 2]
                        lcg = lc[:, g * CH2:(g + 1) * CH2]
                        n1d = [n1]
                        init = 0.0
                        if gprev is not None:
                            cz = cp("dve", r8c[:, 0:1], Gc[:, CH2 - 1:CH2], deps=[gprev])
                            init = r8c[:, 0:1]
                            n1d = [n1, cz]
                        sc_ = P.op("dve", lambda e, init=init, lcg=lcg: e.tensor_tensor_scan(out=Gc[:], data0=on8c[:], data1=lcg,
                                                                                            initial=init, op0=ALU.mult, op1=ALU.add),
                                   n1d + [c_on8c, limclast])
                        d1 = ts("dve", limc[:, 1, :], Gc[:], 8.0, None, ALU.mult, deps=[sc_, limclast, c_limc])
                        d2 = stt(r8c[:], Gc[:], 8.0, limc[:, 1, :], ALU.mult, ALU.subtract, deps=[d1])
                        d3 = cp("dve", limc[:, 2, :], r8c[:], deps=[d2])
                        d4 = tt("dve", limc[:, 3, :], r8c[:], limc[:, 2, :], ALU.subtract, deps=[d3])
                        limcl[g % 2] = dma("pool", sem_limc2[g % 2], Kss[:, 64:68, g * CH2:(g + 1) * CH2], limc[:], deps=[d4])
                        gprev = d4
                    chk("d")
                    bi, bank, o = fm_chunk(Wb, 3072, 128, hTs, NSAMP, wdeps)
                    cz = cp("dve", G8[1][:, 511:512], Gc[:, CH2 - 1:CH2], deps=[gprev])
                    st["Gprev"] = (G8[1][:, 511:512], cz)
                    st["Gi"] = 0
                    fl = fpath(bank, bi, o, NSAMP, LS, Kss, NQ, True, bls)
                    st["Gprev"] = None
                    st["Gi"] = 0
                    chk("e")
                    st["Kready"] = [kats_c, kats_new]
                    st["Qready"] = [qats_ready]
                    st["Vready"] = [vas_c, vas_new]
                    for hp in range(4):
                        heads = (2 * hp, 2 * hp + 1)
                        tiles_of = []
                        for h in heads:
                            tiles = []
                            for kt in range(5):
                                nk = 128 if kt < 4 else NSAMP
                                tiles.append((KATs[:, hp, kt * 128:kt * 128 + nk], VAs[0:nk, kt, h, :], nk, 0, NSAMP,
                                              EB[0:nk, h, 4 - kt, 0:NSAMP], False))
                            tiles_of.append(tiles)
                        band2(heads, [QATs[:, heads[0], :], QATs[:, heads[1], :]], tiles_of, NSAMP, NQ, False)
                    st["pend_norm"]()
                    st["pend_norm"] = None
                    P.finalize()
                    if stop == "p1s":
                        raise _Stop(nc)
                st["Kready"] = []
                st["Qready"] = []
                st["Vready"] = []
                st["mpfree"] = [[], [], []]
                for k_ in ("xfree", "hbfree", "tpfree", "fmlast", "t32last", "t16last", "rsfree", "sTfree",
                           "yalast", "osfree"):
                    st[k_] = [None] * len(st[k_])
                st["oTfree"] = [[], []]
                st["pbfree"] = [[None, None], [None, None]]
                st["ptfree"] = [[None, None], [None, None]]
                st["rdfree"] = [None, None]
                for k_ in ("limlast", "qstlast", "lflast", "elfree", "r8free"):
                    st[k_] = None

            hT = [sb("hT%d" % i, [128, DC, 512], BF16) for i in range(2)]
            QAT = sb("QATz", [128, 8, 512], BF16)
            c_qzp = memset("pool", QAT[:], 0.0)
            KAT = sb("KAT", [128, 4, 1024], BF16)
            VA = sb("VA", [128, 8, 8, 65], BF16)
            c_va = memset("pool", VA[:, :, :, 64:65], 1.0)
            def xsrc(p, t):
                return xw[p * 512 + t * 128:p * 512 + (t + 1) * 128, :]

            hnds = {}
            cps_of = {}

            def emit_a(p, t):
                if p < NSLOT:
                    hnds[(p, t)] = norm_a(xsrc(p, t), 128)

            def emit_b(p, t):
                if p < NSLOT:
                    co = norm_b(hnds.pop((p, t)), hT[p % 2][:, :, t * 128:(t + 1) * 128], [st["hTfree"][p % 2]])
                    cps_of.setdefault(p, []).append(co)

            for t in range(4):
                emit_a(0, t) if t < 2 else None
            emit_b(0, 0)
            emit_b(0, 1)
            emit_a(0, 2)
            emit_a(0, 3)
            emit_b(0, 2)
            emit_b(0, 3)

            WD = {}

            def make_slot(p):
                own = (p % 4 == 3)
                ctx = (p % 4 == 2)
                m = p // 4
                hTp = hT[p % 2]
                lm = {"o": None}
                groups = []

                def g_kb(i):
                    bi, bank, o = fm_chunk(Wb, 2048 + i * 128, 128, hTp, 512, WD[p])
                    store_fm_heads(bank, bi, o, Ks, i, p * 512, 512)
                    lm["o"] = o

                def g_f():
                    bi, bank, o = fm_chunk(Wb, 3072, 128, hTp, 512, WD[p])
                    fpath(bank, bi, o, 512, p * 512, Ks, m * 512, own, blf_own[m * 512:(m + 1) * 512, :] if own else None)
                    lm["o"] = o

                def g_ka(i):
                    kcol = 512 if own else 0
                    bi, bank, o = fm_chunk(Wb, 512 + i * 128, 128, hTp, 512, WD[p])
                    ev = cp("dve", KAT[:, i, kcol:kcol + 512], bank[:, :], deps=[o] + _aslist(st["KATfree"]))
                    st["mpfree"][bi] = [ev]
                    st["Kready"] = [ev]
                    lm["o"] = o

                def g_qa(i):
                    bi, bank, o = fm_chunk(Wb, 0 + i * 128, 128, hTp, 512, WD[p])
                    ev0 = cp("dve", QAT[0:64, 2 * i, :], bank[0:64, :], deps=[o, c_qzp] + _aslist(st["QATfree"]))
                    ev = cp("dve", QAT[64:128, 2 * i + 1, :], bank[64:128, :], deps=[o, c_qzp] + _aslist(st["QATfree"]))
                    st["mpfree"][bi] = [ev]
                    st["Qready"] = [ev]
                    lm["o"] = o

                def g_qb(i):
                    bi, bank, o = fm_chunk(Wb, 1536 + i * 128, 128, hTp, 512, WD[p])
                    store_fm_heads(bank, bi, o, Qs, i, m * 512, 512)
                    lm["o"] = o

                def g_vb(t):
                    tok0 = p * 512 + t * 128
                    otok0 = m * 512 + t * 128
                    bi, bank, o = tm_group(Wb, 2560, 512, hTp, t * 128, 128, WD[p])
                    lm["o"] = o
                    s2 = st["t16i"] % NT16
                    st["t16i"] += 1
                    if own:
                        s = st["t32i"] % NT32
                        st["t32i"] += 1
                        ev = cp("dve", tm32[s][:], bank[:, :], deps=[o] + _aslist(st["t32last"][s]))
                        d32 = dma("pool", sem_t32[s], bv_own[otok0:otok0 + 128, :], tm32[s][:], deps=[ev])
                        ev2 = cp("pool", tm16[s2][:], tm32[s][:], deps=[ev, st["t16last"][s2]])
                        st["t32last"][s] = [d32, ev2]
                        st["mpfree"][bi] = [ev]
                    else:
                        ev2 = cp("act", tm16[s2][:], bank[:, :], deps=[o, st["t16last"][s2]])
                        st["mpfree"][bi] = [ev2]
                    st["t16last"][s2] = dma("pool", sem_t16[s2], Vs[tok0:tok0 + 128, :], tm16[s2][:], deps=[ev2])

                def g_kbt(t):
                    otok0 = m * 512 + t * 128
                    bi, bank, o = tm_group(Wb, 2048, 512, hTp, t * 128, 128, WD[p])
                    lm["o"] = o
                    s = st["t32i"] % NT32
                    st["t32i"] += 1
                    ev = cp("dve", tm32[s][:], bank[:, :], deps=[o] + _aslist(st["t32last"][s]))
                    st["t32last"][s] = dma("pool", sem_t32[s], bk_own[otok0:otok0 + 128, :], tm32[s][:], deps=[ev])
                    st["mpfree"][bi] = [ev]

                def g_va(t):
                    kt = (4 if own else 0) + t
                    bi, bank, o = tm_group(Wb, 1024, 512, hTp, t * 128, 128, WD[p])
                    lm["o"] = o
                    ev2 = cp("dve", VA[:, kt, :, 0:64], bank[:, :].rearrange("p (h d) -> p h d", h=8),
                             deps=[o, c_va] + _aslist(st["VAfree"]))
                    st["Vready"] = [ev2]
                    st["mpfree"][bi] = [ev2]
                    if own and m == NOWN - 1:
                        s = st["t32i"] % NT32
                        st["t32i"] += 1
                        ev = cp("dve", tm32[s][:], bank[:, :], deps=[o] + _aslist(st["t32last"][s]))
                        st["t32last"][s] = dma("pool", sem_t32[s], avp[t * 128:(t + 1) * 128, :], tm32[s][:], deps=[ev])
                        st["mpfree"][bi] = [ev, ev2]

                def g_kat(t):
                    bi, bank, o = tm_group(Wb, 512, 512, hTp, t * 128, 128, WD[p])
                    lm["o"] = o
                    s = st["t32i"] % NT32
                    st["t32i"] += 1
                    ev = cp("dve", tm32[s][:], bank[:, :], deps=[o] + _aslist(st["t32last"][s]))
                    st["t32last"][s] = dma("pool", sem_t32[s], akp[t * 128:(t + 1) * 128, :], tm32[s][:], deps=[ev])
                    st["mpfree"][bi] = [ev]

                def g_band(hp, filler=None):
                    heads = (2 * hp, 2 * hp + 1)
                    tiles_of = []
                    for h in heads:
                        tiles = []
                        for kt in range(8):
                            q0 = max(0, kt - 4)
                            q1 = min(3, kt)
                            d0 = 4 + q0 - kt
                            n = (q1 - q0 + 1) * 128
                            tiles.append((KAT[:, hp, kt * 128:(kt + 1) * 128], VA[:, kt, h, :], 128, q0 * 128, n,
                                          EB[:, h, d0:d0 + (q1 - q0 + 1), :], kt < 4))
                        tiles_of.append(tiles)
                    pvl = band2(heads, [QAT[:, heads[0], :], QAT[:, heads[1], :]], tiles_of, 512, m * 512, m == 0, filler)
                    if hp == 3:
                        st["KATfree"] = pvl
                        st["VAfree"] = pvl
                        st["QATfree"] = pvl

                def mk(f, a_):
                    return lambda: f(a_)

                for i in range(4):
                    groups.append(mk(g_kb, i))
                groups.append(g_f)
                if own or ctx:
                    for i in range(4):
                        groups.append(mk(g_ka, i))
                if own:
                    for i in range(4):
                        groups.append(mk(g_qa, i))
                    for i in range(4):
                        groups.append(mk(g_qb, i))
                for t in range(4):
                    groups.append(mk(g_vb, t))
                    if own:
                        groups.append(mk(g_kbt, t))
                    if own or ctx:
                        groups.append(mk(g_va, t))
                    if own and m == NOWN - 1:
                        groups.append(mk(g_kat, t))
                nproj = len(groups)
                ng = nproj
                a_at = {0: [0, 1], max(1, ng // 4): [2], max(2, ng // 2): [3]}
                b_at = {max(1, ng // 8): 0, max(2, (3 * ng) // 8): 1, max(3, (5 * ng) // 8): 2, max(4, (7 * ng) // 8): 3}
                done_a, done_b = set(), set()
                units = []

                def unit(gi, g):
                    def run():
                        if gi == 0:
                            WD[p] = cps_of.pop(p) + W_ready
                        for t in a_at.get(gi, []):
                            emit_a(p + 1, t)
                            done_a.add(t)
                        if gi in b_at:
                            t = b_at[gi]
                            if t in done_a:
                                emit_b(p + 1, t)
                                done_b.add(t)
                        g()
                        if gi == 2 and st["pend_norm"] is not None and not own:
                            st["pend_norm"]()
                            st["pend_norm"] = None
                        if gi == nproj - 1:
                            st["hTfree"][p % 2] = lm["o"]
                            for t in range(4):
                                if t not in done_a:
                                    emit_a(p + 1, t)
                                if t not in done_b:
                                    emit_b(p + 1, t)
                    return run

                for gi, g in enumerate(groups):
                    units.append(("plain", unit(gi, g), p))
                if own:
                    for hp in range(4):
                        units.append(("band", (lambda f, hp=hp: g_band(hp, f)), p))
                return units

            seq = []
            for p in range(NSLOT):
                seq.extend(make_slot(p))
            pos = [0]
            while pos[0] < len(seq):
                kind, fn, slot = seq[pos[0]]
                pos[0] += 1
                if kind == "band":
                    def filler(slot=slot):
                        i = pos[0]
                        while i < len(seq) and seq[i][0] == "band":
                            i += 1
                        if i < len(seq) and seq[i][2] % 4 in (0, 1) and seq[i][2] <= slot + 2:
                            u = seq.pop(i)
                            u[1]()
                    fn(filler)
                else:
                    fn()
                    if st["pend_norm"] is not None:
                        st["pend_norm"]()
                        st["pend_norm"] = None
            if st["pend_norm"] is not None:
                st["pend_norm"]()
                st["pend_norm"] = None
            P.finalize()
            if stop == "p1":
                raise _Stop(nc)

        ph = ExitStack()
        with ph:
            def sb(name, shape, dt):
                return ph.enter_context(nc.sbuf_tensor(name, shape, dt))

            def ps(name, shape, dt):
                return ph.enter_context(nc.psum_tensor(name, shape, dt))

            NKT = L // 128
            Kb = [sb("Kb%d" % i, [68, L], BF16) for i in range(2)]
            Vb = [sb("Vb%d" % i, [128, NKT, 65], BF16) for i in range(2)]
            Qb = [sb("Qb%d" % i, [68, NQA], BF16) for i in range(2)]
            sem_kc = [[newsem(ph) for _ in range(4)] for _ in range(2)]
            sem_ks = [newsem(ph), newsem(ph)]
            if sample:
                Ksb = [sb("Ksb%d" % i, [68, LSP], BF16) for i in range(2)]
                Vsb = [sb("Vsb%d" % i, [128, LSP // 128, 65], BF16) for i in range(2)]
            NPB = 4
            pb2 = [sb("pb2_%d" % i, [128, 512], BF16) for i in range(NPB)]
            osb2 = sb("osb2", [65, 512], F32)
            rd2 = sb("rd2", [65, 512], F32)
            yb2 = [sb("yb2_%d" % i, [64, 512], BF16) for i in range(2)]
            sem_yb = [newsem(ph), newsem(ph)]
            sT2 = [ps("sT2_%d" % i, [128, 512], F32) for i in range(NPB)]
            oT2 = [ps("oT2_%d" % i, [128, 512], F32) for i in range(2)]
            bc2 = ps("bc2", [128, 512], F32)

            wq32 = [sb("wq32_%d" % i, [128, C3], F32) for i in range(3)]
            wq16 = [sb("wq16_%d" % i, [128, C3], BF16) for i in range(3)]
            sem_wq = [newsem(ph) for _ in range(3)]
            sem_wqs = [newsem(ph) for _ in range(3)]
            wq = dict(i=0, free32=[None] * 3, last16=[None] * 3, pend=[])
            wtasks = [(w3, W3s, C3, dc, True) for dc in range(DC)] + [(wbr, Wbrs, D, dc, False) for dc in range(DC)] + \
                     [(wout, Wouts, D, dc, False) for dc in range(DC)]

            def wtask_load():
                if not wtasks:
                    return
                src, dst, W_, dc, scaled = wtasks.pop(0)
                k_ = wq["i"] % 3
                wq["i"] += 1
                ld = dma("sp", sem_wq[k_], wq32[k_][:, 0:W_], src[dc * 128:(dc + 1) * 128, :], deps=[wq["free32"][k_]])
                wq["pend"].append((k_, ld, dst, W_, dc, scaled))

            def wtask_convert():
                if not wq["pend"]:
                    return
                k_, ld, dst, W_, dc, scaled = wq["pend"].pop(0)
                if scaled:
                    cv = ts("dve", wq16[k_][:, 0:W_], wq32[k_][:, 0:W_], gpre_sb[:, dc:dc + 1], None, ALU.mult,
                            deps=[ld, wq["last16"][k_]])
                else:
                    cv = cp("dve", wq16[k_][:, 0:W_], wq32[k_][:, 0:W_], deps=[ld, wq["last16"][k_]])
                wq["free32"][k_] = cv
                wq["last16"][k_] = dma("pool", sem_wqs[k_], dst[dc * 128:(dc + 1) * 128, :], wq16[k_][:, 0:W_], deps=[cv])

            for _ in range(3):
                wtask_load()

            c_v = [memset("pool", Vb[i][:, :, 64:65], 1.0) for i in range(2)]
            if sample:
                c_vs = [memset("pool", Vsb[i][:, :, 64:65], 1.0) for i in range(2)]
            bufree = [None, None]
            s2 = dict(sTfree=[None] * NPB, pbfree=[None] * NPB, oTfree=[None, None], oTfree_b=[None, None], pend=None,
                      ji=0, oi=0, rdfree=None, osfree=None, bcfree=None, yi=0, ylast=[None, None])

            for h in range(8):
                par = h % 2
                fdep = [bufree[par]]
                NCH = 4
                TPC = NKT // NCH
                chunk_ld = []
                for c_ in range(NCH):
                    k0, k1 = c_ * TPC, (c_ + 1) * TPC
                    l_ = None
                    if c_ == 0:
                        l_ = dma("sp", sem_kc[par][c_], Qb[par][:], Qs[h], deps=fdep)
                    l_ = dma("sp", sem_kc[par][c_], Kb[par][:, k0 * 128:k1 * 128], Ks[h, :, k0 * 128:k1 * 128], deps=fdep)
                    for g in range(k0, k1, 16):
                        g1 = min(k1, g + 16)
                        l_ = dma("sp", sem_kc[par][c_], Vb[par][:, g:g1, 0:64],
                                 Vs[g * 128:g1 * 128, h * 64:(h + 1) * 64].rearrange("(k p) d -> p k d", p=128), deps=fdep)
                    chunk_ld.append(l_)
                samp_ld = None
                if sample:
                    samp_ld = dma("sp", sem_ks[par], Ksb[par][:, 0:LS + NSAMP], Kss[h, :, 0:LS + NSAMP], deps=fdep)
                    for g in range(0, LS // 128, 16):
                        g1 = min(LS // 128, g + 16)
                        samp_ld = dma("sp", sem_ks[par], Vsb[par][:, g:g1, 0:64],
                                      Vss[g * 128:g1 * 128, h * 64:(h + 1) * 64].rearrange("(k p) d -> p k d", p=128),
                                      deps=fdep)
                    samp_ld = dma("sp", sem_ks[par], Vsb[par][0:NSAMP, LS // 128, 0:64],
                                  Vss[LS:LS + NSAMP, h * 64:(h + 1) * 64], deps=fdep)
                if h > 0:
                    for _ in range(3):
                        wtask_convert()
                    for _ in range(3):
                        wtask_load()
                def kdeps(kt):
                    c_ = kt // TPC
                    return [chunk_ld[0], c_v[par]] + ([chunk_ld[c_]] if c_ > 0 else [])
                sdeps_ = [chunk_ld[0], samp_ld, c_vs[par]] if sample else []

                groups = []
                for m in range(NOWN):
                    jobs = []
                    nfull = (4 * m + 3) * 4
                    for kt in range(nfull):
                        bias = smask[:, kt // 4:kt // 4 + 1] if kt < 12 else None
                        jobs.append((Kb[par][:, kt * 128:(kt + 1) * 128], Qb[par][:, m * 512:(m + 1) * 512],
                                     Vb[par][:, kt, :], 128, 0, 512, bias, False, kdeps(kt)))
                    for b in range(4):
                        kt = nfull + b
                        jobs.append((Kb[par][:, kt * 128:(kt + 1) * 128], Qb[par][:, m * 512 + b * 128:(m + 1) * 512],
                                     Vb[par][:, kt, :], 128, b * 128, 512 - b * 128, None, True, kdeps(kt)))
                    groups.append((jobs, 512, m * 512))
                if sample:
                    jobs = []
                    for kt in range(LS // 128):
                        jobs.append((Ksb[par][:, kt * 128:(kt + 1) * 128], Qb[par][:, NQ:NQ + NSAMP],
                                     Vsb[par][:, kt, :], 128, 0, NSAMP, None, False, sdeps_))
                    kt = LS // 128
                    jobs.append((Ksb[par][:, LS:LS + NSAMP], Qb[par][:, NQ:NQ + NSAMP],
                                 Vsb[par][0:NSAMP, kt, :], NSAMP, 0, NSAMP, None, True, sdeps_))
                    groups.append((jobs, NSAMP, NQ))

                flat = []
                for gi, (jobs, nq, ycol) in enumerate(groups):
                    for ji, jb in enumerate(jobs):
                        flat.append((gi, ji, len(jobs), jb))
                nflat = len(flat)
                LOOK = 3
                qk_pend = {}

                def emit_qk(fi):
                    gi, ji, nj, (Kap, Qap, Vap, nk, c0, n, bias, trif, jd) = flat[fi]
                    b = s2["ji"] % NPB
                    s2["ji"] += 1
                    o = mm(sT2[b][0:nk, 0:n], Kap, Qap, True, True, deps=jd + [s2["sTfree"][b]])
                    qk_pend[fi] = (b, o)

                for fi in range(min(LOOK, nflat)):
                    emit_qk(fi)
                cur_o = None
                last_pv = None
                for fi in range(nflat):
                    gi, ji, nj, (Kap, Qap, Vap, nk, c0, n, bias, trif, jd) = flat[fi]
                    if fi + LOOK < nflat:
                        emit_qk(fi + LOOK)
                    b, qk = qk_pend.pop(fi)
                    if ji == 0:
                        cur_o = s2["oi"] % 2
                        s2["oi"] += 1
                    kw = {"scale": SCALE}
                    if bias is not None:
                        kw["bias"] = bias
                    ex = act(pb2[b][0:nk, 0:n], sT2[b][0:nk, 0:n], AF.Exp, deps=[qk, s2["pbfree"][b], l_bf], **kw)
                    s2["sTfree"][b] = ex
                    pdep = ex
                    if trif:
                        w_ = min(128, n)
                        pdep = tt("pool", pb2[b][0:nk, 0:w_], pb2[b][0:nk, 0:w_], tri[0:nk, 0:w_], ALU.mult, deps=[ex, c_tri])
                    pv = mm(oT2[cur_o][0:65, c0:c0 + n], Vap, pb2[b][0:nk, 0:n], ji == 0, ji == nj - 1,
                            deps=[pdep, s2["oTfree"][cur_o], s2["oTfree_b"][cur_o]] + jd, skip=True)
                    s2["pbfree"][b] = pv
                    last_pv = pv
                    if ji == nj - 1:
                        jobs, nq, ycol = groups[gi]
                        o_ = oT2[cur_o]
                        c1 = act(osb2[0:65, 0:nq], o_[0:65, 0:nq], AF.Copy, deps=[pv, s2["osfree"]])
                        s2["oTfree"][cur_o] = c1
                        r1 = P.op("dve", lambda e, nq=nq: e.reciprocal(out=rd2[64:65, 0:nq], in_=osb2[64:65, 0:nq]),
                                  [c1, s2["rdfree"]])

                        def finish(r1=r1, c1=c1, nq=nq, ycol=ycol, h=h):
                            bc = mm(bc2[0:64, 0:nq], onesf[64:65, 0:64], rd2[64:65, 0:nq], True, True,
                                    deps=[r1, s2["bcfree"]])
                            s2["rdfree"] = bc
                            yi = s2["yi"] % 2
                            s2["yi"] += 1
                            y1 = tt("dve", yb2[yi][:, 0:nq], osb2[0:64, 0:nq], bc2[0:64, 0:nq], ALU.mult,
                                    deps=[c1, bc, s2["ylast"][yi]])
                            s2["bcfree"] = y1
                            s2["osfree"] = y1
                            s2["ylast"][yi] = dma("pool", sem_yb[yi], YB[h * 64:(h + 1) * 64, ycol:ycol + nq], yb2[yi][:, 0:nq],
                                                  deps=[y1])
                        s2["pend"] = [finish, 6]
                    elif s2["pend"] is not None:
                        s2["pend"][1] -= 1
                        if s2["pend"][1] <= 0:
                            s2["pend"][0]()
                            s2["pend"] = None
                if s2["pend"] is not None:
                    s2["pend"][0]()
                    s2["pend"] = None
                bufree[par] = last_pv
            while wq["pend"] or wtasks:
                for _ in range(3):
                    wtask_convert()
                for _ in range(3):
                    wtask_load()
            P.finalize()
            if stop == "p2":
                raise _Stop(nc)

        ph = ExitStack()
        with ph:
            def sb(name, shape, dt):
                return ph.enter_context(nc.sbuf_tensor(name, shape, dt))

            def ps(name, shape, dt):
                return ph.enter_context(nc.psum_tensor(name, shape, dt))

            W3b = sb("W3b", [128, DC, C3], BF16)
            Wbrb = sb("Wbrb", [128, DC, D], BF16)
            Woutb = sb("Woutb", [128, DC, D], BF16)
            gpb = sb("gpb", [128, D], F32)
            xs4 = [sb("xs4_%d" % i, [128, 4, D], F32) for i in range(2)]
            junk3 = sb("junk3", [128, D], BF16)
            ssq3 = sb("ssq3", [128, 8], F32)
            rsd3 = sb("rsd3", [128, 8], F32)
            hb3 = [sb("hb3_%d" % i, [128, D], BF16) for i in range(2)]
            hT3 = [sb("hT3_%d" % i, [128, DC, 512], BF16) for i in range(2)]
            GZ = sb("GZ", [128, 8, 512], BF16)
            GM = sb("GM", [128, 16, 512], BF16)
            yl = sb("yl", [128, 8, 512], BF16)
            gmul = sb("gmul", [128, 8, 512], BF16)
            mrg = sb("mrg", [128, DC, 512], BF16)
            t1 = [sb("t1_%d" % i, [128, 512], F32) for i in range(2)]
            t2 = [sb("t2_%d" % i, [128, 512], F32) for i in range(2)]
            osb3 = [sb("osb3_%d" % i, [128, D], F32) for i in range(2)]
            ss2 = sb("ss2", [128, 4], F32)
            rs2 = sb("rs2", [128, 2], F32)
            sem_m3 = newsem(ph)
            sem_w3 = [newsem(ph) for _ in range(8)]
            sem_x3 = [newsem(ph), newsem(ph)]
            sem_y3 = newsem(ph)
            sem_o3 = [newsem(ph), newsem(ph)]
            tp3 = [ps("tp3_%d" % i, [128, DC, 128], BF16) for i in range(2)]
            NM3 = 6
            mp3 = [ps("mp3_%d" % i, [128, 512], F32) for i in range(NM3)]

            mhalf = sb("mhalf", [128, 1], F32)
            c_mh = memset("pool", mhalf[:], -0.5)
            l_gpb = dma("sp", sem_m3, gpb[:], gpost.partition_broadcast(128))
            w3ld = []
            for cb in range(6):
                w3ld.append(dma("sp", sem_w3[cb], W3b[:, :, cb * 512:(cb + 1) * 512],
                                W3s[:, cb * 512:(cb + 1) * 512].rearrange("(c p) n -> p c n", p=128)))
            wbrld = dma("sp", sem_w3[6], Wbrb[:], Wbrs.rearrange("(c p) n -> p c n", p=128))
            woutld = dma("sp", sem_w3[7], Woutb[:], Wouts.rearrange("(c p) n -> p c n", p=128))

            s3 = dict(mpi=0, mpfree=[[] for _ in range(NM3)], junkfree=None, ssfree=None, rs2free=None, hbfree=[None, None],
                      tpfree=[None, None], xfree=[None, None], hTfree=[None, None],
                      gzfree=None, gmfree=None, ylfree=None, gmulfree=None, mrgfree=None, t1free=[None, None],
                      t2free=[None, None], oi=0, olast=[None, None], rsfree=[None] * 8, hi=0)

            def mm3(pairs, M, N, deps):
                bi = s3["mpi"] % NM3
                s3["mpi"] += 1
                bank = mp3[bi]
                o = None
                for kk, (l_, r_) in enumerate(pairs):
                    o = mm(bank[0:M, 0:N], l_, r_, kk == 0, kk == len(pairs) - 1, deps=list(deps) + s3["mpfree"][bi])
                return bi, bank, o

            units = [(xw, (4 * m + 3) * 512, m * 512, 512, y_own, m * 512) for m in range(NOWN)]
            if sample:
                units.append((xs_d, 0, NQ, NSAMP, ys_o, 0))
            NU = len(units)
            xld = {}
            hnd3 = {}
            cps3 = {}

            def u_geom(u):
                xsrc, x0, ycol, ntok, ydst, ybase = units[u]
                return (ntok + 127) // 128, min(128, ntok)

            def p3_load(u):
                if u >= NU:
                    return
                xsrc, x0, ycol, ntok, ydst, ybase = units[u]
                ntile, tn = u_geom(u)
                ld = None
                for t in range(ntile):
                    ld = dma("sp", sem_x3[u % 2], xs4[u % 2][0:tn, t, :], xsrc[x0 + t * 128:x0 + t * 128 + tn, :],
                             deps=[s3["xfree"][u % 2]])
                xld[u] = ld

            def p3_a(u, t):
                if u >= NU:
                    return
                ntile, tn = u_geom(u)
                if t >= ntile:
                    return
                c = (u % 2) * 4 + t
                ld = xld[u]
                xin = xs4[u % 2][0:tn, t, :]
                sq = act(junk3[0:tn, :], xin, AF.Square, deps=[ld, s3["junkfree"]], accum_out=ssq3[0:tn, c:c + 1])
                s3["junkfree"] = sq
                sr = ts("pool", rsd3[0:tn, c:c + 1], ssq3[0:tn, c:c + 1], 1.0 / D, EPS, ALU.mult, ALU.add,
                        deps=[sq, s3["rsfree"][c]])
                rc = tt("pool", rsd3[0:tn, c:c + 1], rsd3[0:tn, c:c + 1], mhalf[0:tn, :], ALU.pow, deps=[sr, c_mh])
                b = s3["hi"] % 2
                s3["hi"] += 1
                sc = ts("dve", hb3[b][0:tn, :], xin, rsd3[0:tn, c:c + 1], None, ALU.mult, deps=[rc, s3["hbfree"][b]])
                s3["rsfree"][c] = sc
                hnd3[(u, t)] = (b, sc)

            def p3_b(u, t):
                if u >= NU:
                    return
                ntile, tn = u_geom(u)
                if t >= ntile:
                    return
                b, sc = hnd3.pop((u, t))
                t_ = None
                for dc in range(DC):
                    t_ = tr(tp3[b][:, dc, 0:tn], hb3[b][0:tn, dc * 128:(dc + 1) * 128], ident[0:tn, 0:tn],
                            deps=[sc, s3["tpfree"][b]])
                s3["hbfree"][b] = t_
                co = act(hT3[u % 2][:, :, t * 128:t * 128 + tn], tp3[b][:, :, 0:tn], AF.Copy,
                         deps=[t_, s3["hTfree"][u % 2]])
                s3["tpfree"][b] = co
                cps3.setdefault(u, []).append(co)

            p3_load(0)
            p3_load(1)
            for t in range(4):
                p3_a(0, t) if t < 2 else None
            p3_b(0, 0)
            p3_b(0, 1)
            p3_a(0, 2)
            p3_a(0, 3)
            p3_b(0, 2)
            p3_b(0, 3)

            for u in range(NU):
                xsrc, x0, ycol, ntok, ydst, ybase = units[u]
                ntile, tn = u_geom(u)
                N = ntok
                hTu = hT3[u % 2]
                cps = cps3.pop(u)
                dma("sp", sem_y3, yl[:, 0:4, 0:N], YA[:, ycol:ycol + N].rearrange("(c p) t -> p c t", p=128),
                    deps=[s3["ylfree"]])
                yld = dma("sp", sem_y3, yl[:, 4:8, 0:N], YB[:, ycol:ycol + N].rearrange("(c p) t -> p c t", p=128),
                          deps=[s3["ylfree"]])
                lastmm = None
                gz_last = None
                gm_last = None
                g1 = None
                for c in range(24):
                    pairs = [(W3b[:, dc, c * 128:(c + 1) * 128], hTu[:, dc, 0:N]) for dc in range(DC)]
                    bi, bank, o = mm3(pairs, 128, N, cps + [w3ld[c // 4]])
                    lastmm = o
                    if c < 8:
                        ev = act(GZ[:, c, 0:N], bank[:, 0:N], AF.Silu, deps=[o, s3["gzfree"]])
                        gz_last = ev
                    else:
                        ev = act(GM[:, c - 8, 0:N], bank[:, 0:N], AF.Sigmoid, deps=[o, s3["gmfree"]])
                        gm_last = ev
                    s3["mpfree"][bi] = [ev]
                    if c == 7:
                        g1 = tt("dve", gmul[:, :, 0:N], yl[:, :, 0:N], GZ[:, :, 0:N], ALU.mult,
                                deps=[yld, gz_last, s3["gmulfree"]])
                        s3["ylfree"] = g1
                        s3["gzfree"] = g1
                        p3_a(u + 1, 0)
                        p3_a(u + 1, 1)
                    if c == 15:
                        p3_b(u + 1, 0)
                        p3_b(u + 1, 1)
                        p3_a(u + 1, 2)
                        p3_a(u + 1, 3)
                s3["hTfree"][u % 2] = lastmm
                lastbr = None
                mr_ops = []
                for dc in range(DC):
                    pa = [(Wbrb[:, c, dc * 128:(dc + 1) * 128], gmul[:, c, 0:N]) for c in range(4)]
                    bia, banka, oa = mm3(pa, 128, N, [g1, wbrld])
                    pbb = [(Wbrb[:, 4 + c, dc * 128:(dc + 1) * 128], gmul[:, 4 + c, 0:N]) for c in range(4)]
                    bib, bankb, ob = mm3(pbb, 128, N, [g1, wbrld])
                    lastbr = ob
                    k2 = dc % 2
                    m1 = tt("dve", t1[k2][:, 0:N], banka[:, 0:N], GM[:, dc, 0:N], ALU.mult, deps=[oa, gm_last, s3["t1free"][k2]])
                    s3["mpfree"][bia] = [m1]
                    m2 = tt("dve", t2[k2][:, 0:N], bankb[:, 0:N], GM[:, 8 + dc, 0:N], ALU.mult, deps=[ob, gm_last, s3["t2free"][k2]])
                    s3["mpfree"][bib] = [m2]
                    m3_ = tt("pool", mrg[:, dc, 0:N], t1[k2][:, 0:N], t2[k2][:, 0:N], ALU.add, deps=[m1, m2, s3["mrgfree"]])
                    s3["t1free"][k2] = m3_
                    s3["t2free"][k2] = m3_
                    mr_ops.append(m3_)
                    if dc == 3:
                        p3_b(u + 1, 2)
                        p3_b(u + 1, 3)
                s3["gmulfree"] = lastbr
                s3["gmfree"] = mr_ops[-1]
                lastout = None
                fin = None
                for t in range(ntile):
                    oi = s3["oi"] % 2
                    s3["oi"] += 1
                    ob_ = osb3[oi]
                    sqs = []
                    for hf in range(2):
                        pairs = [(mrg[:, dc, t * 128:t * 128 + tn], Woutb[:, dc, hf * 512:(hf + 1) * 512]) for dc in range(DC)]
                        bi, bank, o = mm3(pairs, tn, 512, mr_ops + [woutld])
                        lastout = o
                        ev = act(ob_[0:tn, hf * 512:(hf + 1) * 512], bank[0:tn, :], AF.Copy, deps=[o, s3["olast"][oi]])
                        s3["mpfree"][bi] = [ev]
                        sq = act(junk3[0:tn, 0:512], ob_[0:tn, hf * 512:(hf + 1) * 512], AF.Square,
                                 deps=[ev, s3["ssfree"], s3["junkfree"]], accum_out=ss2[0:tn, hf:hf + 1])
                        s3["junkfree"] = sq
                        sqs.append(sq)
                    a_ = tt("dve", ss2[0:tn, 2:3], ss2[0:tn, 0:1], ss2[0:tn, 1:2], ALU.add, deps=sqs)
                    s3["ssfree"] = a_
                    sr = ts("pool", rs2[0:tn, 0:1], ss2[0:tn, 2:3], 1.0 / D, EPS, ALU.mult, ALU.add, deps=[a_, s3["rs2free"]])
                    rc = tt("pool", rs2[0:tn, 0:1], rs2[0:tn, 0:1], mhalf[0:tn, :], ALU.pow, deps=[sr, c_mh])
                    f1 = stt(ob_[0:tn, :], ob_[0:tn, :], rs2[0:tn, 0:1], gpb[0:tn, :], ALU.mult, ALU.mult, deps=[rc, l_gpb])
                    s3["rs2free"] = f1
                    f2 = tt("pool", ob_[0:tn, :], ob_[0:tn, :], xs4[u % 2][0:tn, t, :], ALU.add, deps=[f1])
                    s3["olast"][oi] = dma("pool", sem_o3[oi], ydst[ybase + t * 128:ybase + t * 128 + tn, :],
                                          ob_[0:tn, :], deps=[f2])
                    fin = f2
                s3["mrgfree"] = lastout
                s3["xfree"][u % 2] = fin
                p3_load(u + 2)
                for t in range(4):
                    if (u + 1, t) in hnd3:
                        p3_b(u + 1, t)
            P.finalize()
    return nc


_CACHE = {}


def _get_nc(NSLOT=32, sample=True):
    key = (NSLOT, sample)
    if key not in _CACHE:
        _CACHE[key] = build(NSLOT, sample)
    return _CACHE[key]


def make_in_maps(inputs, NSLOT=32):
    f32 = np.float32
    xp = np.asarray(inputs["x_prompt"], f32)
    B, S, _ = xp.shape
    L = NSLOT * 512
    w_in = np.asarray(inputs["w_in"], f32)[0]
    cols1 = np.r_[0:512, 512:1024, 1024:1536, 2048:2560, 2560:3072, 3072:3584, 4096:4104]
    cols3 = np.r_[1536:2048, 3584:4096, 4104:5128, 5128:6152]
    w1 = np.ascontiguousarray(w_in[:, cols1])
    w3 = np.ascontiguousarray(w_in[:, cols3])
    wbr = np.ascontiguousarray(np.concatenate([np.asarray(inputs["w_br_a"], f32)[0], np.asarray(inputs["w_br_b"], f32)[0]], 0))
    wout = np.ascontiguousarray(np.asarray(inputs["w_out"], f32)[0])
    maps = []
    for c in range(8):
        b, j = c // 4, c % 4
        start = (j - 3) * 512
        win = np.zeros((L, D), f32)
        lo = max(start, 0)
        hi = min(start + L, S)
        win[lo - start:hi - start] = xp[b, lo:hi]
        sm = np.zeros((128, 3), f32)
        for s_ in range(3):
            if s_ + j - 3 < 0:
                sm[:, s_] = -BIG
        maps.append({
            "xw": win, "w1": w1, "w3": w3, "wbr": wbr, "wout": wout,
            "gpre": np.ascontiguousarray(np.asarray(inputs["g_pre"], f32)[0]),
            "gpost": np.ascontiguousarray(np.asarray(inputs["g_post"], f32)[0]),
            "bfv": np.ascontiguousarray(np.asarray(inputs["b_f"], f32)[0].reshape(8, 1)),
            "rel": np.ascontiguousarray(np.asarray(inputs["rel_table"], f32)[0]),
            "smask": sm,
            "xs": np.ascontiguousarray(np.asarray(inputs["x_sample"], f32)[c]),
            "cak": np.ascontiguousarray(np.asarray(inputs["cache_a_k"], f32)[0, c].reshape(512, 512)),
            "cav": np.ascontiguousarray(np.asarray(inputs["cache_a_v"], f32)[0, c].reshape(512, 512)),
            "cbk": np.ascontiguousarray(np.asarray(inputs["cache_b_k"], f32)[0, c].reshape(LS, 512)),
            "cbv": np.ascontiguousarray(np.asarray(inputs["cache_b_v"], f32)[0, c].reshape(LS, 512)),
            "cbl": np.ascontiguousarray(np.asarray(inputs["cache_b_logf"], f32)[0, c].reshape(LS, 8)),
        })
    return maps


def assemble(results, B, S, NSLOT=32):
    f32 = np.float32
    NOWN = NSLOT // 4
    y = np.zeros((B, S, D), f32)
    bk = np.zeros((1, B, S, 8, 64), f32)
    bv = np.zeros((1, B, S, 8, 64), f32)
    blf = np.zeros((1, B, S, 8), f32)
    akp = np.zeros((1, B, 512, 8, 64), f32)
    avp = np.zeros((1, B, 512, 8, 64), f32)
    ys = np.zeros((8, NSAMP, D), f32)
    aks = np.zeros((1, 8, 512, 8, 64), f32)
    avs = np.zeros((1, 8, 512, 8, 64), f32)
    bks = np.zeros((1, 8, NSAMP, 8, 64), f32)
    bvs = np.zeros((1, 8, NSAMP, 8, 64), f32)
    bls = np.zeros((1, 8, NSAMP, 8), f32)
    for c in range(8):
        r = results[c]
        b, j = c // 4, c % 4
        for m in range(NOWN):
            s0 = (4 * m + j) * 512
            y[b, s0:s0 + 512] = r["y_own"][m * 512:(m + 1) * 512]
            bk[0, b, s0:s0 + 512] = r["bk_own"][m * 512:(m + 1) * 512].reshape(512, 8, 64)
            bv[0, b, s0:s0 + 512] = r["bv_own"][m * 512:(m + 1) * 512].reshape(512, 8, 64)
            blf[0, b, s0:s0 + 512] = r["blf_own"][m * 512:(m + 1) * 512]
        if j == 3:
            akp[0, b] = r["akp"].reshape(512, 8, 64)
            avp[0, b] = r["avp"].reshape(512, 8, 64)
        ys[c] = r["ys"]
        aks[0, c] = r["aks"].reshape(512, 8, 64)
        avs[0, c] = r["avs"].reshape(512, 8, 64)
        bks[0, c] = r["bks"].reshape(NSAMP, 8, 64)
        bvs[0, c] = r["bvs"].reshape(NSAMP, 8, 64)
        bls[0, c] = r["bls"]
    return (y, ys, akp, avp, bk, bv, blf, aks, avs, bks, bvs, bls)


def kernel(**inputs):
    nc = _get_nc(32, True)
    maps = make_in_maps(inputs, 32)
    res = run_bass_kernel_spmd(nc, maps, core_ids=list(range(8)))
    B, S, _ = np.asarray(inputs["x_prompt"]).shape
    return assemble(res.results, B, S, 32)
```

```python
import numpy as np
from contextlib import ExitStack
import concourse.bass as bass
import concourse.mybir as mybir
from concourse.bass_utils import run_bass_kernel_spmd

F32 = mybir.dt.float32
BF16 = mybir.dt.bfloat16
AF = mybir.ActivationFunctionType
ALU = mybir.AluOpType

D = 1024
DC = 8
C1 = 3080
NT16 = 4
C1P = 3200
C3 = 3072
SCALE = 0.125
BIG = 30000.0
EPS = 1e-6
NSAMP = 16
LS = 4096
LSP = 4224
CH = 512
CH2 = 512


class _Stop(Exception):
    pass


def _aslist(x):
    if x is None:
        return []
    return list(x) if isinstance(x, (list, tuple)) else [x]


def _last2(a, b):
    return _aslist(a) + [b]


class Op:
    __slots__ = ("eng", "fn", "deps", "sig", "val", "dsem", "dval")

    def __init__(self, eng, fn, deps, dsem=None, dval=0):
        self.eng = eng
        self.fn = fn
        self.deps = [d for d in deps if d is not None]
        self.sig = False
        self.val = 0
        self.dsem = dsem
        self.dval = dval


class Prog:
    ENGS = ("pe", "act", "dve", "pool", "sp")

    def __init__(self, nc, sems):
        self.nc = nc
        self.sems = sems
        self.lists = {e: [] for e in self.ENGS}
        self.dma_cnt = {}
        self.dma_sems = {}
        self.base = {e: 0 for e in self.ENGS}

    def op(self, eng, fn, deps=()):
        o = Op(eng, fn, deps)
        self.lists[eng].append(o)
        return o

    def dma(self, eng, sem, fn, deps=()):
        k = id(sem)
        self.dma_cnt[k] = self.dma_cnt.get(k, 0) + 16
        self.dma_sems[k] = sem
        o = Op(eng, fn, deps, dsem=sem, dval=self.dma_cnt[k])
        self.lists[eng].append(o)
        return o

    def finalize(self):
        sems = self.sems
        for e in self.ENGS:
            for o in self.lists[e]:
                for d in o.deps:
                    if d.dsem is None and not (d.eng == "pe" and e == "pe"):
                        d.sig = True
        for e in self.ENGS:
            c = self.base[e]
            for o in self.lists[e]:
                if o.dsem is None and o.sig:
                    c += 1
                o.val = c
            self.base[e] = c
        nc = self.nc
        lists = self.lists
        final = [(self.dma_sems[k], v) for k, v in self.dma_cnt.items()]
        with nc.Block() as block:
            def run(e, engine):
                waited = {}
                for o in lists[e]:
                    for d in o.deps:
                        if d.dsem is not None:
                            s, v = d.dsem, d.dval
                        else:
                            if d.eng == "pe" and e == "pe":
                                continue
                            s, v = sems[d.eng], d.val
                        key = id(s)
                        if waited.get(key, 0) >= v:
                            continue
                        waited[key] = v
                        engine.wait_ge(s, v)
                    ins = o.fn(engine)
                    if o.dsem is not None:
                        ins.then_inc(o.dsem, 16)
                    elif o.sig:
                        ins.then_inc(sems[e], 1)
                if e == "sp":
                    for (s, v) in final:
                        engine.wait_ge(s, v)

            @block.tensor
            def _(eng):
                run("pe", eng)

            @block.scalar
            def _(eng):
                run("act", eng)

            @block.vector
            def _(eng):
                run("dve", eng)

            @block.gpsimd
            def _(eng):
                run("pool", eng)

            @block.sync
            def _(eng):
                run("sp", eng)
        self.lists = {e: [] for e in self.ENGS}


def build(NSLOT=32, sample=True, stop=None):
    try:
        return _build(NSLOT, sample, stop)
    except _Stop as e:
        return e.args[0]


def _build(NSLOT=32, sample=True, stop=None):
    NOWN = NSLOT // 4
    L = NSLOT * 512
    NQ = NOWN * 512
    NQA = NQ + NSAMP
    nc = bass.Bass("TRN2", target_bir_lowering=False)

    def din(name, shape, dt=F32):
        return nc.dram_tensor(name, shape, dt, kind="ExternalInput").ap()

    def dout(name, shape, dt=F32):
        return nc.dram_tensor(name, shape, dt, kind="ExternalOutput").ap()

    def dscr(name, shape, dt):
        return nc.dram_tensor(name, shape, dt, kind="Internal").ap()

    xw = din("xw", [L, D])
    w1 = din("w1", [D, C1])
    w3 = din("w3", [D, C3])
    wbr = din("wbr", [D, D])
    wout = din("wout", [D, D])
    gpre = din("gpre", [D])
    gpost = din("gpost", [D])
    bfv = din("bfv", [8, 1])
    rel = din("rel", [8, 192])
    smask_d = din("smask", [128, 3])
    xs_d = din("xs", [NSAMP, D])
    cak = din("cak", [512, 512])
    cav = din("cav", [512, 512])
    cbk = din("cbk", [LS, 512])
    cbv = din("cbv", [LS, 512])
    cbl = din("cbl", [LS, 8])

    y_own = dout("y_own", [NQ, D])
    bk_own = dout("bk_own", [NQ, 512])
    bv_own = dout("bv_own", [NQ, 512])
    blf_own = dout("blf_own", [NQ, 8])
    akp = dout("akp", [512, 512])
    avp = dout("avp", [512, 512])
    ys_o = dout("ys", [NSAMP, D])
    aks = dout("aks", [512, 512])
    avs = dout("avs", [512, 512])
    bks = dout("bks", [NSAMP, 512])
    bvs = dout("bvs", [NSAMP, 512])
    bls = dout("bls", [NSAMP, 8])

    Ks = dscr("Ks", [8, 68, L], BF16)
    Vs = dscr("Vs", [L, 512], BF16)
    Qs = dscr("Qs", [8, 68, NQA], BF16)
    YA = dscr("YA", [512, NQA], BF16)
    YB = dscr("YB", [512, NQA], BF16)
    ext = dscr("ext", [8, 768], F32)
    Erep = dscr("Erep", [8, 128, 768], F32)
    W3s = dscr("W3s", [D, C3], BF16)
    Wbrs = dscr("Wbrs", [D, D], BF16)
    Wouts = dscr("Wouts", [D, D], BF16)
    Kss = dscr("Kss", [8, 68, LSP], BF16)
    Vss = dscr("Vss", [LSP, 512], BF16)

    outer = ExitStack()
    with outer:
        def osb(name, shape, dt):
            return outer.enter_context(nc.sbuf_tensor(name, shape, dt))

        sems = {e: outer.enter_context(nc.semaphore("s_" + e)) for e in ("pe", "act", "dve", "pool")}
        nsem = [0]

        def newsem(stack):
            nsem[0] += 1
            return outer.enter_context(nc.semaphore("d%d" % nsem[0]))

        P = Prog(nc, sems)

        def chk(tag):
            if stop == tag:
                P.finalize()
                raise _Stop(nc)

        def mm(out, lhsT, rhs, start, stop, deps=(), skip=False):
            return P.op("pe", lambda e: e.matmul(out, lhsT=lhsT, rhs=rhs, start=start, stop=stop,
                                                 skip_group_check=skip), deps)

        def tr(out, in_, idn, deps=()):
            return P.op("pe", lambda e: e.transpose(out=out, in_=in_, identity=idn), deps)

        def act(out, in_, func, deps=(), **kw):
            return P.op("act", lambda e: e.activation(out=out, in_=in_, func=func, **kw), deps)

        def ts(eng, out, in0, s1, s2, op0, op1=None, deps=()):
            if op1 is None:
                return P.op(eng, lambda e: e.tensor_scalar(out=out, in0=in0, scalar1=s1, scalar2=None, op0=op0), deps)
            return P.op(eng, lambda e: e.tensor_scalar(out=out, in0=in0, scalar1=s1, scalar2=s2, op0=op0, op1=op1), deps)

        def tt(eng, out, in0, in1, op, deps=()):
            return P.op(eng, lambda e: e.tensor_tensor(out=out, in0=in0, in1=in1, op=op), deps)

        def stt(out, in0, scalar, in1, op0, op1, deps=()):
            return P.op("dve", lambda e: e.scalar_tensor_tensor(out=out, in0=in0, scalar=scalar, in1=in1,
                                                                op0=op0, op1=op1), deps)

        def cp(eng, out, in_, deps=()):
            if eng == "act":
                return act(out, in_, AF.Copy, deps)
            return P.op(eng, lambda e: e.tensor_copy(out=out, in_=in_), deps)

        def memset(eng, ap, val, deps=()):
            return P.op(eng, lambda e: e.memset(ap, val), deps)

        def dma(q, sem, out, in_, deps=(), slow=False):
            if slow:
                q = "pool"
            return P.dma(q, sem, lambda e: e.dma_start(out=out, in_=in_, allow_slow_non_contiguous=slow), deps)

        ident = osb("ident", [128, 128], BF16)
        tri = osb("tri", [128, 128], BF16)
        onesf = osb("onesf", [128, 128], F32)
        smask = osb("smask_sb", [128, 3], F32)
        nbf = osb("nbf", [8, 1], F32)
        gpre_sb = osb("gpre_sb", [128, DC], F32)

        ph = ExitStack()
        with ph:
            def sb(name, shape, dt):
                return ph.enter_context(nc.sbuf_tensor(name, shape, dt))

            def ps(name, shape, dt):
                return ph.enter_context(nc.psum_tensor(name, shape, dt))

            sem_misc = newsem(ph)
            sem_gp = newsem(ph)
            sem_e = newsem(ph)
            sem_e4 = newsem(ph)
            sem_e1 = newsem(ph)
            sem_tl = newsem(ph)
            sem_w = [newsem(ph), newsem(ph)]
            c_ones = memset("pool", onesf[:], 1.0)
            c_id = P.op("pool", lambda e: e.affine_select(out=ident[:], in_=onesf[:], pattern=[[1, 128]],
                                                         compare_op=ALU.is_equal, fill=0.0, base=0,
                                                         channel_multiplier=-1), [c_ones])
            c_tri = P.op("pool", lambda e: e.affine_select(out=tri[:], in_=onesf[:], pattern=[[1, 128]],
                                                          compare_op=ALU.is_ge, fill=0.0, base=0,
                                                          channel_multiplier=-1), [c_ones])
            l_sm = dma("sp", sem_misc, smask[:], smask_d)
            l_bf = dma("sp", sem_misc, nbf[:], bfv)
            l_gp = dma("sp", sem_gp, gpre_sb[:], gpre.rearrange("(c p) -> p c", p=128), slow=True)
            c_nbf = ts("dve", nbf[:], nbf[:], -1.0, None, ALU.mult, deps=[l_bf])

            chk("a0")
            tm32_early = [sb("tm32_%d" % i, [128, 512], F32) for i in range(3)]
            EB = sb("EB", [128, 8, 5, 128], F32)
            if True:
                T = EB
                ea, eb = tm32_early[0], tm32_early[1]
                e0 = dma("sp", sem_e, ea[0:8, 64:256], rel)
                z1 = memset("dve", ea[0:8, 0:64], 0.0)
                z2 = memset("dve", eb[0:8, :], 0.0)
                z3 = ts("dve", ea[0:8, 0:64], ea[0:8, 0:64], ea[0:8, 64:65], None, ALU.add, deps=[e0, z1])
                z4 = ts("dve", eb[0:8, :], eb[0:8, :], ea[0:8, 255:256], None, ALU.add, deps=[e0, z2])
                e1 = dma("sp", sem_e1, ext[:, 0:256], ea[0:8, 0:256], deps=[z3])
                e3 = dma("sp", sem_e1, ext[:, 256:768], eb[0:8, :], deps=[z4])
                e4 = dma("sp", sem_e4, Erep,
                         bass.AP(tensor=ext.tensor, offset=0, ap=[[768, 8], [0, 128], [1, 768]]), deps=[e3])
                chk("a1")
                tl = None
                for h in range(8):
                    src = bass.AP(tensor=Erep.tensor, offset=h * 128 * 768 + 127, ap=[[767, 128], [128, 5], [1, 128]])
                    tl = dma("sp", sem_tl, T[:, h, :, :], src, deps=[e4])
                c_t0 = memset("dve", T[64:128, :, 0, 0:64], -BIG, deps=[tl])
                c_t4 = memset("dve", T[0:64, :, 4, 64:128], -BIG, deps=[tl])
                Tf = T[:].rearrange("p a b c -> p (a b c)")
                c_tl = act(Tf, Tf, AF.Exp, deps=[c_t0, c_t4])
            T_ready = [c_tl]
            chk("a2")
            Wb = sb("Wb", [128, DC, C1P], BF16)
            c_wpad = memset("dve", Wb[:, :, C1:C1P], 0.0, deps=[c_tl])
            CQ = C1 // 4
            wst = [sb("wst%d" % i, [128, CQ], F32) for i in range(4)]
            sem_w4 = [newsem(ph) for _ in range(4)]
            wfree = [c_tl] * 4
            W_ready = []
            k = 0
            for dc in range(DC):
                for hf in range(4):
                    c0 = hf * CQ
                    s = k % 4
                    ld = dma("sp", sem_w4[s], wst[s][:], w1[dc * 128:(dc + 1) * 128, c0:c0 + CQ],
                             deps=[wfree[s]])
                    o = ts("dve", Wb[:, dc, c0:c0 + CQ], wst[s][:], gpre_sb[:, dc:dc + 1], None, ALU.mult,
                           deps=[ld, l_gp])
                    wfree[s] = o
                    W_ready.append(o)
                    k += 1
            W_ready = W_ready[-4:] + [c_wpad]

            chk("a")
            NX = 3
            xt = [sb("xt%d" % i, [128, D], F32) for i in range(NX)]
            sem_x = [newsem(ph) for _ in range(NX)]
            junk = sb("junk", [128, D], BF16)
            ssq = sb("ssq", [128, 4], F32)
            rsd = sb("rsd", [128, 4], F32)
            hb = [sb("hb%d" % i, [128, D], BF16) for i in range(2)]
            NF = 6
            fmst = [sb("fmst%d" % i, [128, 512], BF16) for i in range(NF)]
            sem_fm = [newsem(ph) for _ in range(NF)]
            NT32 = 3
            tm32 = tm32_early
            sem_t32 = [newsem(ph) for _ in range(NT32)]
            tm16 = [sb("tm16_%d" % i, [128, 512], BF16) for i in range(NT16)]
            sem_t16 = [newsem(ph) for _ in range(NT16)]
            pbt = [[sb("pbt%d_%d" % (a_, i), [128, 512], BF16) for i in range(2)] for a_ in range(2)]
            ptmp = [[sb("ptmp%d_%d" % (a_, i), [128, 512], BF16) for i in range(2)] for a_ in range(2)]
            osbuf = [sb("osbuf%d" % i, [65, 512], F32) for i in range(2)]
            rd = sb("rd", [128, 1024], F32)
            sel = sb("sel", [128, 128], F32)
            yab = [sb("yab%d" % i, [64, 512], BF16) for i in range(2)]
            sem_ya = [newsem(ph) for _ in range(2)]
            el8 = sb("el8", [8, 512], F32)
            G8 = [sb("G8_%d" % i, [8, 512], F32) for i in range(2)]
            on8 = sb("on8", [8, 512], F32)
            r8 = sb("r8", [8, 512], F32)
            lim = sb("lim", [8, 4, 512], BF16)
            qst = sb("qst", [8, 4, 512], BF16)
            lf8 = sb("lf8", [8, 512], F32)
            sem_lim = newsem(ph)
            sem_qst = newsem(ph)
            sem_lf = newsem(ph)

            tp = [ps("tp%d" % i, [128, DC, 128], BF16) for i in range(1)] * 2
            mp = [ps("mp%d" % i, [128, 512], F32) for i in range(3)]
            sT = [ps("sT%d" % i, [128, 512], F32) for i in range(2)]
            oT = [ps("oT%d" % i, [128, 512], F32) for i in range(2)]
            c_rd = memset("pool", rd[:], 0.0)
            c_sel0 = memset("pool", sel[:], 0.0)
            c_sel = memset("pool", sel[64:65, :], 1.0, deps=[c_sel0])

            c_on8 = memset("pool", on8[:], 1.0)
            c_lim = memset("pool", lim[:, 0, :], 1.0)
            c_qst = memset("pool", qst[:, 1:4, :], 1.0)

            st = dict(xi=0, xfree=[None] * NX, hbfree=[None, None], tpfree=[None, None], hTfree=[None, None],
                      mpi=0, mpfree=[[], [], []], fmi=0, fmlast=[None] * NF, t32i=0, t32last=[e3, e3, None],
                      t16i=0, t16last=[None] * NT16, rsfree=[None] * 4, sTi=0, sTfree=[None, None],
                      pbfree=[[None, None], [None, None]], ptfree=[[None, None], [None, None]], oTfree=[[], []], yai=0, yalast=[None, None],
                      limlast=None, qstlast=None, lflast=None, Gi=0, Gprev=None, KATfree=None, VAfree=None,
                      QATfree=None, osfree=[None, None], rdfree=[None, None], pend_norm=None, elfree=None, r8free=None)

            def norm_a(src_ap, n):
                i = st["xi"]
                st["xi"] += 1
                xs_ = xt[i % NX]
                ld = dma("sp", sem_x[i % NX], xs_[0:n, :], src_ap, deps=[st["xfree"][i % NX]])
                c = i % 4
                sq = act(junk[0:n, :], xs_[0:n, :], AF.Square, deps=[ld], accum_out=ssq[0:n, c:c + 1])
                sr = act(rsd[0:n, c:c + 1], ssq[0:n, c:c + 1], AF.Ln, deps=[sq, st["rsfree"][c]],
                         scale=1.0 / D, bias=EPS)
                rc = act(rsd[0:n, c:c + 1], rsd[0:n, c:c + 1], AF.Exp, deps=[sr], scale=-0.5)
                b = i % 2
                sc = ts("dve", hb[b][0:n, :], xs_[0:n, :], rsd[0:n, c:c + 1], None, ALU.mult,
                        deps=[rc, ld, st["hbfree"][b]])
                st["rsfree"][c] = sc
                st["xfree"][i % NX] = sc
                return (b, sc, n)

            def norm_b(hnd, hTdst, deps_hT):
                b, sc, n = hnd
                t_ = None
                for dc in range(DC):
                    t_ = tr(tp[b][:, dc, 0:n], hb[b][0:n, dc * 128:(dc + 1) * 128], ident[0:n, 0:n],
                            deps=[sc, st["tpfree"][0], c_id])
                st["hbfree"][b] = t_
                co = act(hTdst, tp[b][:, :, 0:n], AF.Copy, deps=[t_] + list(deps_hT))
                st["tpfree"][0] = co
                return co

            def norm_transpose(src_ap, n, hTdst, deps_hT):
                co = norm_b(norm_a(src_ap, n), hTdst, deps_hT)
                return co, None, None

            def mm_group(pairs, M, N, deps):
                bi = st["mpi"] % 3
                st["mpi"] += 1
                bank = mp[bi]
                o = None
                n_ = len(pairs)
                for kk, (l_, r_) in enumerate(pairs):
                    o = mm(bank[0:M, 0:N], l_, r_, kk == 0, kk == n_ - 1, deps=list(deps) + st["mpfree"][bi])
                return bi, bank, o

            def fm_chunk(Wt, c0, M, hTs, N, deps):
                pairs = [(Wt[:, dc, c0:c0 + M], hTs[:, dc, 0:N]) for dc in range(DC)]
                return mm_group(pairs, M, N, deps)

            def tm_group(Wt, c0, ncol, hTs, t0, n, deps):
                pairs = [(hTs[:, dc, t0:t0 + n], Wt[:, dc, c0:c0 + ncol]) for dc in range(DC)]
                return mm_group(pairs, n, ncol, deps)

            def store_fm_heads(bank, bi, mmop, dst, hpair, col0, N, eng="dve"):
                s = st["fmi"] % NF
                st["fmi"] += 1
                ev = cp(eng, fmst[s][:, 0:N], bank[:, 0:N], deps=[mmop, st["fmlast"][s]])
                st["mpfree"][bi] = [ev]
                d_ = None
                for hh in range(2):
                    d_ = dma("pool", sem_fm[s], dst[2 * hpair + hh, 0:64, col0:col0 + N],
                             fmst[s][hh * 64:(hh + 1) * 64, 0:N], deps=[ev])
                st["fmlast"][s] = d_
                return ev

            def fpath(bank, bi, mmop, N, col0, Kdst, qcol0, own, lfdst):
                gi = st["Gi"] % 2
                st["Gi"] += 1
                G = G8[gi]
                a1 = act(el8[:, 0:N], bank[0:8, 0:N], AF.Exp, deps=[mmop, c_nbf, st["elfree"]], scale=-1.0, bias=nbf[:])
                st["mpfree"][bi] = [a1]
                a2 = act(el8[:, 0:N], el8[:, 0:N], AF.Ln, deps=[a1], bias=1.0)
                init = 0.0 if st["Gprev"] is None else st["Gprev"][0]
                sdeps = [a2, c_on8] + ([] if st["Gprev"] is None else [st["Gprev"][1]])
                sc_ = P.op("dve", lambda e: e.tensor_tensor_scan(out=G[:, 0:N], data0=on8[:, 0:N], data1=el8[:, 0:N],
                                                                 initial=init, op0=ALU.mult, op1=ALU.add),
                           sdeps + [st["limlast"], st["qstlast"]])
                st["Gprev"] = (G[:, N - 1:N], sc_)
                d1 = ts("dve", lim[:, 1, 0:N], G[:, 0:N], 8.0, None, ALU.mult, deps=[sc_, st["limlast"], c_lim])
                d2 = stt(r8[:, 0:N], G[:, 0:N], 8.0, lim[:, 1, 0:N], ALU.mult, ALU.subtract, deps=[d1, st["r8free"]])
                d3 = cp("dve", lim[:, 2, 0:N], r8[:, 0:N], deps=[d2])
                d4 = tt("dve", lim[:, 3, 0:N], r8[:, 0:N], lim[:, 2, 0:N], ALU.subtract, deps=[d3])
                st["r8free"] = d4
                st["limlast"] = dma("pool", sem_lim, Kdst[:, 64:68, col0:col0 + N], lim[:, :, 0:N], deps=[d4])
                last = d4
                if own:
                    q1 = ts("dve", qst[:, 0, 0:N], G[:, 0:N], -8.0, None, ALU.mult, deps=[sc_, st["qstlast"], c_qst])
                    st["qstlast"] = dma("pool", sem_qst, Qs[:, 64:68, qcol0:qcol0 + N], qst[:, :, 0:N], deps=[q1])
                    q2 = ts("dve", lf8[:, 0:N], el8[:, 0:N], -1.0, None, ALU.mult, deps=[a2, st["lflast"]])
                    st["lflast"] = dma("sp", sem_lf, lfdst.rearrange("t h -> h t"), lf8[:, 0:N], deps=[q2], slow=True)
                    last = q2
                st["elfree"] = last
                return last

            def band2(heads, Qaps, tiles_of, nq, ycol0, m0ctx, filler=None):
                nt = len(tiles_of[0])

                def emit_qk(a_, ti):
                    Kap, Vap, nk, q0, n, Eb, cf = tiles_of[a_][ti]
                    return mm(sT[a_][0:nk, 0:n], Kap, Qaps[a_][:, q0:q0 + n], True, True,
                              deps=[st["sTfree"][a_]] + st["Kready"] + st["Qready"])

                def emit_exp(a_, ti, qk):
                    Kap, Vap, nk, q0, n, Eb, cf = tiles_of[a_][ti]
                    kw = {"scale": SCALE}
                    if cf and m0ctx:
                        kw["bias"] = smask[0:nk, 2:3]
                    ex = act(ptmp[a_][ti % 2][0:nk, 0:n], sT[a_][0:nk, 0:n], AF.Exp,
                             deps=[qk, st["ptfree"][a_][ti % 2], l_bf], **kw)
                    st["sTfree"][a_] = ex
                    po_, pi_ = pbt[a_][ti % 2][0:nk, 0:n], ptmp[a_][ti % 2][0:nk, 0:n]
                    if len(Eb.shape) == 3:
                        po_ = po_.rearrange("p (a q) -> p a q", q=128)
                        pi_ = pi_.rearrange("p (a q) -> p a q", q=128)
                    mu = tt("dve", po_, pi_, Eb, ALU.mult, deps=[ex, st["pbfree"][a_][ti % 2]] + T_ready)
                    st["ptfree"][a_][ti % 2] = mu
                    return mu

                def emit_pv(a_, ti, ex):
                    Kap, Vap, nk, q0, n, Eb, cf = tiles_of[a_][ti]
                    pv = mm(oT[a_][0:65, q0:q0 + n], Vap, pbt[a_][ti % 2][0:nk, 0:n], ti == 0, ti == nt - 1,
                            deps=[ex] + st["oTfree"][a_] + st["Vready"], skip=True)
                    st["pbfree"][a_][ti % 2] = pv
                    return pv

                exs = {}
                pvs = {}
                qkA = emit_qk(0, 0)
                exs[(0, 0)] = emit_exp(0, 0, qkA)
                qkB = emit_qk(1, 0)
                exs[(1, 0)] = emit_exp(1, 0, qkB)
                for ti in range(nt):
                    if ti == 1 and st["pend_norm"] is not None:
                        st["pend_norm"]()
                        st["pend_norm"] = None
                    pvs[0] = emit_pv(0, ti, exs.pop((0, ti)))
                    if ti + 1 < nt:
                        q_ = emit_qk(0, ti + 1)
                        exs[(0, ti + 1)] = emit_exp(0, ti + 1, q_)
                    pvs[1] = emit_pv(1, ti, exs.pop((1, ti)))
                    if ti + 1 < nt:
                        q_ = emit_qk(1, ti + 1)
                        exs[(1, ti + 1)] = emit_exp(1, ti + 1, q_)
                    if filler is not None and ti % 2 == 1:
                        filler()
                norm_jobs = []
                c1s = []
                for a_ in range(2):
                    c1 = act(osbuf[a_][0:65, 0:nq], oT[a_][0:65, 0:nq], AF.Copy, deps=[pvs[a_], st["osfree"][a_]])
                    st["oTfree"][a_] = [c1]
                    c1s.append(c1)
                for a_ in range(2):
                    h = heads[a_]
                    l1 = act(rd[64:65, a_ * 512:a_ * 512 + nq], osbuf[a_][64:65, 0:nq], AF.Ln,
                             deps=[c1s[a_], st["rdfree"][a_], c_rd])
                    r1 = act(rd[64:65, a_ * 512:a_ * 512 + nq], rd[64:65, a_ * 512:a_ * 512 + nq], AF.Exp, deps=[l1], scale=-1.0)
                    norm_jobs.append((a_, h, r1, c1s[a_]))

                def finish(norm_jobs=norm_jobs, nq=nq, ycol0=ycol0):
                    for (a_, h, r1, c1) in norm_jobs:
                        bi = st["mpi"] % 3
                        st["mpi"] += 1
                        bc = mm(mp[bi][:, 0:nq], sel[:, :], rd[:, a_ * 512:a_ * 512 + nq], True, True,
                                deps=[r1, c_sel] + st["mpfree"][bi])
                        st["rdfree"][a_] = bc
                        yi = st["yai"] % 2
                        st["yai"] += 1
                        y1 = tt("dve", yab[yi][:, 0:nq], osbuf[a_][0:64, 0:nq], mp[bi][0:64, 0:nq], ALU.mult,
                                deps=[c1, bc, st["yalast"][yi]])
                        st["mpfree"][bi] = [y1]
                        st["osfree"][a_] = y1
                        st["yalast"][yi] = dma("pool", sem_ya[yi], YA[h * 64:(h + 1) * 64, ycol0:ycol0 + nq], yab[yi][:, 0:nq],
                                               deps=[y1])
                st["pend_norm"] = finish
                return [pvs[0], pvs[1]]

            st["Kready"] = []
            st["Qready"] = []
            st["Vready"] = []

            if sample:
                sp_ = ExitStack()
                with sp_:
                    def ssb(name, shape, dt):
                        return sp_.enter_context(nc.sbuf_tensor(name, shape, dt))
                    hTs = ssb("hTs", [128, DC, NSAMP], BF16)
                    QATs = ssb("QATs", [128, 8, NSAMP], BF16)
                    c_qz = memset("pool", QATs[:], 0.0)
                    KATs = ssb("KATs", [128, 4, 640], BF16)
                    VAs = ssb("VAs", [128, 5, 8, 65], BF16)
                    cst = [ssb("cst%d" % i, [128, 512], F32) for i in range(2)]
                    sem_c = [newsem(sp_), newsem(sp_)]
                    c16 = [ssb("c16_%d" % i, [128, 512], BF16) for i in range(2)]
                    kst = ssb("kst", [128, 4, 512], BF16)
                    sem_kst = newsem(sp_)
                    lc = ssb("lc", [8, LS], F32)
                    Gc = ssb("Gc", [8, CH2], F32)
                    on8c = ssb("on8c", [8, CH2], F32)
                    r8c = ssb("r8c", [8, CH2], F32)
                    limc2 = [ssb("limc%d" % i, [8, 4, CH2], BF16) for i in range(2)]
                    sem_lc = newsem(sp_)
                    sem_limc2 = [newsem(sp_), newsem(sp_)]
                    sem_dd = newsem(sp_)
                    sem_vc = newsem(sp_)
                    lcl = None
                    for g in range(LS // CH):
                        lcl = dma("sp", sem_lc, lc[:, g * CH:(g + 1) * CH], cbl[g * CH:(g + 1) * CH, :].rearrange("t h -> h t"),
                                  slow=True)
                    c_vas = memset("pool", VAs[:, :, :, 64:65], 1.0)
                    c_on8c = memset("pool", on8c[:], 1.0)
                    c_limc = memset("pool", limc2[0][:, 0, :], 1.0)
                    c_limc = memset("pool", limc2[1][:, 0, :], 1.0, deps=[c_limc])

                    dma("sp", sem_dd, aks[0:496, :], cak[16:512, :])
                    dma("sp", sem_dd, avs[0:496, :], cav[16:512, :])

                    co, _, _ = norm_transpose(xs_d, NSAMP, hTs[:, :, :], [])
                    wdeps = [co] + W_ready
                    chk("b0")
                    for i in range(4):
                        bi, bank, o = fm_chunk(Wb, 0 + i * 128, 128, hTs, NSAMP, wdeps)
                        ev0 = cp("dve", QATs[0:64, 2 * i, :], bank[0:64, 0:NSAMP], deps=[o, c_qz])
                        ev = cp("dve", QATs[64:128, 2 * i + 1, :], bank[64:128, 0:NSAMP], deps=[o, c_qz])
                        st["mpfree"][bi] = [ev]
                    qats_ready = ev
                    for i in range(4):
                        bi, bank, o = fm_chunk(Wb, 512 + i * 128, 128, hTs, NSAMP, wdeps)
                        ev = cp("dve", KATs[:, i, 512:512 + NSAMP], bank[:, 0:NSAMP], deps=[o])
                        st["mpfree"][bi] = [ev]
                    kats_new = ev
                    chk("b1")
                    for i in range(4):
                        bi, bank, o = fm_chunk(Wb, 1536 + i * 128, 128, hTs, NSAMP, wdeps)
                        store_fm_heads(bank, bi, o, Qs, i, NQ, NSAMP)
                    for i in range(4):
                        bi, bank, o = fm_chunk(Wb, 2048 + i * 128, 128, hTs, NSAMP, wdeps)
                        store_fm_heads(bank, bi, o, Kss, i, LS, NSAMP)
                    chk("b2")
                    vas_new = None
                    for (c0, dst32, dst16) in ((512, aks[496:512, :], None), (1024, avs[496:512, :], "va"),
                                               (2048, bks, None), (2560, bvs, "vb")):
                        bi, bank, o = tm_group(Wb, c0, 512, hTs, 0, NSAMP, wdeps)
                        s = st["t32i"] % NT32
                        st["t32i"] += 1
                        ev = cp("dve", tm32[s][0:NSAMP, :], bank[0:NSAMP, :], deps=[o] + _aslist(st["t32last"][s]))
                        st["t32last"][s] = dma("pool", sem_t32[s], dst32, tm32[s][0:NSAMP, :], deps=[ev])
                        if dst16 == "va":
                            vas_new = cp("dve", VAs[0:NSAMP, 4, :, 0:64], tm32[s][0:NSAMP, :].rearrange("p (h d) -> p h d", h=8),
                                         deps=[ev, c_vas])
                            st["t32last"][s] = _last2(st["t32last"][s], vas_new)
                            st["mpfree"][bi] = [ev]
                        elif dst16 == "vb":
                            s2 = st["t16i"] % NT16
                            st["t16i"] += 1
                            ev2 = cp("pool", tm16[s2][0:NSAMP, :], tm32[s][0:NSAMP, :], deps=[ev, st["t16last"][s2]])
                            st["t16last"][s2] = dma("pool", sem_t16[s2], Vss[LS:LS + NSAMP, :], tm16[s2][0:NSAMP, :], deps=[ev2])
                            st["t32last"][s] = _last2(st["t32last"][s], ev2)
                            st["mpfree"][bi] = [ev]
                        else:
                            st["mpfree"][bi] = [ev]
                        chk("b3_%d" % c0)
                    chk("b")
                    cfree = [None, None]
                    c16free = [None, None]
                    kk = 0

                    def load_cast(src):
                        nonlocal kk
                        s = kk % 2
                        kk += 1
                        ld = dma("sp", sem_c[s], cst[s][:], src, deps=[cfree[s]])
                        return s, ld

                    kats_c = None
                    vas_c = None
                    for kt in range(4):
                        s, ld = load_cast(cak[kt * 128:(kt + 1) * 128, :])
                        cc = cp("dve", c16[s][:], cst[s][:], deps=[ld, c16free[s]])
                        cfree[s] = cc
                        b = kt % 2
                        t_ = None
                        for i in range(4):
                            t_ = tr(tp[b][:, i, :], c16[s][:, i * 128:(i + 1) * 128], ident[:], deps=[cc, st["tpfree"][0], c_id])
                        c16free[s] = t_
                        kats_c = act(KATs[:, :, kt * 128:(kt + 1) * 128], tp[b][:, 0:4, :], AF.Copy, deps=[t_])
                        st["tpfree"][0] = kats_c
                        s, ld = load_cast(cav[kt * 128:(kt + 1) * 128, :])
                        vas_c = cp("dve", VAs[:, kt, :, 0:64], cst[s][:].rearrange("p (h d) -> p h d", h=8), deps=[ld, c_vas])
                        cfree[s] = vas_c
                    kstlast = None
                    for g in range(LS // 512):
                        evs = []
                        for q in range(4):
                            kt = g * 4 + q
                            s, ld = load_cast(cbk[kt * 128:(kt + 1) * 128, :])
                            cc = cp("dve", c16[s][:], cst[s][:], deps=[ld, c16free[s]])
                            cfree[s] = cc
                            b = kt % 2
                            t_ = None
                            for i in range(4):
                                t_ = tr(tp[b][:, i, :], c16[s][:, i * 128:(i + 1) * 128], ident[:], deps=[cc, st["tpfree"][0], c_id])
                            c16free[s] = t_
                            ev = act(kst[:, :, q * 128:(q + 1) * 128], tp[b][:, 0:4, :], AF.Copy, deps=[t_, kstlast])
                            st["tpfree"][0] = ev
                            evs.append(ev)
                        d_ = None
                        for hh in range(8):
                            d_ = dma("pool", sem_kst, Kss[hh, 0:64, g * 512:(g + 1) * 512],
                                     kst[(hh % 2) * 64:(hh % 2) * 64 + 64, hh // 2, :], deps=evs)
                        kstlast = d_
                    for g in range(8):
                        r0, r1_ = g * (LS // 8), (g + 1) * (LS // 8)
                        dma("pool", sem_vc, Vss[r0:r1_, :], cbv[r0:r1_, :])
                    chk("c")
                    limcl = [None, None]
                    limclast = None
                    gprev = None
                    n1 = ts("dve", lc[:], lc[:], -1.0, None, ALU.mult, deps=[lcl])
                    for g in range(LS // CH2):
                        limc = limc2[g % 2]
                        limclast = limcl[g % 2]
                        lcg = lc[:, g * CH2:(g + 1) * CH2]
                        n1d = [n1]
                        init = 0.0
                        if gprev is not None:
                            cz = cp("dve", r8c[:, 0:1], Gc[:, CH2 - 1:CH2], deps=[gprev])
                            init = r8c[:, 0:1]
                            n1d = [n1, cz]
                        sc_ = P.op("dve", lambda e, init=init, lcg=lcg: e.tensor_tensor_scan(out=Gc[:], data0=on8c[:], data1=lcg,
                                                                                            initial=init, op0=ALU.mult, op1=ALU.add),
                                   n1d + [c_on8c, limclast])
                        d1 = ts("dve", limc[:, 1, :], Gc[:], 8.0, None, ALU.mult, deps=[sc_, limclast, c_limc])
                        d2 = stt(r8c[:], Gc[:], 8.0, limc[:, 1, :], ALU.mult, ALU.subtract, deps=[d1])
                        d3 = cp("dve", limc[:, 2, :], r8c[:], deps=[d2])
                        d4 = tt("dve", limc[:, 3, :], r8c[:], limc[:, 2, :], ALU.subtract, deps=[d3])
                        limcl[g % 2] = dma("pool", sem_limc2[g % 2], Kss[:, 64:68, g * CH2:(g + 1) * CH2], limc[:], deps=[d4])
                        gprev = d4
                    chk("d")
                    bi, bank, o = fm_chunk(Wb, 3072, 128, hTs, NSAMP, wdeps)
                    cz = cp("dve", G8[1][:, 511:512], Gc[:, CH2 - 1:CH2], deps=[gprev])
                    st["Gprev"] = (G8[1][:, 511:512], cz)
                    st["Gi"] = 0
                    fl = fpath(bank, bi, o, NSAMP, LS, Kss, NQ, True, bls)
                    st["Gprev"] = None
                    st["Gi"] = 0
                    chk("e")
                    st["Kready"] = [kats_c, kats_new]
                    st["Qready"] = [qats_ready]
                    st["Vready"] = [vas_c, vas_new]
                    for hp in range(4):
                        heads = (2 * hp, 2 * hp + 1)
                        tiles_of = []
                        for h in heads:
                            tiles = []
                            for kt in range(5):
                                nk = 128 if kt < 4 else NSAMP
                                tiles.append((KATs[:, hp, kt * 128:kt * 128 + nk], VAs[0:nk, kt, h, :], nk, 0, NSAMP,
                                              EB[0:nk, h, 4 - kt, 0:NSAMP], False))
                            tiles_of.append(tiles)
                        band2(heads, [QATs[:, heads[0], :], QATs[:, heads[1], :]], tiles_of, NSAMP, NQ, False)
                    st["pend_norm"]()
                    st["pend_norm"] = None
                    P.finalize()
                    if stop == "p1s":
                        raise _Stop(nc)
                st["Kready"] = []
                st["Qready"] = []
                st["Vready"] = []
                st["mpfree"] = [[], [], []]
                for k_ in ("xfree", "hbfree", "tpfree", "fmlast", "t32last", "t16last", "rsfree", "sTfree",
                           "yalast", "osfree"):
                    st[k_] = [None] * len(st[k_])
                st["oTfree"] = [[], []]
                st["pbfree"] = [[None, None], [None, None]]
                st["ptfree"] = [[None, None], [None, None]]
                st["rdfree"] = [None, None]
                for k_ in ("limlast", "qstlast", "lflast", "elfree", "r8free"):
                    st[k_] = None

            hT = [sb("hT%d" % i, [128, DC, 512], BF16) for i in range(2)]
            QAT = sb("QATz", [128, 8, 512], BF16)
            c_qzp = memset("pool", QAT[:], 0.0)
            KAT = sb("KAT", [128, 4, 1024], BF16)
            VA = sb("VA", [128, 8, 8, 65], BF16)
            c_va = memset("pool", VA[:, :, :, 64:65], 1.0)
            def xsrc(p, t):
                return xw[p * 512 + t * 128:p * 512 + (t + 1) * 128, :]

            hnds = {}
            cps_of = {}

            def emit_a(p, t):
                if p < NSLOT:
                    hnds[(p, t)] = norm_a(xsrc(p, t), 128)

            def emit_b(p, t):
                if p < NSLOT:
                    co = norm_b(hnds.pop((p, t)), hT[p % 2][:, :, t * 128:(t + 1) * 128], [st["hTfree"][p % 2]])
                    cps_of.setdefault(p, []).append(co)

            for t in range(4):
                emit_a(0, t) if t < 2 else None
            emit_b(0, 0)
            emit_b(0, 1)
            emit_a(0, 2)
            emit_a(0, 3)
            emit_b(0, 2)
            emit_b(0, 3)

            WD = {}

            def make_slot(p):
                own = (p % 4 == 3)
                ctx = (p % 4 == 2)
                m = p // 4
                hTp = hT[p % 2]
                lm = {"o": None}
                groups = []

                def g_kb(i):
                    bi, bank, o = fm_chunk(Wb, 2048 + i * 128, 128, hTp, 512, WD[p])
                    store_fm_heads(bank, bi, o, Ks, i, p * 512, 512)
                    lm["o"] = o

                def g_f():
                    bi, bank, o = fm_chunk(Wb, 3072, 128, hTp, 512, WD[p])
                    fpath(bank, bi, o, 512, p * 512, Ks, m * 512, own, blf_own[m * 512:(m + 1) * 512, :] if own else None)
                    lm["o"] = o

                def g_ka(i):
                    kcol = 512 if own else 0
                    bi, bank, o = fm_chunk(Wb, 512 + i * 128, 128, hTp, 512, WD[p])
                    ev = cp("dve", KAT[:, i, kcol:kcol + 512], bank[:, :], deps=[o] + _aslist(st["KATfree"]))
                    st["mpfree"][bi] = [ev]
                    st["Kready"] = [ev]
                    lm["o"] = o

                def g_qa(i):
                    bi, bank, o = fm_chunk(Wb, 0 + i * 128, 128, hTp, 512, WD[p])
                    ev0 = cp("dve", QAT[0:64, 2 * i, :], bank[0:64, :], deps=[o, c_qzp] + _aslist(st["QATfree"]))
                    ev = cp("dve", QAT[64:128, 2 * i + 1, :], bank[64:128, :], deps=[o, c_qzp] + _aslist(st["QATfree"]))
                    st["mpfree"][bi] = [ev]
                    st["Qready"] = [ev]
                    lm["o"] = o

                def g_qb(i):
                    bi, bank, o = fm_chunk(Wb, 1536 + i * 128, 128, hTp, 512, WD[p])
                    store_fm_heads(bank, bi, o, Qs, i, m * 512, 512)
                    lm["o"] = o

                def g_vb(t):
                    tok0 = p * 512 + t * 128
                    otok0 = m * 512 + t * 128
                    bi, bank, o = tm_group(Wb, 2560, 512, hTp, t * 128, 128, WD[p])
                    lm["o"] = o
                    s2 = st["t16i"] % NT16
                    st["t16i"] += 1
                    if own:
                        s = st["t32i"] % NT32
                        st["t32i"] += 1
                        ev = cp("dve", tm32[s][:], bank[:, :], deps=[o] + _aslist(st["t32last"][s]))
                        d32 = dma("pool", sem_t32[s], bv_own[otok0:otok0 + 128, :], tm32[s][:], deps=[ev])
                        ev2 = cp("pool", tm16[s2][:], tm32[s][:], deps=[ev, st["t16last"][s2]])
                        st["t32last"][s] = [d32, ev2]
                        st["mpfree"][bi] = [ev]
                    else:
                        ev2 = cp("act", tm16[s2][:], bank[:, :], deps=[o, st["t16last"][s2]])
                        st["mpfree"][bi] = [ev2]
                    st["t16last"][s2] = dma("pool", sem_t16[s2], Vs[tok0:tok0 + 128, :], tm16[s2][:], deps=[ev2])

                def g_kbt(t):
                    otok0 = m * 512 + t * 128
                    bi, bank, o = tm_group(Wb, 2048, 512, hTp, t * 128, 128, WD[p])
                    lm["o"] = o
                    s = st["t32i"] % NT32
                    st["t32i"] += 1
                    ev = cp("dve", tm32[s][:], bank[:, :], deps=[o] + _aslist(st["t32last"][s]))
                    st["t32last"][s] = dma("pool", sem_t32[s], bk_own[otok0:otok0 + 128, :], tm32[s][:], deps=[ev])
                    st["mpfree"][bi] = [ev]

                def g_va(t):
                    kt = (4 if own else 0) + t
                    bi, bank, o = tm_group(Wb, 1024, 512, hTp, t * 128, 128, WD[p])
                    lm["o"] = o
                    ev2 = cp("dve", VA[:, kt, :, 0:64], bank[:, :].rearrange("p (h d) -> p h d", h=8),
                             deps=[o, c_va] + _aslist(st["VAfree"]))
                    st["Vready"] = [ev2]
                    st["mpfree"][bi] = [ev2]
                    if own and m == NOWN - 1:
                        s = st["t32i"] % NT32
                        st["t32i"] += 1
                        ev = cp("dve", tm32[s][:], bank[:, :], deps=[o] + _aslist(st["t32last"][s]))
                        st["t32last"][s] = dma("pool", sem_t32[s], avp[t * 128:(t + 1) * 128, :], tm32[s][:], deps=[ev])
                        st["mpfree"][bi] = [ev, ev2]

                def g_kat(t):
                    bi, bank, o = tm_group(Wb, 512, 512, hTp, t * 128, 128, WD[p])
                    lm["o"] = o
                    s = st["t32i"] % NT32
                    st["t32i"] += 1
                    ev = cp("dve", tm32[s][:], bank[:, :], deps=[o] + _aslist(st["t32last"][s]))
                    st["t32last"][s] = dma("pool", sem_t32[s], akp[t * 128:(t + 1) * 128, :], tm32[s][:], deps=[ev])
                    st["mpfree"][bi] = [ev]

                def g_band(hp, filler=None):
                    heads = (2 * hp, 2 * hp + 1)
                    tiles_of = []
                    for h in heads:
                        tiles = []
                        for kt in range(8):
                            q0 = max(0, kt - 4)
                            q1 = min(3, kt)
                            d0 = 4 + q0 - kt
                            n = (q1 - q0 + 1) * 128
                            tiles.append((KAT[:, hp, kt * 128:(kt + 1) * 128], VA[:, kt, h, :], 128, q0 * 128, n,
                                          EB[:, h, d0:d0 + (q1 - q0 + 1), :], kt < 4))
                        tiles_of.append(tiles)
                    pvl = band2(heads, [QAT[:, heads[0], :], QAT[:, heads[1], :]], tiles_of, 512, m * 512, m == 0, filler)
                    if hp == 3:
                        st["KATfree"] = pvl
                        st["VAfree"] = pvl
                        st["QATfree"] = pvl

                def mk(f, a_):
                    return lambda: f(a_)

                for i in range(4):
                    groups.append(mk(g_kb, i))
                groups.append(g_f)
                if own or ctx:
                    for i in range(4):
                        groups.append(mk(g_ka, i))
                if own:
                    for i in range(4):
                        groups.append(mk(g_qa, i))
                    for i in range(4):
                        groups.append(mk(g_qb, i))
                for t in range(4):
                    groups.append(mk(g_vb, t))
                    if own:
                        groups.append(mk(g_kbt, t))
                    if own or ctx:
                        groups.append(mk(g_va, t))
                    if own and m == NOWN - 1:
                        groups.append(mk(g_kat, t))
                nproj = len(groups)
                ng = nproj
                a_at = {0: [0, 1], max(1, ng // 4): [2], max(2, ng // 2): [3]}
                b_at = {max(1, ng // 8): 0, max(2, (3 * ng) // 8): 1, max(3, (5 * ng) // 8): 2, max(4, (7 * ng) // 8): 3}
                done_a, done_b = set(), set()
                units = []

                def unit(gi, g):
                    def run():
                        if gi == 0:
                            WD[p] = cps_of.pop(p) + W_ready
                        for t in a_at.get(gi, []):
                            emit_a(p + 1, t)
                            done_a.add(t)
                        if gi in b_at:
                            t = b_at[gi]
                            if t in done_a:
                                emit_b(p + 1, t)
                                done_b.add(t)
                        g()
                        if gi == 2 and st["pend_norm"] is not None and not own:
                            st["pend_norm"]()
                            st["pend_norm"] = None
                        if gi == nproj - 1:
                            st["hTfree"][p % 2] = lm["o"]
                            for t in range(4):
                                if t not in done_a:
                                    emit_a(p + 1, t)
                                if t not in done_b:
                                    emit_b(p + 1, t)
                    return run

                for gi, g in enumerate(groups):
                    units.append(("plain", unit(gi, g), p))
                if own:
                    for hp in range(4):
                        units.append(("band", (lambda f, hp=hp: g_band(hp, f)), p))
                return units

            seq = []
            for p in range(NSLOT):
                seq.extend(make_slot(p))
            pos = [0]
            while pos[0] < len(seq):
                kind, fn, slot = seq[pos[0]]
                pos[0] += 1
                if kind == "band":
                    def filler(slot=slot):
                        i = pos[0]
                        while i < len(seq) and seq[i][0] == "band":
                            i += 1
                        if i < len(seq) and seq[i][2] % 4 in (0, 1) and seq[i][2] <= slot + 2:
                            u = seq.pop(i)
                            u[1]()
                    fn(filler)
                else:
                    fn()
                    if st["pend_norm"] is not None:
                        st["pend_norm"]()
                        st["pend_norm"] = None
            if st["pend_norm"] is not None:
                st["pend_norm"]()
                st["pend_norm"] = None
            P.finalize()
            if stop == "p1":
                raise _Stop(nc)

        ph = ExitStack()
        with ph:
            def sb(name, shape, dt):
                return ph.enter_context(nc.sbuf_tensor(name, shape, dt))

            def ps(name, shape, dt):
                return ph.enter_context(nc.psum_tensor(name, shape, dt))

            NKT = L // 128
            Kb = [sb("Kb%d" % i, [68, L], BF16) for i in range(2)]
            Vb = [sb("Vb%d" % i, [128, NKT, 65], BF16) for i in range(2)]
            Qb = [sb("Qb%d" % i, [68, NQA], BF16) for i in range(2)]
            sem_kc = [[newsem(ph) for _ in range(4)] for _ in range(2)]
            sem_ks = [newsem(ph), newsem(ph)]
            if sample:
                Ksb = [sb("Ksb%d" % i, [68, LSP], BF16) for i in range(2)]
                Vsb = [sb("Vsb%d" % i, [128, LSP // 128, 65], BF16) for i in range(2)]
            NPB = 5
            pb2 = [sb("pb2_%d" % i, [128, 512], BF16) for i in range(NPB)]
            osb2 = sb("osb2", [65, 512], F32)
            rd2 = sb("rd2", [65, 512], F32)
            yb2 = [sb("yb2_%d" % i, [64, 512], BF16) for i in range(2)]
            sem_yb = [newsem(ph), newsem(ph)]
            sT2 = [ps("sT2_%d" % i, [128, 512], F32) for i in range(NPB)]
            oT2 = [ps("oT2_%d" % i, [128, 512], F32) for i in range(2)]
            bc2 = ps("bc2", [128, 512], F32)

            wq32 = [sb("wq32_%d" % i, [128, C3], F32) for i in range(3)]
            wq16 = [sb("wq16_%d" % i, [128, C3], BF16) for i in range(3)]
            sem_wq = [newsem(ph) for _ in range(3)]
            sem_wqs = [newsem(ph) for _ in range(3)]
            wq = dict(i=0, free32=[None] * 3, last16=[None] * 3, pend=[])
            wtasks = [(w3, W3s, C3, dc, True) for dc in range(DC)] + [(wbr, Wbrs, D, dc, False) for dc in range(DC)] + \
                     [(wout, Wouts, D, dc, False) for dc in range(DC)]

            def wtask_load():
                if not wtasks:
                    return
                src, dst, W_, dc, scaled = wtasks.pop(0)
                k_ = wq["i"] % 3
                wq["i"] += 1
                ld = dma("sp", sem_wq[k_], wq32[k_][:, 0:W_], src[dc * 128:(dc + 1) * 128, :], deps=[wq["free32"][k_]])
                wq["pend"].append((k_, ld, dst, W_, dc, scaled))

            def wtask_convert():
                if not wq["pend"]:
                    return
                k_, ld, dst, W_, dc, scaled = wq["pend"].pop(0)
                if scaled:
                    cv = ts("dve", wq16[k_][:, 0:W_], wq32[k_][:, 0:W_], gpre_sb[:, dc:dc + 1], None, ALU.mult,
                            deps=[ld, wq["last16"][k_]])
                else:
                    cv = cp("dve", wq16[k_][:, 0:W_], wq32[k_][:, 0:W_], deps=[ld, wq["last16"][k_]])
                wq["free32"][k_] = cv
                wq["last16"][k_] = dma("pool", sem_wqs[k_], dst[dc * 128:(dc + 1) * 128, :], wq16[k_][:, 0:W_], deps=[cv])

            for _ in range(3):
                wtask_load()

            c_v = [memset("pool", Vb[i][:, :, 64:65], 1.0) for i in range(2)]
            if sample:
                c_vs = [memset("pool", Vsb[i][:, :, 64:65], 1.0) for i in range(2)]
            bufree = [None, None]
            s2 = dict(sTfree=[None] * NPB, pbfree=[None] * NPB, oTfree=[None, None], oTfree_b=[None, None], pend=None,
                      ji=0, oi=0, rdfree=None, osfree=None, bcfree=None, yi=0, ylast=[None, None])

            for h in range(8):
                par = h % 2
                fdep = [bufree[par]]
                NCH = 4
                TPC = NKT // NCH
                chunk_ld = []
                for c_ in range(NCH):
                    k0, k1 = c_ * TPC, (c_ + 1) * TPC
                    l_ = None
                    if c_ == 0:
                        l_ = dma("sp", sem_kc[par][c_], Qb[par][:], Qs[h], deps=fdep)
                    l_ = dma("sp", sem_kc[par][c_], Kb[par][:, k0 * 128:k1 * 128], Ks[h, :, k0 * 128:k1 * 128], deps=fdep)
                    for g in range(k0, k1, 16):
                        g1 = min(k1, g + 16)
                        l_ = dma("sp", sem_kc[par][c_], Vb[par][:, g:g1, 0:64],
                                 Vs[g * 128:g1 * 128, h * 64:(h + 1) * 64].rearrange("(k p) d -> p k d", p=128), deps=fdep)
                    chunk_ld.append(l_)
                samp_ld = None
                if sample:
                    samp_ld = dma("sp", sem_ks[par], Ksb[par][:, 0:LS + NSAMP], Kss[h, :, 0:LS + NSAMP], deps=fdep)
                    for g in range(0, LS // 128, 16):
                        g1 = min(LS // 128, g + 16)
                        samp_ld = dma("sp", sem_ks[par], Vsb[par][:, g:g1, 0:64],
                                      Vss[g * 128:g1 * 128, h * 64:(h + 1) * 64].rearrange("(k p) d -> p k d", p=128),
                                      deps=fdep)
                    samp_ld = dma("sp", sem_ks[par], Vsb[par][0:NSAMP, LS // 128, 0:64],
                                  Vss[LS:LS + NSAMP, h * 64:(h + 1) * 64], deps=fdep)
                if h > 0:
                    for _ in range(3):
                        wtask_convert()
                    for _ in range(3):
                        wtask_load()
                def kdeps(kt):
                    c_ = kt // TPC
                    return [chunk_ld[0], c_v[par]] + ([chunk_ld[c_]] if c_ > 0 else [])
                sdeps_ = [chunk_ld[0], samp_ld, c_vs[par]] if sample else []

                groups = []
                for m in range(NOWN):
                    jobs = []
                    nfull = (4 * m + 3) * 4
                    for kt in range(nfull):
                        bias = smask[:, kt // 4:kt // 4 + 1] if kt < 12 else None
                        jobs.append((Kb[par][:, kt * 128:(kt + 1) * 128], Qb[par][:, m * 512:(m + 1) * 512],
                                     Vb[par][:, kt, :], 128, 0, 512, bias, False, kdeps(kt)))
                    for b in range(4):
                        kt = nfull + b
                        jobs.append((Kb[par][:, kt * 128:(kt + 1) * 128], Qb[par][:, m * 512 + b * 128:(m + 1) * 512],
                                     Vb[par][:, kt, :], 128, b * 128, 512 - b * 128, None, True, kdeps(kt)))
                    groups.append((jobs, 512, m * 512))
                if sample:
                    jobs = []
                    for kt in range(LS // 128):
                        jobs.append((Ksb[par][:, kt * 128:(kt + 1) * 128], Qb[par][:, NQ:NQ + NSAMP],
                                     Vsb[par][:, kt, :], 128, 0, NSAMP, None, False, sdeps_))
                    kt = LS // 128
                    jobs.append((Ksb[par][:, LS:LS + NSAMP], Qb[par][:, NQ:NQ + NSAMP],
                                 Vsb[par][0:NSAMP, kt, :], NSAMP, 0, NSAMP, None, True, sdeps_))
                    groups.append((jobs, NSAMP, NQ))

                flat = []
                for gi, (jobs, nq, ycol) in enumerate(groups):
                    for ji, jb in enumerate(jobs):
                        flat.append((gi, ji, len(jobs), jb))
                nflat = len(flat)
                LOOK = 4
                qk_pend = {}

                def emit_qk(fi):
                    gi, ji, nj, (Kap, Qap, Vap, nk, c0, n, bias, trif, jd) = flat[fi]
                    b = s2["ji"] % NPB
                    s2["ji"] += 1
                    o = mm(sT2[b][0:nk, 0:n], Kap, Qap, True, True, deps=jd + [s2["sTfree"][b]])
                    qk_pend[fi] = (b, o)

                for fi in range(min(LOOK, nflat)):
                    emit_qk(fi)
                cur_o = None
                last_pv = None
                for fi in range(nflat):
                    gi, ji, nj, (Kap, Qap, Vap, nk, c0, n, bias, trif, jd) = flat[fi]
                    if fi + LOOK < nflat:
                        emit_qk(fi + LOOK)
                    b, qk = qk_pend.pop(fi)
                    if ji == 0:
                        cur_o = s2["oi"] % 2
                        s2["oi"] += 1
                    kw = {"scale": SCALE}
                    if bias is not None:
                        kw["bias"] = bias
                    ex = act(pb2[b][0:nk, 0:n], sT2[b][0:nk, 0:n], AF.Exp, deps=[qk, s2["pbfree"][b], l_bf], **kw)
                    s2["sTfree"][b] = ex
                    pdep = ex
                    if trif:
                        w_ = min(128, n)
                        pdep = tt("pool", pb2[b][0:nk, 0:w_], pb2[b][0:nk, 0:w_], tri[0:nk, 0:w_], ALU.mult, deps=[ex, c_tri])
                    pv = mm(oT2[cur_o][0:65, c0:c0 + n], Vap, pb2[b][0:nk, 0:n], ji == 0, ji == nj - 1,
                            deps=[pdep, s2["oTfree"][cur_o], s2["oTfree_b"][cur_o]] + jd, skip=True)
                    s2["pbfree"][b] = pv
                    last_pv = pv
                    if ji == nj - 1:
                        jobs, nq, ycol = groups[gi]
                        o_ = oT2[cur_o]
                        c1 = act(osb2[0:65, 0:nq], o_[0:65, 0:nq], AF.Copy, deps=[pv, s2["osfree"]])
                        s2["oTfree"][cur_o] = c1
                        r1 = P.op("dve", lambda e, nq=nq: e.reciprocal(out=rd2[64:65, 0:nq], in_=osb2[64:65, 0:nq]),
                                  [c1, s2["rdfree"]])

                        def finish(r1=r1, c1=c1, nq=nq, ycol=ycol, h=h):
                            bc = mm(bc2[0:64, 0:nq], onesf[64:65, 0:64], rd2[64:65, 0:nq], True, True,
                                    deps=[r1, s2["bcfree"]])
                            s2["rdfree"] = bc
                            yi = s2["yi"] % 2
                            s2["yi"] += 1
                            y1 = tt("dve", yb2[yi][:, 0:nq], osb2[0:64, 0:nq], bc2[0:64, 0:nq], ALU.mult,
                                    deps=[c1, bc, s2["ylast"][yi]])
                            s2["bcfree"] = y1
                            s2["osfree"] = y1
                            s2["ylast"][yi] = dma("pool", sem_yb[yi], YB[h * 64:(h + 1) * 64, ycol:ycol + nq], yb2[yi][:, 0:nq],
                                                  deps=[y1])
                        s2["pend"] = [finish, 6]
                    elif s2["pend"] is not None:
                        s2["pend"][1] -= 1
                        if s2["pend"][1] <= 0:
                            s2["pend"][0]()
                            s2["pend"] = None
                if s2["pend"] is not None:
                    s2["pend"][0]()
                    s2["pend"] = None
                bufree[par] = last_pv
            while wq["pend"] or wtasks:
                for _ in range(3):
                    wtask_convert()
                for _ in range(3):
                    wtask_load()
            P.finalize()
            if stop == "p2":
                raise _Stop(nc)

        ph = ExitStack()
        with ph:
            def sb(name, shape, dt):
                return ph.enter_context(nc.sbuf_tensor(name, shape, dt))

            def ps(name, shape, dt):
                return ph.enter_context(nc.psum_tensor(name, shape, dt))

            W3b = sb("W3b", [128, DC, C3], BF16)
            Wbrb = sb("Wbrb", [128, DC, D], BF16)
            Woutb = sb("Woutb", [128, DC, D], BF16)
            gpb = sb("gpb", [128, D], F32)
            xs4 = [sb("xs4_%d" % i, [128, 4, D], F32) for i in range(2)]
            junk3 = sb("junk3", [128, D], BF16)
            ssq3 = sb("ssq3", [128, 8], F32)
            rsd3 = sb("rsd3", [128, 8], F32)
            hb3 = [sb("hb3_%d" % i, [128, D], BF16) for i in range(2)]
            hT3 = [sb("hT3_%d" % i, [128, DC, 512], BF16) for i in range(2)]
            GZ = sb("GZ", [128, 8, 512], BF16)
            GM = sb("GM", [128, 16, 512], BF16)
            yl = sb("yl", [128, 8, 512], BF16)
            gmul = sb("gmul", [128, 8, 512], BF16)
            mrg = sb("mrg", [128, DC, 512], BF16)
            t1 = [sb("t1_%d" % i, [128, 512], F32) for i in range(2)]
            t2 = [sb("t2_%d" % i, [128, 512], F32) for i in range(2)]
            osb3 = [sb("osb3_%d" % i, [128, D], F32) for i in range(2)]
            ss2 = sb("ss2", [128, 4], F32)
            rs2 = sb("rs2", [128, 2], F32)
            sem_m3 = newsem(ph)
            sem_w3 = [newsem(ph) for _ in range(8)]
            sem_x3 = [newsem(ph), newsem(ph)]
            sem_y3 = newsem(ph)
            sem_o3 = [newsem(ph), newsem(ph)]
            tp3 = [ps("tp3_%d" % i, [128, DC, 128], BF16) for i in range(2)]
            NM3 = 6
            mp3 = [ps("mp3_%d" % i, [128, 512], F32) for i in range(NM3)]

            mhalf = sb("mhalf", [128, 1], F32)
            c_mh = memset("pool", mhalf[:], -0.5)
            l_gpb = dma("sp", sem_m3, gpb[:], gpost.partition_broadcast(128))
            w3ld = []
            for cb in range(6):
                w3ld.append(dma("sp", sem_w3[cb], W3b[:, :, cb * 512:(cb + 1) * 512],
                                W3s[:, cb * 512:(cb + 1) * 512].rearrange("(c p) n -> p c n", p=128)))
            wbrld = dma("sp", sem_w3[6], Wbrb[:], Wbrs.rearrange("(c p) n -> p c n", p=128))
            woutld = dma("sp", sem_w3[7], Woutb[:], Wouts.rearrange("(c p) n -> p c n", p=128))

            s3 = dict(mpi=0, mpfree=[[] for _ in range(NM3)], junkfree=None, ssfree=None, rs2free=None, hbfree=[None, None],
                      tpfree=[None, None], xfree=[None, None], hTfree=[None, None],
                      gzfree=None, gmfree=None, ylfree=None, gmulfree=None, mrgfree=None, t1free=[None, None],
                      t2free=[None, None], oi=0, olast=[None, None], rsfree=[None] * 8, hi=0)

            def mm3(pairs, M, N, deps):
                bi = s3["mpi"] % NM3
                s3["mpi"] += 1
                bank = mp3[bi]
                o = None
                for kk, (l_, r_) in enumerate(pairs):
                    o = mm(bank[0:M, 0:N], l_, r_, kk == 0, kk == len(pairs) - 1, deps=list(deps) + s3["mpfree"][bi])
                return bi, bank, o

            units = [(xw, (4 * m + 3) * 512, m * 512, 512, y_own, m * 512) for m in range(NOWN)]
            if sample:
                units.append((xs_d, 0, NQ, NSAMP, ys_o, 0))
            NU = len(units)
            xld = {}
            hnd3 = {}
            cps3 = {}

            def u_geom(u):
                xsrc, x0, ycol, ntok, ydst, ybase = units[u]
                return (ntok + 127) // 128, min(128, ntok)

            def p3_load(u):
                if u >= NU:
                    return
                xsrc, x0, ycol, ntok, ydst, ybase = units[u]
                ntile, tn = u_geom(u)
                ld = None
                for t in range(ntile):
                    ld = dma("sp", sem_x3[u % 2], xs4[u % 2][0:tn, t, :], xsrc[x0 + t * 128:x0 + t * 128 + tn, :],
                             deps=[s3["xfree"][u % 2]])
                xld[u] = ld

            def p3_a(u, t):
                if u >= NU:
                    return
                ntile, tn = u_geom(u)
                if t >= ntile:
                    return
                c = (u % 2) * 4 + t
                ld = xld[u]
                xin = xs4[u % 2][0:tn, t, :]
                sq = act(junk3[0:tn, :], xin, AF.Square, deps=[ld, s3["junkfree"]], accum_out=ssq3[0:tn, c:c + 1])
                s3["junkfree"] = sq
                sr = ts("pool", rsd3[0:tn, c:c + 1], ssq3[0:tn, c:c + 1], 1.0 / D, EPS, ALU.mult, ALU.add,
                        deps=[sq, s3["rsfree"][c]])
                rc = tt("pool", rsd3[0:tn, c:c + 1], rsd3[0:tn, c:c + 1], mhalf[0:tn, :], ALU.pow, deps=[sr, c_mh])
                b = s3["hi"] % 2
                s3["hi"] += 1
                sc = ts("dve", hb3[b][0:tn, :], xin, rsd3[0:tn, c:c + 1], None, ALU.mult, deps=[rc, s3["hbfree"][b]])
                s3["rsfree"][c] = sc
                hnd3[(u, t)] = (b, sc)

            def p3_b(u, t):
                if u >= NU:
                    return
                ntile, tn = u_geom(u)
                if t >= ntile:
                    return
                b, sc = hnd3.pop((u, t))
                t_ = None
                for dc in range(DC):
                    t_ = tr(tp3[b][:, dc, 0:tn], hb3[b][0:tn, dc * 128:(dc + 1) * 128], ident[0:tn, 0:tn],
                            deps=[sc, s3["tpfree"][b]])
                s3["hbfree"][b] = t_
                co = act(hT3[u % 2][:, :, t * 128:t * 128 + tn], tp3[b][:, :, 0:tn], AF.Copy,
                         deps=[t_, s3["hTfree"][u % 2]])
                s3["tpfree"][b] = co
                cps3.setdefault(u, []).append(co)

            p3_load(0)
            p3_load(1)
            for t in range(4):
                p3_a(0, t) if t < 2 else None
            p3_b(0, 0)
            p3_b(0, 1)
            p3_a(0, 2)
            p3_a(0, 3)
            p3_b(0, 2)
            p3_b(0, 3)

            for u in range(NU):
                xsrc, x0, ycol, ntok, ydst, ybase = units[u]
                ntile, tn = u_geom(u)
                N = ntok
                hTu = hT3[u % 2]
                cps = cps3.pop(u)
                dma("sp", sem_y3, yl[:, 0:4, 0:N], YA[:, ycol:ycol + N].rearrange("(c p) t -> p c t", p=128),
                    deps=[s3["ylfree"]])
                yld = dma("sp", sem_y3, yl[:, 4:8, 0:N], YB[:, ycol:ycol + N].rearrange("(c p) t -> p c t", p=128),
                          deps=[s3["ylfree"]])
                lastmm = None
                gz_last = None
                gm_last = None
                g1 = None
                for c in range(24):
                    pairs = [(W3b[:, dc, c * 128:(c + 1) * 128], hTu[:, dc, 0:N]) for dc in range(DC)]
                    bi, bank, o = mm3(pairs, 128, N, cps + [w3ld[c // 4]])
                    lastmm = o
                    if c < 8:
                        ev = act(GZ[:, c, 0:N], bank[:, 0:N], AF.Silu, deps=[o, s3["gzfree"]])
                        gz_last = ev
                    else:
                        ev = act(GM[:, c - 8, 0:N], bank[:, 0:N], AF.Sigmoid, deps=[o, s3["gmfree"]])
                        gm_last = ev
                    s3["mpfree"][bi] = [ev]
                    if c == 7:
                        g1 = tt("dve", gmul[:, :, 0:N], yl[:, :, 0:N], GZ[:, :, 0:N], ALU.mult,
                                deps=[yld, gz_last, s3["gmulfree"]])
                        s3["ylfree"] = g1
                        s3["gzfree"] = g1
                        p3_a(u + 1, 0)
                        p3_a(u + 1, 1)
                    if c == 15:
                        p3_b(u + 1, 0)
                        p3_b(u + 1, 1)
                        p3_a(u + 1, 2)
                        p3_a(u + 1, 3)
                s3["hTfree"][u % 2] = lastmm
                lastbr = None
                mr_ops = []
                for dc in range(DC):
                    pa = [(Wbrb[:, c, dc * 128:(dc + 1) * 128], gmul[:, c, 0:N]) for c in range(4)]
                    bia, banka, oa = mm3(pa, 128, N, [g1, wbrld])
                    pbb = [(Wbrb[:, 4 + c, dc * 128:(dc + 1) * 128], gmul[:, 4 + c, 0:N]) for c in range(4)]
                    bib, bankb, ob = mm3(pbb, 128, N, [g1, wbrld])
                    lastbr = ob
                    k2 = dc % 2
                    m1 = tt("dve", t1[k2][:, 0:N], banka[:, 0:N], GM[:, dc, 0:N], ALU.mult, deps=[oa, gm_last, s3["t1free"][k2]])
                    s3["mpfree"][bia] = [m1]
                    m2 = tt("dve", t2[k2][:, 0:N], bankb[:, 0:N], GM[:, 8 + dc, 0:N], ALU.mult, deps=[ob, gm_last, s3["t2free"][k2]])
                    s3["mpfree"][bib] = [m2]
                    m3_ = tt("pool", mrg[:, dc, 0:N], t1[k2][:, 0:N], t2[k2][:, 0:N], ALU.add, deps=[m1, m2, s3["mrgfree"]])
                    s3["t1free"][k2] = m3_
                    s3["t2free"][k2] = m3_
                    mr_ops.append(m3_)
                    if dc == 3:
                        p3_b(u + 1, 2)
                        p3_b(u + 1, 3)
                s3["gmulfree"] = lastbr
                s3["gmfree"] = mr_ops[-1]
                lastout = None
                fin = None
                for t in range(ntile):
                    oi = s3["oi"] % 2
                    s3["oi"] += 1
                    ob_ = osb3[oi]
                    sqs = []
                    for hf in range(2):
                        pairs = [(mrg[:, dc, t * 128:t * 128 + tn], Woutb[:, dc, hf * 512:(hf + 1) * 512]) for dc in range(DC)]
                        bi, bank, o = mm3(pairs, tn, 512, mr_ops + [woutld])
                        lastout = o
                        ev = act(ob_[0:tn, hf * 512:(hf + 1) * 512], bank[0:tn, :], AF.Copy, deps=[o, s3["olast"][oi]])
                        s3["mpfree"][bi] = [ev]
                        sq = act(junk3[0:tn, 0:512], ob_[0:tn, hf * 512:(hf + 1) * 512], AF.Square,
                                 deps=[ev, s3["ssfree"], s3["junkfree"]], accum_out=ss2[0:tn, hf:hf + 1])
                        s3["junkfree"] = sq
                        sqs.append(sq)
                    a_ = tt("dve", ss2[0:tn, 2:3], ss2[0:tn, 0:1], ss2[0:tn, 1:2], ALU.add, deps=sqs)
                    s3["ssfree"] = a_
                    sr = ts("pool", rs2[0:tn, 0:1], ss2[0:tn, 2:3], 1.0 / D, EPS, ALU.mult, ALU.add, deps=[a_, s3["rs2free"]])
                    rc = tt("pool", rs2[0:tn, 0:1], rs2[0:tn, 0:1], mhalf[0:tn, :], ALU.pow, deps=[sr, c_mh])
                    f1 = stt(ob_[0:tn, :], ob_[0:tn, :], rs2[0:tn, 0:1], gpb[0:tn, :], ALU.mult, ALU.mult, deps=[rc, l_gpb])
                    s3["rs2free"] = f1
                    f2 = tt("pool", ob_[0:tn, :], ob_[0:tn, :], xs4[u % 2][0:tn, t, :], ALU.add, deps=[f1])
                    s3["olast"][oi] = dma("pool", sem_o3[oi], ydst[ybase + t * 128:ybase + t * 128 + tn, :],
                                          ob_[0:tn, :], deps=[f2])
                    fin = f2
                s3["mrgfree"] = lastout
                s3["xfree"][u % 2] = fin
                p3_load(u + 2)
                for t in range(4):
                    if (u + 1, t) in hnd3:
                        p3_b(u + 1, t)
            P.finalize()
    return nc


_CACHE = {}


def _get_nc(NSLOT=32, sample=True):
    key = (NSLOT, sample)
    if key not in _CACHE:
        _CACHE[key] = build(NSLOT, sample)
    return _CACHE[key]


def make_in_maps(inputs, NSLOT=32):
    f32 = np.float32
    xp = np.asarray(inputs["x_prompt"], f32)
    B, S, _ = xp.shape
    L = NSLOT * 512
    w_in = np.asarray(inputs["w_in"], f32)[0]
    cols1 = np.r_[0:512, 512:1024, 1024:1536, 2048:2560, 2560:3072, 3072:3584, 4096:4104]
    cols3 = np.r_[1536:2048, 3584:4096, 4104:5128, 5128:6152]
    w1 = np.ascontiguousarray(w_in[:, cols1])
    w3 = np.ascontiguousarray(w_in[:, cols3])
    wbr = np.ascontiguousarray(np.concatenate([np.asarray(inputs["w_br_a"], f32)[0], np.asarray(inputs["w_br_b"], f32)[0]], 0))
    wout = np.ascontiguousarray(np.asarray(inputs["w_out"], f32)[0])
    maps = []
    for c in range(8):
        b, j = c // 4, c % 4
        start = (j - 3) * 512
        win = np.zeros((L, D), f32)
        lo = max(start, 0)
        hi = min(start + L, S)
        win[lo - start:hi - start] = xp[b, lo:hi]
        sm = np.zeros((128, 3), f32)
        for s_ in range(3):
            if s_ + j - 3 < 0:
                sm[:, s_] = -BIG
        maps.append({
            "xw": win, "w1": w1, "w3": w3, "wbr": wbr, "wout": wout,
            "gpre": np.ascontiguousarray(np.asarray(inputs["g_pre"], f32)[0]),
            "gpost": np.ascontiguousarray(np.asarray(inputs["g_post"], f32)[0]),
            "bfv": np.ascontiguousarray(np.asarray(inputs["b_f"], f32)[0].reshape(8, 1)),
            "rel": np.ascontiguousarray(np.asarray(inputs["rel_table"], f32)[0]),
            "smask": sm,
            "xs": np.ascontiguousarray(np.asarray(inputs["x_sample"], f32)[c]),
            "cak": np.ascontiguousarray(np.asarray(inputs["cache_a_k"], f32)[0, c].reshape(512, 512)),
            "cav": np.ascontiguousarray(np.asarray(inputs["cache_a_v"], f32)[0, c].reshape(512, 512)),
            "cbk": np.ascontiguousarray(np.asarray(inputs["cache_b_k"], f32)[0, c].reshape(LS, 512)),
            "cbv": np.ascontiguousarray(np.asarray(inputs["cache_b_v"], f32)[0, c].reshape(LS, 512)),
            "cbl": np.ascontiguousarray(np.asarray(inputs["cache_b_logf"], f32)[0, c].reshape(LS, 8)),
        })
    return maps


def assemble(results, B, S, NSLOT=32):
    f32 = np.float32
    NOWN = NSLOT // 4
    y = np.zeros((B, S, D), f32)
    bk = np.zeros((1, B, S, 8, 64), f32)
    bv = np.zeros((1, B, S, 8, 64), f32)
    blf = np.zeros((1, B, S, 8), f32)
    akp = np.zeros((1, B, 512, 8, 64), f32)
    avp = np.zeros((1, B, 512, 8, 64), f32)
    ys = np.zeros((8, NSAMP, D), f32)
    aks = np.zeros((1, 8, 512, 8, 64), f32)
    avs = np.zeros((1, 8, 512, 8, 64), f32)
    bks = np.zeros((1, 8, NSAMP, 8, 64), f32)
    bvs = np.zeros((1, 8, NSAMP, 8, 64), f32)
    bls = np.zeros((1, 8, NSAMP, 8), f32)
    for c in range(8):
        r = results[c]
        b, j = c // 4, c % 4
        for m in range(NOWN):
            s0 = (4 * m + j) * 512
            y[b, s0:s0 + 512] = r["y_own"][m * 512:(m + 1) * 512]
            bk[0, b, s0:s0 + 512] = r["bk_own"][m * 512:(m + 1) * 512].reshape(512, 8, 64)
            bv[0, b, s0:s0 + 512] = r["bv_own"][m * 512:(m + 1) * 512].reshape(512, 8, 64)
            blf[0, b, s0:s0 + 512] = r["blf_own"][m * 512:(m + 1) * 512]
        if j == 3:
            akp[0, b] = r["akp"].reshape(512, 8, 64)
            avp[0, b] = r["avp"].reshape(512, 8, 64)
        ys[c] = r["ys"]
        aks[0, c] = r["aks"].reshape(512, 8, 64)
        avs[0, c] = r["avs"].reshape(512, 8, 64)
        bks[0, c] = r["bks"].reshape(NSAMP, 8, 64)
        bvs[0, c] = r["bvs"].reshape(NSAMP, 8, 64)
        bls[0, c] = r["bls"]
    return (y, ys, akp, avp, bk, bv, blf, aks, avs, bks, bvs, bls)


def kernel(**inputs):
    nc = _get_nc(32, True)
    maps = make_in_maps(inputs, 32)
    res = run_bass_kernel_spmd(nc, maps, core_ids=list(range(8)))
    B, S, _ = np.asarray(inputs["x_prompt"]).shape
    return assemble(res.results, B, S, 32)
```
